# Optimizing a Trainium2 kernel written in Bass

```python
import math
import jax, jax.numpy as jnp
from jax import lax
import numpy as np

D_MODEL = 1024
BATCH = 8
SEQ = 4096
DEPTH = 1
DEC_BATCH = 8
DEC_SEQ = 8192
PAST_LEN = 128

HEAD_DIM = 64
H_A = 8
H_B = 8
KVH_B = 2
G_B = H_B // KVH_B
W_A = H_A * HEAD_DIM
W_B = H_B * HEAD_DIM
W_KV_B = KVH_B * HEAD_DIM
GRID_W = 64
NA_ROWS = 8
NA_COLS = 16
WINDOW = 128
BLOCK = 128
T5_BUCKETS = 32
T5_MAX_DIST = 128
EPS = 1e-6
SPLITS = (W_A, W_A, W_A, W_A, W_B, W_KV_B, W_KV_B, W_B, D_MODEL, D_MODEL)
D_IN = sum(SPLITS)

kernel_name = "hybrid_natten_window_gqa_gated_encoder"


def rmsnorm(x, g):
    xf = x.astype(jnp.float32)
    y = xf * lax.rsqrt(jnp.mean(xf * xf, axis=-1, keepdims=True) + EPS)
    return (y * g.astype(jnp.float32)).astype(x.dtype)


def t5_bucket(rel):
    half = T5_BUCKETS // 2
    max_exact = half // 2
    n = jnp.abs(rel)
    large = max_exact + (jnp.log(jnp.maximum(n, 1).astype(jnp.float32) / max_exact)
                         / math.log(T5_MAX_DIST / max_exact) * (half - max_exact)).astype(jnp.int32)
    large = jnp.minimum(large, half - 1)
    return jnp.where(rel > 0, half, 0) + jnp.where(n < max_exact, n, large)


def neighbourhood_attention(q, k, v, rpb):
    B, L, H, dh = q.shape
    rows = L // GRID_W
    kr = min(NA_ROWS, rows)
    qg = q.reshape(B, rows, GRID_W, H, dh)
    kg = k.reshape(B, rows, GRID_W, H, dh)
    vg = v.reshape(B, rows, GRID_W, H, dh)
    col = np.arange(GRID_W)
    cs = np.clip(col - NA_COLS // 2, 0, GRID_W - NA_COLS)
    cidx = cs[:, None] + np.arange(NA_COLS)[None, :]
    dc_idx = (cidx - col[:, None]) + (NA_COLS - 1)
    scale = 1.0 / math.sqrt(dh)

    def row_step(args):
        r, q_r = args
        rs = jnp.clip(r - kr // 2, 0, rows - kr)
        k_rows = lax.dynamic_slice_in_dim(kg, rs, kr, axis=1)
        v_rows = lax.dynamic_slice_in_dim(vg, rs, kr, axis=1)
        k_n = k_rows[:, :, cidx]
        v_n = v_rows[:, :, cidx]
        dr_idx = rs + jnp.arange(kr) - r + (NA_ROWS - 1)
        bias = rpb[:, dr_idx[None, :, None], dc_idx[:, None, :]]
        s = jnp.einsum('bqhd,brqchd->bhqrc', q_r, k_n).astype(jnp.float32) * scale
        s = s + bias[None].astype(jnp.float32)
        p = jax.nn.softmax(s.reshape(B, H, GRID_W, kr * NA_COLS), axis=-1)
        p = p.reshape(B, H, GRID_W, kr, NA_COLS).astype(v.dtype)
        return jnp.einsum('bhqrc,brqchd->bqhd', p, v_n)

    q_rows = jnp.moveaxis(qg, 1, 0)
    out = lax.map(row_step, (jnp.arange(rows), q_rows))
    return jnp.moveaxis(out, 0, 1).reshape(B, L, H * dh)


def window_gqa_attention(q, k, v, t5_table, sink):
    B, L, H, dh = q.shape
    nb = L // BLOCK
    pad = ((0, 0), (BLOCK, BLOCK), (0, 0), (0, 0))
    kp = jnp.pad(k, pad).reshape(B, nb + 2, BLOCK, KVH_B, dh)
    vp = jnp.pad(v, pad).reshape(B, nb + 2, BLOCK, KVH_B, dh)
    kb = jnp.concatenate([kp[:, :-2], kp[:, 1:-1], kp[:, 2:]], axis=2)
    vb = jnp.concatenate([vp[:, :-2], vp[:, 1:-1], vp[:, 2:]], axis=2)
    qb = q.reshape(B, nb, BLOCK, KVH_B, G_B, dh)
    scale = 1.0 / math.sqrt(dh)
    s = jnp.einsum('bnqkgd,bnskd->bnkgqs', qb, kb).astype(jnp.float32) * scale
    sidx = jnp.arange(3 * BLOCK)
    rel = sidx[None, :] - BLOCK - jnp.arange(BLOCK)[:, None]
    key_pos = jnp.arange(nb)[:, None] * BLOCK + sidx[None, :] - BLOCK
    mask = (jnp.abs(rel) <= WINDOW)[None] & ((key_pos >= 0) & (key_pos < L))[:, None, :]
    bias = jnp.transpose(t5_table[t5_bucket(rel)], (2, 0, 1)).reshape(KVH_B, G_B, BLOCK, 3 * BLOCK)
    s = s + bias[None, None].astype(jnp.float32)
    s = jnp.where(mask[None, :, None, None], s, jnp.float32(-1e30))
    sink_l = jnp.broadcast_to(sink.astype(jnp.float32).reshape(1, 1, KVH_B, G_B, 1, 1),
                              s.shape[:-1] + (1,))
    p = jax.nn.softmax(jnp.concatenate([s, sink_l], axis=-1), axis=-1)[..., :-1].astype(v.dtype)
    o = jnp.einsum('bnkgqs,bnskd->bnqkgd', p, vb)
    return o.reshape(B, L, H * dh)


def encoder_layer(x, norm_g, w_in, qn_a, kn_a, rpb_a, qn_b, kn_b, sink_b, w_o_a, w_o_b, w_out, t5_table):
    B, L, D = x.shape
    h = rmsnorm(x, norm_g)
    proj = h @ w_in
    qa, ka, va, za, qb, kb, vb, zb, ga, gb = jnp.split(proj, np.cumsum(SPLITS)[:-1], axis=-1)
    qa = rmsnorm(qa.reshape(B, L, H_A, HEAD_DIM), qn_a)
    ka = rmsnorm(ka.reshape(B, L, H_A, HEAD_DIM), kn_a)
    va = va.reshape(B, L, H_A, HEAD_DIM)
    oa = neighbourhood_attention(qa, ka, va, rpb_a) * jax.nn.silu(za)
    oa = oa @ w_o_a
    qb = rmsnorm(qb.reshape(B, L, H_B, HEAD_DIM), qn_b)
    kb = rmsnorm(kb.reshape(B, L, KVH_B, HEAD_DIM), kn_b)
    vb = vb.reshape(B, L, KVH_B, HEAD_DIM)
    ob = window_gqa_attention(qb, kb, vb, t5_table, sink_b) * jax.nn.silu(zb)
    ob = ob @ w_o_b
    merged = jax.nn.sigmoid(ga) * oa + jax.nn.sigmoid(gb) * ob
    return x + merged @ w_out


def setup_inputs(seed: int = 0) -> dict:
    key = jax.random.key(seed)
    ks = jax.random.split(key, 16)
    f32 = jnp.float32
    nrm = lambda k, shape, s: jax.random.normal(k, shape, f32) * s
    return {
        "x_prompt": nrm(ks[0], (BATCH, SEQ, D_MODEL), 1.0),
        "x_sample": nrm(ks[1], (DEC_BATCH, DEC_SEQ, D_MODEL), 1.0),
        "norm_g": 1.0 + nrm(ks[2], (DEPTH, D_MODEL), 0.02),
        "w_in": nrm(ks[3], (DEPTH, D_MODEL, D_IN), D_MODEL ** -0.5),
        "qn_a": 1.0 + nrm(ks[4], (DEPTH, HEAD_DIM), 0.02),
        "kn_a": 1.0 + nrm(ks[5], (DEPTH, HEAD_DIM), 0.02),
        "rpb_a": nrm(ks[6], (DEPTH, H_A, 2 * NA_ROWS - 1, 2 * NA_COLS - 1), 0.1),
        "qn_b": 1.0 + nrm(ks[7], (DEPTH, HEAD_DIM), 0.02),
        "kn_b": 1.0 + nrm(ks[8], (DEPTH, HEAD_DIM), 0.02),
        "sink_b": nrm(ks[9], (DEPTH, H_B), 0.5),
        "w_o_a": nrm(ks[10], (DEPTH, W_A, D_MODEL), W_A ** -0.5),
        "w_o_b": nrm(ks[11], (DEPTH, W_B, D_MODEL), W_B ** -0.5),
        "w_out": nrm(ks[12], (DEPTH, D_MODEL, D_MODEL), D_MODEL ** -0.5),
        "t5_table": nrm(ks[13], (T5_BUCKETS, H_B), 0.1),
    }


def reference(x_prompt, x_sample, norm_g, w_in, qn_a, kn_a, rpb_a, qn_b, kn_b, sink_b,
              w_o_a, w_o_b, w_out, t5_table):
    y_prompt = x_prompt
    y_sample = x_sample
    for l in range(DEPTH):
        params = (norm_g[l], w_in[l], qn_a[l], kn_a[l], rpb_a[l], qn_b[l], kn_b[l], sink_b[l],
                  w_o_a[l], w_o_b[l], w_out[l], t5_table)
        y_prompt = encoder_layer(y_prompt, *params)
        y_sample = encoder_layer(y_sample, *params)
    return (y_prompt, y_sample)
```

```python
import math
from contextlib import ExitStack

import numpy as np
import concourse.bass as bass
import concourse.mybir as mybir
from concourse.bass_utils import run_bass_kernel_spmd

F32 = mybir.dt.float32
BF16 = mybir.dt.bfloat16
AF = mybir.ActivationFunctionType
ALU = mybir.AluOpType
AX = mybir.AxisListType

D_MODEL = 1024
D_IN = 5376
EPS = 1e-6
NEG = -30000.0
C_QA, C_KA, C_VA, C_ZA, C_QB, C_KB, C_VB, C_ZB, C_GA, C_GB = 0, 512, 1024, 1536, 2048, 2560, 2688, 2816, 3328, 4352
DEBUG_SEQ = False
KV_R = 8
HT_R = 6
N_UA = 9
UA_ORDER = [(-3, False), (-2, False), (-2, True), (-1, False), (0, False), (1, False), (2, True), (2, False), (3, False)]


class Res:
    __slots__ = ("name", "w", "r", "excl")

    def __init__(self, name, excl=False):
        self.name = name
        self.w = None
        self.r = {}
        self.excl = excl


class Sched:
    SAME_ENGINE_SYNC = True

    def __init__(self, nc, stack):
        self.nc = nc
        self.stack = stack
        self.eng = {"pe": nc.tensor, "act": nc.scalar, "dve": nc.vector, "pool": nc.gpsimd, "sp": nc.sync}
        self.ops = {k: [] for k in self.eng}
        self.sems = {}
        self.cnt = {}
        for k in ("pe", "act", "dve", "pool"):
            self.sems[k] = stack.enter_context(nc.semaphore("s_" + k))
            self.cnt[k] = 0
        self.known = {k: {} for k in self.eng}

    def dma_sem(self, name):
        self.sems[name] = self.stack.enter_context(self.nc.semaphore(name))
        self.cnt[name] = 0
        return name

    dry = False

    def op(self, eng, fn, reads=(), writes=(), dma=None, standalone=False):
        if self.dry:
            return None
        deps = {}

        def add(ev):
            if ev is not None and deps.get(ev[0], 0) < ev[1]:
                deps[ev[0]] = ev[1]

        def addall(d):
            for k, v in d.items():
                if deps.get(k, 0) < v:
                    deps[k] = v

        for r in reads:
            add(r.w)
            if r.excl:
                addall(r.r)
        for w in writes:
            add(w.w)
            addall(w.r)
        waits = []
        kn = self.known[eng]
        for k, v in deps.items():
            if k == eng and (eng == "pe" or not self.SAME_ENGINE_SYNC):
                continue
            if kn.get(k, 0) >= v:
                continue
            waits.append((k, v))
            kn[k] = v
        if dma is None:
            self.cnt[eng] += 1
            ev = (eng, self.cnt[eng])
            inc = (eng, 1)
        else:
            self.cnt[dma] += 16
            ev = (dma, self.cnt[dma])
            inc = (dma, 16)
        self.ops[eng].append((fn, waits, inc, standalone))
        for r in reads:
            if r.excl:
                r.w = ev
                r.r = {}
            elif r.r.get(ev[0], 0) < ev[1]:
                r.r[ev[0]] = ev[1]
        for w in writes:
            w.w = ev
            w.r = {}
        return ev

    def final_wait(self, eng, events):
        self.ops[eng].append((None, [e for e in events if e is not None], None, True))

    def emit(self):
        sems = self.sems
        with self.nc.Block() as block:
            for name, deco in (("sp", block.sync), ("act", block.scalar), ("dve", block.vector),
                               ("pool", block.gpsimd), ("pe", block.tensor)):
                ops = self.ops[name]

                def body(e, ops=ops, name=name):
                    for fn, waits, inc, standalone in ops:
                        if fn is None:
                            for k, v in waits:
                                e.wait_ge(sems[k], v)
                            continue
                        attach = None
                        ws = waits
                        if name in ("act", "dve", "pool") and waits and not standalone:
                            attach = waits[-1]
                            ws = waits[:-1]
                        for k, v in ws:
                            e.wait_ge(sems[k], v)
                        ins = fn(e)
                        if attach is not None:
                            ins._wait_ge(sems[attach[0]], attach[1])
                        ins.then_inc(sems[inc[0]], inc[1])

                deco(body)


def _a_tables(rpb):
    p = np.arange(128)
    a, kc = p // 64, p % 64
    q = np.arange(128)
    b, qc = q // 64, q % 64
    cs = np.clip(qc - 8, 0, 48)
    colv = (kc[:, None] >= cs[None, :]) & (kc[:, None] < cs[None, :] + 16)
    dc = np.clip(kc[:, None] - qc[None, :] + 15, 0, 30)
    bias = np.zeros((128, 8, N_UA, 128), np.float32)
    mask = np.zeros((128, 8, N_UA, 128), np.float32)
    for t, (delta, gen) in enumerate(UA_ORDER):
        dr = 2 * delta + a[:, None] - b[None, :]
        rowv = (dr >= -4) & (dr <= 3) if gen else (np.abs(dr) <= 7)
        valid = rowv & colv
        dri = np.clip(dr + 7, 0, 14)
        for h in range(8):
            bias[:, h, t, :] = rpb[h][dri, dc]
        mask[:, :, t, :] = np.where(valid, 0.0, NEG)[:, None, :]
    return bias.reshape(128, -1), mask.reshape(128, -1)


def _t5_bucket_np(rel):
    half, max_exact = 16, 8
    n = np.abs(rel)
    try:
        import jax
        import jax.numpy as jnp
        with jax.default_device(jax.devices("cpu")[0]):
            nn = jnp.asarray(n.astype(np.int32))
            large = max_exact + (jnp.log(jnp.maximum(nn, 1).astype(jnp.float32) / max_exact)
                                 / math.log(128 / max_exact) * (half - max_exact)).astype(jnp.int32)
            large = np.asarray(jnp.minimum(large, half - 1))
    except Exception:
        lg = (np.log(np.maximum(n, 1).astype(np.float32) / np.float32(max_exact)) / np.float32(math.log(128 / max_exact))
              * np.float32(half - max_exact))
        large = np.minimum(max_exact + lg.astype(np.int32), half - 1)
    return np.where(rel > 0, half, 0) + np.where(n < max_exact, n, large)


def _b_tables(t5):
    p = np.arange(128)[:, None]
    q = np.arange(128)[None, :]
    bias = np.zeros((128, 8, 3, 128), np.float32)
    mask = np.zeros((128, 8, 3, 128), np.float32)
    for t, delta in enumerate((-1, 0, 1)):
        rel = 128 * delta + p - q
        bk = _t5_bucket_np(np.clip(rel, -128, 128))
        valid = np.abs(rel) <= 128
        for h in range(8):
            bias[:, h, t, :] = t5[bk, h]
        mask[:, :, t, :] = np.where(valid, 0.0, NEG)[:, None, :]
    return bias.reshape(128, -1), mask.reshape(128, -1)


def a_plan(i, T):
    if T < 6:
        raise ValueError("sequence too short")
    if 2 <= i <= T - 3:
        return list(range(i - 2, i + 3)), [(2, 7)]
    if i == 0:
        return [0, 1, 2, 3], [(4, 6), (7, 9)]
    if i == 1:
        return [0, 1, 2, 3], [(3, 6), (7, 8)]
    if i == T - 2:
        return list(range(T - 4, T)), [(1, 2), (3, 6)]
    return list(range(T - 4, T)), [(0, 2), (3, 5)]


def build_program(seq_tiles, dbg=None):
    NT = sum(seq_tiles)
    nc = bass.Bass("TRN2", target_bir_lowering=False, dynamic_dma_scratch_size=256)

    def dram(name, shape, kind):
        return nc.dram_tensor(name, shape, F32, kind=kind).ap()

    x_d = dram("x", [NT * 128, D_MODEL], "ExternalInput")
    y_d = dram("y", [NT * 128, D_MODEL], "ExternalOutput")
    win_d = dram("w_in", [D_MODEL, D_IN], "ExternalInput")
    woa_d = dram("w_oa", [512, D_MODEL], "ExternalInput")
    wob_d = dram("w_ob", [512, D_MODEL], "ExternalInput")
    wout_d = dram("w_out", [D_MODEL, D_MODEL], "ExternalInput")
    small_d = dram("small", [128, 160], "ExternalInput")
    uab_d = dram("ua_bias", [128, 8 * N_UA * 128], "ExternalInput")
    uam_d = dram("ua_mask", [128, 8 * N_UA * 128], "ExternalInput")
    tbb_d = dram("tb_bias", [128, 8 * 3 * 128], "ExternalInput")
    tbm_d = dram("tb_mask", [128, 8 * 3 * 128], "ExternalInput")

    with ExitStack() as st:
        S = Sched(nc, st)

        def sb(name, shape, dt):
            return st.enter_context(nc.sbuf_tensor("sb_" + name, shape, dt))

        win = sb("win", [128, 8, D_IN], BF16); win_r = [Res(f"win{j}") for j in range(6)]
        woa = sb("woa", [128, 4, D_MODEL], BF16); woa_r = Res("woa")
        wob = sb("wob", [128, 4, D_MODEL], BF16); wob_r = Res("wob")
        wout = sb("wout", [128, 8, D_MODEL], BF16); wout_r = Res("wout")
        UA = sb("UA", [128, 8, N_UA * 128], BF16); ua_r = Res("UA")
        TB = sb("TB", [128, 8, 3 * 128], BF16); tb_r = Res("TB")
        small = sb("small", [128, 160], F32); small_r = Res("small")
        identf = small[:, 0:128]
        gcol = small[:, 128:136]
        cvec = small[:, 136:140]
        sinkv = small[:, 140:148]
        identb = sb("identb", [128, 128], BF16); identb_r = Res("identb")
        misc = sb("misc", [128, 64], F32); misc_r = Res("misc")
        cab = misc[:, 0:2]
        esink = misc[:, 2:10]
        cm05 = misc[:, 16:32]
        xf = [sb(f"xf{i}", [128, D_MODEL], F32) for i in range(1)]
        xf_r = [Res(f"xf{i}") for i in range(1)]
        xf_sem = [S.dma_sem(f"d_xf{i}") for i in range(1)]
        qf = sb("qf", [128, 1024], F32); qf_r = Res("qf")
        xr = sb("xr", [128, D_MODEL], F32); xr_r = Res("xr"); xr_sem = S.dma_sem("d_xr"); st_sem = S.dma_sem("d_st")
        hT = [sb(f"hT{i}", [128, 8, 128], BF16) for i in range(HT_R)]
        hT_r = [(Res(f"hTa{i}"), Res(f"hTb{i}")) for i in range(HT_R)]
        stat = sb("stat", [128, 3 * HT_R], F32)
        ss_r = [Res(f"ss{i}") for i in range(HT_R)]
        rstd_r = [Res(f"rstd{i}") for i in range(HT_R)]
        kTa = [sb(f"kTa{i}", [128, 4, 128], BF16) for i in range(KV_R)]
        va = [sb(f"va{i}", [128, 8, 65], BF16) for i in range(KV_R)]
        kTb = [sb(f"kTb{i}", [128, 2, 128], BF16) for i in range(KV_R)]
        vb = [sb(f"vb{i}", [128, 2, 65], BF16) for i in range(KV_R)]
        kTa_r = [Res(f"kTa{i}") for i in range(KV_R)]
        va_r = [Res(f"va{i}") for i in range(KV_R)]
        kTb_r = [Res(f"kTb{i}") for i in range(KV_R)]
        vb_r = [Res(f"vb{i}") for i in range(KV_R)]
        sq = sb("sq", [128, 1024], F32); sq_r = Res("sq")
        ssq = sb("ssq", [128, 32], F32); ssq_r = (Res("ssq_q"), Res("ssq_k"))
        nrm = sb("nrm", [128, 1024], BF16); nrm_r = Res("nrm")
        nrmk = sb("nrmk", [128, 640], BF16); nrmk_r = Res("nrmk")
        kbf = sb("kbf", [128, 128], F32); kbf_r = Res("kbf")
        kf = sb("kf", [128, 512], F32); kf_r = Res("kf")
        sqk = sb("sqk", [128, 640], BF16); sqk_r = Res("sqk")
        junk = sqk[:].bitcast(mybir.dt.int8)[:, 0:1024]
        qT2 = [sb(f"qT{i}", [128, 8, 128], BF16) for i in range(2)]; qT2_r = [(Res(f"qTa{i}"), Res(f"qTb{i}")) for i in range(2)]
        pT = [sb(f"pT{i}", [128, 640], BF16) for i in range(2)]; pT_r = [Res(f"pT{i}") for i in range(2)]
        onorm = [sb(f"onorm{i}", [128, 1024], F32) for i in range(2)]
        onorm_r = [[Res(f"onormA{i}"), Res(f"onormB{i}")] for i in range(2)]
        rden = sb("rden", [128, 8], F32); rden_r = Res("rden")
        tmp = [sb(f"tmp{i}", [128, 512], F32) for i in range(3)]; tmp_r = [Res(f"tmp{i}") for i in range(3)]
        ubuf = sb("ubuf", [128, 1024], BF16); u_r = Res("u")
        uT = sb("uT", [128, 8, 128], BF16); uT_r = (Res("uTa"), Res("uTb"))
        mbuf, m_r = ubuf, u_r
        mT, mT_r = uT, uT_r
        ps = st.enter_context(nc.psum_tensor("ps", [128, 4096], F32))
        bank_r = [Res(f"bank{b}", excl=True) for b in range(8)]
        B_S, B_O, B_PJ0, B_PJ1, B_T = 0, 4, 5, 6, 7

        def bank(b, lo=0, hi=512):
            return ps[:, b * 512 + lo: b * 512 + hi]

        def bank_bf(b):
            return ps[:, b * 512:(b + 1) * 512].bitcast(BF16)

        pj_ctr = [0]

        def next_pj():
            b = B_PJ0 + (pj_ctr[0] % 2)
            pj_ctr[0] += 1
            return b

        d_small = S.dma_sem("d_small")
        S.op("sp", lambda e: e.dma_start(out=small[:], in_=small_d[:, :]), writes=[small_r], dma=d_small)
        S.op("sp", lambda e: e.dma_start(out=xf[0][:], in_=x_d[0:128, :]), writes=[xf_r[0]], dma=xf_sem[0])
        S.op("dve", lambda e: e.tensor_copy(out=identb[:], in_=identf), reads=[small_r], writes=[identb_r])
        S.op("pool", lambda e: e.memset(misc[:], -0.5), writes=[misc_r])
        S.op("dve", lambda e: e.scalar_tensor_tensor(out=cab[:, 0:1], in0=cvec[:, 0:1], scalar=0.125, in1=cvec[:, 1:2],
                                                      op0=ALU.mult, op1=ALU.mult), reads=[small_r], writes=[misc_r])
        S.op("dve", lambda e: e.scalar_tensor_tensor(out=cab[:, 1:2], in0=cvec[:, 2:3], scalar=0.125, in1=cvec[:, 3:4],
                                                      op0=ALU.mult, op1=ALU.mult), reads=[small_r], writes=[misc_r])
        S.op("act", lambda e: e.activation(out=esink, in_=sinkv, func=AF.Exp), reads=[small_r], writes=[misc_r])
        for i in range(KV_R):
            S.op("pool", lambda e, i=i: e.memset(va[i][:, :, 64:65], 1.0), writes=[va_r[i]])
            S.op("pool", lambda e, i=i: e.memset(vb[i][:, :, 64:65], 1.0), writes=[vb_r[i]])

        stg_all = [(xr, [xr_r], xr_sem), (onorm[0], onorm_r[0], S.dma_sem("d_on0")), (onorm[1], onorm_r[1], S.dma_sem("d_on1")),
                   (qf, [qf_r], S.dma_sem("d_qf")), (sq, [sq_r], S.dma_sem("d_sq"))]
        stg_i = [0]
        cast_i = [0]

        def stage_load(src_ap, n, nslots):
            k = stg_i[0] % nslots
            stg_i[0] += 1
            t, r, sem = stg_all[k]
            S.op("sp", lambda e: e.dma_start(out=t[:, 0:n], in_=src_ap), writes=list(r), dma=sem)
            return t, list(r)

        def cast_scaled(dst_ap, dst_r, src_t, src_r, n, scale, scale_r=()):
            which = ("dve", "pool", "act")[cast_i[0] % 3]
            cast_i[0] += 1
            if which == "act":
                S.op("act", lambda e: e.activation(out=dst_ap, in_=src_t[:, 0:n], func=AF.Copy, scale=scale),
                     reads=[*src_r, *scale_r], writes=[dst_r])
            else:
                S.op(which, lambda e: e.tensor_scalar(out=dst_ap, in0=src_t[:, 0:n], scalar1=scale, scalar2=0.0,
                                                       op0=ALU.mult, op1=ALU.add),
                     reads=[*src_r, *scale_r], writes=[dst_r])

        def stage_win(pieces, nslots):
            for c in range(8):
                for j in pieces:
                    lo = j * 1024
                    n = min(1024, D_IN - lo)
                    t, r = stage_load(win_d[c * 128:(c + 1) * 128, lo:lo + n], n, nslots)
                    cast_scaled(win[:, c, lo:lo + n], win_r[j], t, r, n, gcol[:, c:c + 1], [small_r])
                    yield

        UAf = UA[:].rearrange("p h n -> p (h n)")
        TBf = TB[:].rearrange("p h n -> p (h n)")

        def stage_tables(nslots):
            for (dst, dst_r, bd, md, tot) in ((UAf, ua_r, uab_d, uam_d, 8 * N_UA * 128), (TBf, tb_r, tbb_d, tbm_d, 8 * 3 * 128)):
                for lo in range(0, tot, 1024):
                    n = min(1024, tot - lo)
                    t1_, r1_ = stage_load(bd[:, lo:lo + n], n, nslots)
                    t2_, r2_ = stage_load(md[:, lo:lo + n], n, nslots)
                    which = ("dve", "pool")[(lo // 1024) % 2]
                    S.op(which, lambda e, n=n, t1_=t1_, t2_=t2_: e.tensor_tensor(
                        out=t1_[:, 0:n], in0=t1_[:, 0:n], in1=t2_[:, 0:n], op=ALU.add),
                        reads=[*r2_], writes=[*r1_])
                    S.op("act", lambda e, dst=dst, lo=lo, n=n, t1_=t1_: e.activation(out=dst[:, lo:lo + n], in_=t1_[:, 0:n], func=AF.Exp),
                         reads=[*r1_], writes=[dst_r])
                    yield

        def stage_rest(nslots):
            yield from stage_tables(nslots)
            yield from stage_win([3, 4, 5], nslots)
            for c in range(4):
                t, r = stage_load(woa_d[c * 128:(c + 1) * 128, :], 1024, nslots)
                cast_scaled(woa[:, c, :], woa_r, t, r, 1024, 0.5)
                yield
                t, r = stage_load(wob_d[c * 128:(c + 1) * 128, :], 1024, nslots)
                cast_scaled(wob[:, c, :], wob_r, t, r, 1024, 0.5)
                yield
            for c in range(8):
                t, r = stage_load(wout_d[c * 128:(c + 1) * 128, :], 1024, nslots)
                cast_scaled(wout[:, c, :], wout_r, t, r, 1024, 0.5)
                yield

        tiles = []
        base = 0
        for T in seq_tiles:
            for i in range(T):
                tiles.append((base, i, T))
            base += T

        def load_xf(t):
            if t < NT:
                s2 = 0
                S.op("sp", lambda e: e.dma_start(out=xf[s2][:], in_=x_d[t * 128:(t + 1) * 128, :]), writes=[xf_r[s2]], dma=xf_sem[s2])

        def proj_chunk(t, col, n, b=None):
            s6 = t % HT_R
            if b is None:
                b = next_pj()

            def f(e):
                ins = None
                for c in range(8):
                    ins = e.matmul(bank(b, 0, n), lhsT=hT[s6][:, c, :], rhs=win[:, c, col:col + n], start=(c == 0), stop=(c == 7))
                return ins
            S.op("pe", f, reads=[hT_r[s6][0], hT_r[s6][1]] + [win_r[j] for j in range(col // 1024, (col + n - 1) // 1024 + 1)],
                 writes=[bank_r[b]])
            return b

        def rstd_from(ssq_ap, nheads, r):
            S.op("pool", lambda e: e.tensor_scalar(out=ssq_ap, in0=ssq_ap, scalar1=1.0 / 64, scalar2=EPS, op0=ALU.mult, op1=ALU.add),
                 reads=[r], writes=[r])
            S.op("pool", lambda e: e.tensor_tensor(out=ssq_ap, in0=ssq_ap, in1=cm05[:, 0:nheads], op=ALU.pow),
                 reads=[r, misc_r], writes=[r])

        def transposes_bf(src, src_r, nblk, bT):
            bb = bank_bf(bT)

            def f(e):
                ins = None
                for c in range(nblk):
                    ins = e.transpose(out=bb[:, c * 128:(c + 1) * 128], in_=src[:, c * 128:(c + 1) * 128], identity=identb[:])
                return ins
            S.op("pe", f, reads=[src_r, identb_r], writes=[bank_r[bT]])
            return bb

        def evac_T(dst, dst_r, bT):
            bb = bank_bf(bT)
            S.op("dve", lambda e: e.tensor_copy(out=dst[:].rearrange("p a b -> p (a b)"), in_=bb[:, 0:1024]),
                 reads=[bank_r[bT]], writes=[dst_r[0], dst_r[1]])

        pj_free = [B_PJ0, B_PJ1]
        t_free = [B_T]
        done_f = set()
        done_q = set()

        def acquire(pool):
            while not pool:
                yield
            return pool.pop(0)

        def proj_into(t, col, n, b):
            proj_chunk(t, col, n, b=b)

        def F(t):
            s2, s6, s8 = 0, t % HT_R, t % KV_R
            ssc = stat[:, s6:s6 + 1]
            rstd = stat[:, HT_R + s6:HT_R + s6 + 1]
            rstdh = stat[:, 2 * HT_R + s6:2 * HT_R + s6 + 1]
            S.op("act", lambda e: e.activation(out=junk, in_=xf[s2][:], func=AF.Square, accum_out=ssc),
                 reads=[xf_r[s2]], writes=[ss_r[s6], sqk_r], standalone=True)
            yield
            S.op("pool", lambda e: e.tensor_scalar(out=rstd, in0=ssc, scalar1=1.0 / D_MODEL, scalar2=EPS,
                                                   op0=ALU.mult, op1=ALU.add), reads=[ss_r[s6]], writes=[rstd_r[s6]])
            S.op("pool", lambda e: e.tensor_tensor(out=rstd, in0=rstd, in1=cm05[:, 0:1], op=ALU.pow),
                 reads=[rstd_r[s6], misc_r], writes=[rstd_r[s6]])
            S.op("pool", lambda e: e.tensor_scalar(out=rstdh, in0=rstd, scalar1=0.5, scalar2=0.0,
                                                   op0=ALU.mult, op1=ALU.add), reads=[rstd_r[s6]], writes=[rstd_r[s6]])
            for half in range(2):
                bT = yield from acquire(t_free)

                def f(e, half=half, bT=bT):
                    ins = None
                    for c in range(4):
                        cc = half * 4 + c
                        ins = e.transpose(out=bank(bT, c * 128, (c + 1) * 128), in_=xf[s2][:, cc * 128:(cc + 1) * 128], identity=identf)
                    return ins
                S.op("pe", f, reads=[xf_r[s2], small_r], writes=[bank_r[bT]])
                yield
                dst = hT[s6][:, half * 4:(half + 1) * 4, :].rearrange("p a b -> p (a b)")
                if half == 0:
                    S.op("act", lambda e, dst=dst, bT=bT: e.activation(out=dst, in_=bank(bT), func=AF.Copy),
                         reads=[bank_r[bT]], writes=[hT_r[s6][0]])
                else:
                    S.op("dve", lambda e, dst=dst, bT=bT: e.tensor_copy(out=dst, in_=bank(bT)),
                         reads=[bank_r[bT]], writes=[hT_r[s6][1]])
                t_free.append(bT)
            load_xf(t + 1)
            b2 = yield from acquire(pj_free)
            proj_into(t, C_VA, 512, b2)
            yield
            S.op("act", lambda e: e.activation(out=va[s8][:, :, 0:64], in_=bank(b2).rearrange("p (h d) -> p h d", d=64),
                                               func=AF.Copy, scale=rstd),
                 reads=[bank_r[b2], rstd_r[s6]], writes=[va_r[s8]])
            pj_free.append(b2)
            b3 = yield from acquire(pj_free)
            proj_into(t, C_KB, 256, b3)
            yield
            S.op("act", lambda e: e.activation(out=kbf[:], in_=bank(b3, 0, 128), func=AF.Copy, scale=rstd),
                 reads=[bank_r[b3], rstd_r[s6]], writes=[kbf_r])
            S.op("act", lambda e: e.activation(out=vb[s8][:, :, 0:64], in_=bank(b3, 128, 256).rearrange("p (h d) -> p h d", d=64),
                                               func=AF.Copy, scale=rstd),
                 reads=[bank_r[b3], rstd_r[s6]], writes=[vb_r[s8]])
            pj_free.append(b3)
            b = yield from acquire(pj_free)
            proj_into(t, C_KA, 512, b)
            yield
            S.op("act", lambda e: e.activation(out=kf[:], in_=bank(b), func=AF.Copy, scale=rstd),
                 reads=[bank_r[b], rstd_r[s6]], writes=[kf_r])
            pj_free.append(b)
            yield
            S.op("pool", lambda e: e.tensor_tensor(out=sqk[:, 0:512], in0=kf[:], in1=kf[:], op=ALU.mult), reads=[kf_r], writes=[sqk_r])
            S.op("pool", lambda e: e.tensor_tensor(out=sqk[:, 512:640], in0=kbf[:], in1=kbf[:], op=ALU.mult),
                 reads=[kbf_r], writes=[sqk_r])
            yield
            yield
            S.op("dve", lambda e: e.tensor_reduce(out=ssq[:, 16:26], in_=sqk[:, 0:640].rearrange("p (h d) -> p h d", d=64),
                                                  axis=AX.X, op=ALU.add), reads=[sqk_r], writes=[ssq_r[1]])
            yield
            rstd_from(ssq[:, 16:26], 10, ssq_r[1])
            yield
            yield
            S.op("pool", lambda e: e.tensor_tensor(out=nrmk[:, 0:512].rearrange("p (h d) -> p h d", d=64),
                                                   in0=kf[:].rearrange("p (h d) -> p h d", d=64),
                                                   in1=ssq[:, 16:24].unsqueeze(2).broadcast_to([128, 8, 64]), op=ALU.mult),
                 reads=[kf_r, ssq_r[1]], writes=[nrmk_r])
            S.op("pool", lambda e: e.tensor_tensor(out=nrmk[:, 512:640].rearrange("p (h d) -> p h d", d=64),
                                                   in0=kbf[:].rearrange("p (h d) -> p h d", d=64),
                                                   in1=ssq[:, 24:26].unsqueeze(2).broadcast_to([128, 2, 64]), op=ALU.mult),
                 reads=[kbf_r, ssq_r[1]], writes=[nrmk_r])
            yield
            yield
            bT = yield from acquire(t_free)
            bb = bank_bf(bT)
            transposes_bf(nrmk, nrmk_r, 5, bT)
            yield
            S.op("act", lambda e: e.activation(out=kTa[s8][:].rearrange("p a b -> p (a b)"), in_=bb[:, 0:512], func=AF.Copy,
                                               scale=cab[:, 0:1]),
                 reads=[bank_r[bT], misc_r], writes=[kTa_r[s8]])
            for kv in range(2):
                for dh in range(2):
                    src = bb[kv * 64:(kv + 1) * 64, 512:640]
                    dst = kTb[s8][dh * 64:(dh + 1) * 64, kv, :]
                    sc = cab[kv * 64:(kv + 1) * 64, 1:2]
                    if dh == 0:
                        S.op("act", lambda e, src=src, dst=dst, sc=sc: e.activation(out=dst, in_=src, func=AF.Copy, scale=sc),
                             reads=[bank_r[bT], misc_r], writes=[kTb_r[s8]])
                    else:
                        S.op("dve", lambda e, src=src, dst=dst, sc=sc: e.tensor_scalar(out=dst, in0=src, scalar1=sc, scalar2=None,
                                                                                       op0=ALU.mult),
                             reads=[bank_r[bT], misc_r], writes=[kTb_r[s8]])
            t_free.append(bT)
            done_f.add(t)
            yield

        def att_jobs(t, which):
            base, i, T = tiles[t]
            on, on_r = onorm[t % 2], onorm_r[t % 2]
            qT, qT_r = qT2[t % 2], qT2_r[t % 2]
            if which == "A":
                J, pieces = a_plan(i, T)
                tab, tab_r = UA, ua_r
            else:
                J = [j for j in (i - 1, i, i + 1) if 0 <= j < T]
                t0 = J[0] - i + 1
                pieces = [(t0, t0 + len(J))]
                tab, tab_r = TB, tb_r
            n = len(J)
            slots = [(base + j) % KV_R for j in J]
            br = 0 if which == "A" else 1

            def sinfo(k):
                sbanks = [B_S + 2 * k] + ([B_S + 2 * k + 1] if n > 4 else [])
                return sbanks, (B_S + 2 * k) * 512

            def qk(h, k):
                sbanks, soff = sinfo(k)
                half = h % 2
                qc = h // 2 if which == "A" else 4 + h // 2
                kr = [(kTa_r if which == "A" else kTb_r)[s] for s in slots]

                def fqk(e):
                    ins = None
                    for blk, s in enumerate(slots):
                        if which == "A":
                            lhs = kTa[s][half * 64:(half + 1) * 64, qc, :]
                        else:
                            lhs = kTb[s][half * 64:(half + 1) * 64, h // 4, :]
                        ins = e.matmul(ps[:, soff + blk * 128: soff + (blk + 1) * 128], lhsT=lhs,
                                       rhs=qT[half * 64:(half + 1) * 64, qc, :], start=True, stop=True)
                    return ins
                S.op("pe", fqk, reads=[*kr, qT_r[0], qT_r[1]], writes=[bank_r[b] for b in sbanks])

            def softmax(h, k):
                sbanks, soff = sinfo(k)
                S.op("act", lambda e: e.activation(out=pT[k][:, 0:n * 128], in_=ps[:, soff: soff + n * 128], func=AF.Exp),
                     reads=[bank_r[b] for b in sbanks], writes=[pT_r[k]])
                col = 0
                for (a0, a1) in pieces:
                    w = (a1 - a0) * 128
                    S.op("dve", lambda e, col=col, w=w, a0=a0, a1=a1: e.tensor_tensor(
                        out=pT[k][:, col:col + w], in0=pT[k][:, col:col + w],
                        in1=tab[:, h, a0 * 128:a1 * 128], op=ALU.mult),
                        reads=[pT_r[k], tab_r], writes=[pT_r[k]])
                    col += w

            def pv(h, k):
                vr = [(va_r if which == "A" else vb_r)[s] for s in slots]
                hh = h % 4

                def fpv(e):
                    ins = None
                    for blk, s in enumerate(slots):
                        rhs = va[s][:, h, :] if which == "A" else vb[s][:, h // 4, :]
                        ins = e.matmul(bank(B_O, hh * 65, (hh + 1) * 65), lhsT=pT[k][:, blk * 128:(blk + 1) * 128], rhs=rhs,
                                       start=(blk == 0), stop=(blk == n - 1))
                    return ins
                S.op("pe", fpv, reads=[pT_r[k], *vr], writes=[bank_r[B_O]])
                if hh == 3:
                    g = h // 4
                    ov = bank(B_O, 0, 260).rearrange("p (h d) -> p h d", d=65)
                    rd = rden[:, g * 4:(g + 1) * 4]
                    if which == "A":
                        S.op("dve", lambda e: e.reciprocal(out=rd.unsqueeze(2), in_=ov[:, :, 64:65]),
                             reads=[bank_r[B_O]], writes=[rden_r])
                    else:
                        S.op("dve", lambda e: e.tensor_tensor(out=rd.unsqueeze(2), in0=ov[:, :, 64:65],
                                                              in1=esink[:, g * 4:(g + 1) * 4].unsqueeze(2), op=ALU.add),
                             reads=[bank_r[B_O], misc_r], writes=[rden_r])
                        S.op("dve", lambda e: e.reciprocal(out=rd, in_=rd), reads=[rden_r], writes=[rden_r])
                    S.op("dve", lambda e: e.tensor_tensor(
                        out=on[:, br * 512 + g * 256: br * 512 + (g + 1) * 256].rearrange("p (h d) -> p h d", d=64),
                        in0=ov[:, :, 0:64], in1=rd.unsqueeze(2).broadcast_to([128, 4, 64]), op=ALU.mult),
                        reads=[bank_r[B_O], rden_r], writes=[on_r[br]])

            return [(lambda k, h=h: qk(h, k), lambda k, h=h: softmax(h, k), lambda k, h=h: pv(h, k)) for h in range(8)]

        def Gq(t):
            s6 = t % HT_R
            rstd = stat[:, HT_R + s6:HT_R + s6 + 1]
            for (col, lo) in ((C_QA, 0), (C_QB, 512)):
                b = yield from acquire(pj_free)
                proj_into(t, col, 512, b)
                yield
                S.op("act", lambda e, b=b, lo=lo: e.activation(out=qf[:, lo:lo + 512], in_=bank(b), func=AF.Copy, scale=rstd),
                     reads=[bank_r[b], rstd_r[s6]], writes=[qf_r])
                pj_free.append(b)
            yield
            S.op("pool", lambda e: e.tensor_tensor(out=sq[:], in0=qf[:], in1=qf[:], op=ALU.mult), reads=[qf_r], writes=[sq_r])
            yield
            yield
            S.op("dve", lambda e: e.tensor_reduce(out=ssq[:, 0:16], in_=sq[:].rearrange("p (h d) -> p h d", d=64),
                                                  axis=AX.X, op=ALU.add), reads=[sq_r], writes=[ssq_r[0]])
            yield
            rstd_from(ssq[:, 0:16], 16, ssq_r[0])
            yield
            yield
            S.op("pool", lambda e: e.tensor_tensor(out=nrm[:].rearrange("p (h d) -> p h d", d=64),
                                                   in0=qf[:].rearrange("p (h d) -> p h d", d=64),
                                                   in1=ssq[:, 0:16].unsqueeze(2).broadcast_to([128, 16, 64]), op=ALU.mult),
                 reads=[qf_r, ssq_r[0]], writes=[nrm_r])
            yield
            yield
            yield
            bT = yield from acquire(t_free)
            transposes_bf(nrm, nrm_r, 8, bT)
            yield
            evac_T(qT2[t % 2], qT2_r[t % 2], bT)
            t_free.append(bT)
            done_q.add(t)
            yield

        def at_prologue(t):
            jobs = att_jobs(t, "A")
            jobs[0][0](0)
            jobs[0][1](0)
            jobs[1][0](1)

        def At(t):
            jobs = att_jobs(t, "A") + att_jobs(t, "B")
            nxt = None
            for j in range(16):
                if j + 2 >= 16 and t + 1 < NT and nxt is None:
                    while not ((t + 1) in done_q and (min(t + LEAD, NT - 1)) in done_f):
                        yield
                    nxt = att_jobs(t + 1, "A")[0:2]
                    jobs = jobs + nxt
                if j + 1 < len(jobs):
                    jobs[j + 1][1]((j + 1) % 2)
                jobs[j][2](j % 2)
                if j + 2 < len(jobs):
                    jobs[j + 2][0](j % 2)
                yield

        def Gb(t):
            s6 = t % HT_R
            rstd = stat[:, HT_R + s6:HT_R + s6 + 1]
            rstdh = stat[:, 2 * HT_R + s6:2 * HT_R + s6 + 1]
            on, on_r = onorm[t % 2], onorm_r[t % 2]
            S.op("sp", lambda e: e.dma_start(out=xr[:], in_=x_d[t * 128:(t + 1) * 128, :]), writes=[xr_r], dma=xr_sem)
            for br, col in ((0, C_ZA), (1, C_ZB)):
                b = yield from acquire(pj_free)
                proj_into(t, col, 512, b)
                yield
                th, th_r = tmp[0], tmp_r[0]
                t1, t1_r = tmp[1], tmp_r[1]
                S.op("act", lambda e, b=b, th=th: e.activation(out=th[:], in_=bank(b), func=AF.Tanh, scale=rstdh),
                     reads=[bank_r[b], rstd_r[s6]], writes=[th_r])
                S.op("dve", lambda e, b=b, t1=t1, br=br: e.scalar_tensor_tensor(out=t1[:], in0=bank(b), scalar=rstd,
                                                                                in1=on[:, br * 512:(br + 1) * 512],
                                                                                op0=ALU.mult, op1=ALU.mult),
                     reads=[bank_r[b], rstd_r[s6], on_r[br]], writes=[t1_r])
                pj_free.append(b)
                yield
                S.op("dve", lambda e, th=th, t1=t1, br=br: e.scalar_tensor_tensor(out=ubuf[:, br * 512:(br + 1) * 512], in0=th[:], scalar=1.0,
                                                                                  in1=t1[:], op0=ALU.add, op1=ALU.mult),
                     reads=[th_r, t1_r], writes=[u_r])
            bT = yield from acquire(t_free)
            transposes_bf(ubuf, u_r, 8, bT)
            yield
            evac_T(uT, uT_r, bT)
            t_free.append(bT)
            yield
            for nh in range(2):
                for br, gcol_, wo, wo_r in ((0, C_GA, woa, woa_r), (1, C_GB, wob, wob_r)):
                    b = yield from acquire(pj_free)
                    proj_into(t, gcol_ + nh * 512, 512, b)
                    yield
                    th, th_r = tmp[0], tmp_r[0]
                    S.op("act", lambda e, b=b, th=th: e.activation(out=th[:], in_=bank(b), func=AF.Tanh, scale=rstdh),
                         reads=[bank_r[b], rstd_r[s6]], writes=[th_r])
                    pj_free.append(b)
                    b2 = yield from acquire(pj_free)

                    def fo(e, br=br, wo=wo, b2=b2, nh=nh):
                        ins = None
                        for c in range(4):
                            ins = e.matmul(bank(b2), lhsT=uT[:, br * 4 + c, :], rhs=wo[:, c, nh * 512:(nh + 1) * 512],
                                           start=(c == 0), stop=(c == 3))
                        return ins
                    S.op("pe", fo, reads=[uT_r[br], wo_r], writes=[bank_r[b2]])
                    yield
                    m1, m1_r = tmp[1 + br], tmp_r[1 + br]
                    S.op("dve", lambda e, th=th, m1=m1, b2=b2: e.scalar_tensor_tensor(out=m1[:], in0=th[:], scalar=1.0, in1=bank(b2),
                                                                                      op0=ALU.add, op1=ALU.mult),
                         reads=[th_r, bank_r[b2]], writes=[m1_r])
                    pj_free.append(b2)
                yield
                S.op("pool", lambda e, nh=nh: e.tensor_tensor(out=mbuf[:, nh * 512:(nh + 1) * 512], in0=tmp[1][:], in1=tmp[2][:], op=ALU.add),
                     reads=[tmp_r[1], tmp_r[2]], writes=[m_r])
                yield
            bT = yield from acquire(t_free)
            transposes_bf(mbuf, m_r, 8, bT)
            yield
            evac_T(mT, mT_r, bT)
            t_free.append(bT)
            yield
            for nh in range(2):
                b = yield from acquire(pj_free)

                def fw(e, b=b, nh=nh):
                    ins = None
                    for c in range(8):
                        ins = e.matmul(bank(b), lhsT=mT[:, c, :], rhs=wout[:, c, nh * 512:(nh + 1) * 512], start=(c == 0), stop=(c == 7))
                    return ins
                S.op("pe", fw, reads=[mT_r[0], mT_r[1], wout_r], writes=[bank_r[b]])
                yield
                S.op("dve", lambda e, b=b, nh=nh: e.tensor_tensor(out=xr[:, nh * 512:(nh + 1) * 512], in0=bank(b),
                                                                  in1=xr[:, nh * 512:(nh + 1) * 512], op=ALU.add),
                     reads=[bank_r[b], xr_r], writes=[xr_r])
                pj_free.append(b)
            last_store[0] = S.op("sp", lambda e: e.dma_start(out=y_d[t * 128:(t + 1) * 128, :], in_=xr[:]), reads=[xr_r], dma=st_sem)

        def dump(t):
            s6, s8 = t % HT_R, t % KV_R
            items = [("stat", stat[:], F32, [128, 3 * HT_R], []), ("hT", hT[s6][:].rearrange("p a b -> p (a b)"), BF16, [128, 1024], hT_r[s6]),
                     ("qT", qT2[t % 2][:].rearrange("p a b -> p (a b)"), BF16, [128, 1024], qT2_r[t % 2]),
                     ("kTa", kTa[s8][:].rearrange("p a b -> p (a b)"), BF16, [128, 512], [kTa_r[s8]]),
                     ("va", va[s8][:].rearrange("p a b -> p (a b)"), BF16, [128, 520], [va_r[s8]]),
                     ("kTb", kTb[s8][:].rearrange("p a b -> p (a b)"), BF16, [128, 256], [kTb_r[s8]]),
                     ("vb", vb[s8][:].rearrange("p a b -> p (a b)"), BF16, [128, 130], [vb_r[s8]]),
                     ("onorm", onorm[t % 2][:], F32, [128, 1024], onorm_r[t % 2]), ("ubuf", ubuf[:], BF16, [128, 1024], [u_r]),
                     ("mbuf", mbuf[:], BF16, [128, 1024], [m_r])]
            evs = []
            dsem = S.dma_sem("d_dbg")
            for name, ap, dt, shape, rs in items:
                d = nc.dram_tensor("dbg_" + name, shape, dt, kind="ExternalOutput").ap()
                evs.append(S.op("sp", lambda e, d=d, ap=ap: e.dma_start(out=d[:, :], in_=ap), reads=list(rs), dma=dsem))
            return evs[-1]

        last_store = [None]

        def count_steps(mk):
            S.dry = True
            saved = (pj_ctr[0], list(pj_free), list(t_free), set(done_f), set(done_q))
            n = sum(1 for _ in mk())
            pj_ctr[0] = saved[0]
            pj_free[:] = saved[1]
            t_free[:] = saved[2]
            done_f.clear(); done_f.update(saved[3])
            done_q.clear(); done_q.update(saved[4])
            S.dry = False
            return n + 1

        def run_all(makers):
            makers = [m for m in makers if m is not None]
            order = []
            for pri, (mk, span, dry_ok) in enumerate(makers):
                n = count_steps(mk) if dry_ok is True else (17 if dry_ok is False else dry_ok)
                for j in range(n):
                    order.append(((j + 0.5) / n * span, pri))
            order.sort()
            gens = [mk() for mk, _, _ in makers]
            alive = [True] * len(gens)
            trace_seq = []
            for _, pri in order:
                if alive[pri]:
                    trace_seq.append(pri)
                    try:
                        next(gens[pri])
                    except StopIteration:
                        alive[pri] = False
            guard = 0
            while any(alive):
                for pri in range(len(gens)):
                    if alive[pri]:
                        trace_seq.append(10 + pri)
                        try:
                            next(gens[pri])
                        except StopIteration:
                            alive[pri] = False
                guard += 1
                assert guard < 10000, "stream scheduling deadlock"
            if len(makers) == 4 and DEBUG_SEQ:
                print("SEQ", "".join("ABQF"[p] if p < 10 else "abqf"[p - 10] for p in trace_seq))

        LEAD = 4
        for _ in stage_win([0, 1, 2], 5):
            pass
        rest = stage_rest(3)
        stg_i[0] = 0

        def take(n):
            for _ in range(n):
                try:
                    next(rest)
                except StopIteration:
                    return
                yield

        for t in range(min(LEAD, NT)):
            run_all([(lambda t=t: F(t), 1.0, True), (lambda: take(14), 1.0, 15)])
        run_all([(lambda: Gq(0), 1.0, True), (lambda: take(14), 1.0, 15)])
        for _ in rest:
            pass
        at_prologue(0)
        dbg_ev = None
        for t in range(NT + 1):
            run_all([((lambda t=t: At(t)), 0.88, False) if t < NT else None,
                     ((lambda t=t: Gb(t - 1)), 1.0, True) if t >= 1 else None,
                     ((lambda t=t: Gq(t + 1)), 0.68, True) if t + 1 < NT else None,
                     ((lambda t=t: F(t + LEAD)), 0.68, True) if t + LEAD < NT else None])
            if dbg is not None and t - 1 == dbg:
                dbg_ev = dump(dbg)
        S.final_wait("sp", [last_store[0], dbg_ev])
        S.emit()
    return nc


def _shared_inputs(norm_g, w_in, qn_a, kn_a, rpb_a, qn_b, kn_b, sink_b, w_o_a, w_o_b, w_out, t5_table):
    f = lambda a: np.ascontiguousarray(np.asarray(a, dtype=np.float32))
    small = np.zeros((128, 160), np.float32)
    small[:, 0:128] = np.eye(128, dtype=np.float32)
    small[:, 128:136] = f(norm_g)[0].reshape(8, 128).T
    small[:, 136] = np.tile(f(qn_a)[0], 2)
    small[:, 137] = np.tile(f(kn_a)[0], 2)
    small[:, 138] = np.tile(f(qn_b)[0], 2)
    small[:, 139] = np.tile(f(kn_b)[0], 2)
    small[:, 140:148] = f(sink_b)[0][None, :]
    uab, uam = _a_tables(f(rpb_a)[0])
    tbb, tbm = _b_tables(f(t5_table))
    return {"w_in": f(w_in)[0], "w_oa": f(w_o_a)[0], "w_ob": f(w_o_b)[0], "w_out": f(w_out)[0], "small": small,
            "ua_bias": uab, "ua_mask": uam, "tb_bias": tbb, "tb_mask": tbm}


_PROG_CACHE = {}


def run_layer(seqs_per_core, shared, n_cores, dbg=None):
    seq_tiles = tuple(s.shape[0] // 128 for s in seqs_per_core[0])
    if dbg is not None:
        nc = build_program(list(seq_tiles), dbg=dbg)
    else:
        if seq_tiles not in _PROG_CACHE:
            _PROG_CACHE[seq_tiles] = build_program(list(seq_tiles))
        nc = _PROG_CACHE[seq_tiles]
    in_maps = []
    for c in range(n_cores):
        m = dict(shared)
        m["x"] = np.ascontiguousarray(np.concatenate(seqs_per_core[c], axis=0), dtype=np.float32)
        in_maps.append(m)
    res = run_bass_kernel_spmd(nc, in_maps, core_ids=list(range(n_cores)))
    if dbg is not None:
        return res.results
    return [r["y"] for r in res.results]


def kernel(x_prompt, x_sample, norm_g, w_in, qn_a, kn_a, rpb_a, qn_b, kn_b, sink_b, w_o_a, w_o_b, w_out, t5_table):
    x_prompt = np.asarray(x_prompt, dtype=np.float32)
    x_sample = np.asarray(x_sample, dtype=np.float32)
    shared = _shared_inputs(norm_g, w_in, qn_a, kn_a, rpb_a, qn_b, kn_b, sink_b, w_o_a, w_o_b, w_out, t5_table)
    n = 8
    seqs = [[x_prompt[c], x_sample[c]] for c in range(n)]
    ys = run_layer(seqs, shared, n)
    Lp = x_prompt.shape[1]
    y_prompt = np.stack([ys[c][:Lp] for c in range(n)], axis=0)
    y_sample = np.stack([ys[c][Lp:] for c in range(n)], axis=0)
    return (y_prompt.astype(np.float32), y_sample.astype(np.float32))
```

```python
import math
from contextlib import ExitStack

import numpy as np
import concourse.bass as bass
import concourse.mybir as mybir
from concourse.bass_utils import run_bass_kernel_spmd

F32 = mybir.dt.float32
BF16 = mybir.dt.bfloat16
AF = mybir.ActivationFunctionType
ALU = mybir.AluOpType
AX = mybir.AxisListType

D_MODEL = 1024
D_IN = 5376
EPS = 1e-6
NEG = -30000.0
C_QA, C_KA, C_VA, C_ZA, C_QB, C_KB, C_VB, C_ZB, C_GA, C_GB = 0, 512, 1024, 1536, 2048, 2560, 2688, 2816, 3328, 4352
DEBUG_SEQ = False
KV_R = 8
HT_R = 6
N_UA = 9
UA_ORDER = [(-3, False), (-2, False), (-2, True), (-1, False), (0, False), (1, False), (2, True), (2, False), (3, False)]


class Res:
    __slots__ = ("name", "w", "r", "excl")

    def __init__(self, name, excl=False):
        self.name = name
        self.w = None
        self.r = {}
        self.excl = excl


class Sched:
    SAME_ENGINE_SYNC = True

    def __init__(self, nc, stack):
        self.nc = nc
        self.stack = stack
        self.eng = {"pe": nc.tensor, "act": nc.scalar, "dve": nc.vector, "pool": nc.gpsimd, "sp": nc.sync}
        self.ops = {k: [] for k in self.eng}
        self.sems = {}
        self.cnt = {}
        for k in ("pe", "act", "dve", "pool"):
            self.sems[k] = stack.enter_context(nc.semaphore("s_" + k))
            self.cnt[k] = 0
        self.known = {k: {} for k in self.eng}

    def dma_sem(self, name):
        self.sems[name] = self.stack.enter_context(self.nc.semaphore(name))
        self.cnt[name] = 0
        return name

    dry = False

    def op(self, eng, fn, reads=(), writes=(), dma=None, standalone=False):
        if self.dry:
            return None
        deps = {}

        def add(ev):
            if ev is not None and deps.get(ev[0], 0) < ev[1]:
                deps[ev[0]] = ev[1]

        def addall(d):
            for k, v in d.items():
                if deps.get(k, 0) < v:
                    deps[k] = v

        for r in reads:
            add(r.w)
            if r.excl:
                addall(r.r)
        for w in writes:
            add(w.w)
            addall(w.r)
        waits = []
        kn = self.known[eng]
        for k, v in deps.items():
            if k == eng and (eng == "pe" or not self.SAME_ENGINE_SYNC):
                continue
            if kn.get(k, 0) >= v:
                continue
            waits.append((k, v))
            kn[k] = v
        if dma is None:
            self.cnt[eng] += 1
            ev = (eng, self.cnt[eng])
            inc = (eng, 1)
        else:
            self.cnt[dma] += 16
            ev = (dma, self.cnt[dma])
            inc = (dma, 16)
        self.ops[eng].append((fn, waits, inc, standalone))
        for r in reads:
            if r.excl:
                r.w = ev
                r.r = {}
            elif r.r.get(ev[0], 0) < ev[1]:
                r.r[ev[0]] = ev[1]
        for w in writes:
            w.w = ev
            w.r = {}
        return ev

    def final_wait(self, eng, events):
        self.ops[eng].append((None, [e for e in events if e is not None], None, True))

    def emit(self):
        sems = self.sems
        with self.nc.Block() as block:
            for name, deco in (("sp", block.sync), ("act", block.scalar), ("dve", block.vector),
                               ("pool", block.gpsimd), ("pe", block.tensor)):
                ops = self.ops[name]

                def body(e, ops=ops, name=name):
                    for fn, waits, inc, standalone in ops:
                        if fn is None:
                            for k, v in waits:
                                e.wait_ge(sems[k], v)
                            continue
                        attach = None
                        ws = waits
                        if name in ("act", "dve", "pool") and waits and not standalone:
                            attach = waits[-1]
                            ws = waits[:-1]
                        for k, v in ws:
                            e.wait_ge(sems[k], v)
                        ins = fn(e)
                        if attach is not None:
                            ins._wait_ge(sems[attach[0]], attach[1])
                        ins.then_inc(sems[inc[0]], inc[1])

                deco(body)


def _a_tables(rpb):
    p = np.arange(128)
    a, kc = p // 64, p % 64
    q = np.arange(128)
    b, qc = q // 64, q % 64
    cs = np.clip(qc - 8, 0, 48)
    colv = (kc[:, None] >= cs[None, :]) & (kc[:, None] < cs[None, :] + 16)
    dc = np.clip(kc[:, None] - qc[None, :] + 15, 0, 30)
    bias = np.zeros((128, 8, N_UA, 128), np.float32)
    mask = np.zeros((128, 8, N_UA, 128), np.float32)
    for t, (delta, gen) in enumerate(UA_ORDER):
        dr = 2 * delta + a[:, None] - b[None, :]
        rowv = (dr >= -4) & (dr <= 3) if gen else (np.abs(dr) <= 7)
        valid = rowv & colv
        dri = np.clip(dr + 7, 0, 14)
        for h in range(8):
            bias[:, h, t, :] = rpb[h][dri, dc]
        mask[:, :, t, :] = np.where(valid, 0.0, NEG)[:, None, :]
    return bias.reshape(128, -1), mask.reshape(128, -1)


def _t5_bucket_np(rel):
    half, max_exact = 16, 8
    n = np.abs(rel)
    try:
        import jax
        import jax.numpy as jnp
        with jax.default_device(jax.devices("cpu")[0]):
            nn = jnp.asarray(n.astype(np.int32))
            large = max_exact + (jnp.log(jnp.maximum(nn, 1).astype(jnp.float32) / max_exact)
                                 / math.log(128 / max_exact) * (half - max_exact)).astype(jnp.int32)
            large = np.asarray(jnp.minimum(large, half - 1))
    except Exception:
        lg = (np.log(np.maximum(n, 1).astype(np.float32) / np.float32(max_exact)) / np.float32(math.log(128 / max_exact))
              * np.float32(half - max_exact))
        large = np.minimum(max_exact + lg.astype(np.int32), half - 1)
    return np.where(rel > 0, half, 0) + np.where(n < max_exact, n, large)


def _b_tables(t5):
    p = np.arange(128)[:, None]
    q = np.arange(128)[None, :]
    bias = np.zeros((128, 8, 3, 128), np.float32)
    mask = np.zeros((128, 8, 3, 128), np.float32)
    for t, delta in enumerate((-1, 0, 1)):
        rel = 128 * delta + p - q
        bk = _t5_bucket_np(np.clip(rel, -128, 128))
        valid = np.abs(rel) <= 128
        for h in range(8):
            bias[:, h, t, :] = t5[bk, h]
        mask[:, :, t, :] = np.where(valid, 0.0, NEG)[:, None, :]
    return bias.reshape(128, -1), mask.reshape(128, -1)


def a_plan(i, T):
    if T < 6:
        raise ValueError("sequence too short")
    if 2 <= i <= T - 3:
        return list(range(i - 2, i + 3)), [(2, 7)]
    if i == 0:
        return [0, 1, 2, 3], [(4, 6), (7, 9)]
    if i == 1:
        return [0, 1, 2, 3], [(3, 6), (7, 8)]
    if i == T - 2:
        return list(range(T - 4, T)), [(1, 2), (3, 6)]
    return list(range(T - 4, T)), [(0, 2), (3, 5)]


def build_program(seq_tiles, dbg=None):
    NT = sum(seq_tiles)
    nc = bass.Bass("TRN2", target_bir_lowering=False, dynamic_dma_scratch_size=256)

    def dram(name, shape, kind):
        return nc.dram_tensor(name, shape, F32, kind=kind).ap()

    x_d = dram("x", [NT * 128, D_MODEL], "ExternalInput")
    y_d = dram("y", [NT * 128, D_MODEL], "ExternalOutput")
    win_d = dram("w_in", [D_MODEL, D_IN], "ExternalInput")
    woa_d = dram("w_oa", [512, D_MODEL], "ExternalInput")
    wob_d = dram("w_ob", [512, D_MODEL], "ExternalInput")
    wout_d = dram("w_out", [D_MODEL, D_MODEL], "ExternalInput")
    small_d = dram("small", [128, 160], "ExternalInput")
    uab_d = dram("ua_bias", [128, 8 * N_UA * 128], "ExternalInput")
    uam_d = dram("ua_mask", [128, 8 * N_UA * 128], "ExternalInput")
    tbb_d = dram("tb_bias", [128, 8 * 3 * 128], "ExternalInput")
    tbm_d = dram("tb_mask", [128, 8 * 3 * 128], "ExternalInput")

    with ExitStack() as st:
        S = Sched(nc, st)

        def sb(name, shape, dt):
            return st.enter_context(nc.sbuf_tensor("sb_" + name, shape, dt))

        win = sb("win", [128, 8, D_IN], BF16); win_r = [Res(f"win{j}") for j in range(6)]
        woa = sb("woa", [128, 4, D_MODEL], BF16); woa_r = Res("woa")
        wob = sb("wob", [128, 4, D_MODEL], BF16); wob_r = Res("wob")
        wout = sb("wout", [128, 8, D_MODEL], BF16); wout_r = Res("wout")
        UA = sb("UA", [128, 8, N_UA * 128], BF16); ua_r = Res("UA")
        TB = sb("TB", [128, 8, 3 * 128], BF16); tb_r = Res("TB")
        small = sb("small", [128, 160], F32); small_r = Res("small")
        identf = small[:, 0:128]
        gcol = small[:, 128:136]
        cvec = small[:, 136:140]
        sinkv = small[:, 140:148]
        identb = sb("identb", [128, 128], BF16); identb_r = Res("identb")
        misc = sb("misc", [128, 64], F32); misc_r = Res("misc")
        cab = misc[:, 0:2]
        esink = misc[:, 2:10]
        cm05 = misc[:, 16:32]
        xf = [sb(f"xf{i}", [128, D_MODEL], F32) for i in range(1)]
        xf_r = [Res(f"xf{i}") for i in range(1)]
        xf_sem = [S.dma_sem(f"d_xf{i}") for i in range(1)]
        qf = sb("qf", [128, 1024], F32); qf_r = Res("qf")
        xr = sb("xr", [128, D_MODEL], F32); xr_r = Res("xr"); xr_sem = S.dma_sem("d_xr"); st_sem = S.dma_sem("d_st")
        hT = [sb(f"hT{i}", [128, 8, 128], BF16) for i in range(HT_R)]
        hT_r = [(Res(f"hTa{i}"), Res(f"hTb{i}")) for i in range(HT_R)]
        stat = sb("stat", [128, 3 * HT_R], F32)
        ss_r = [Res(f"ss{i}") for i in range(HT_R)]
        rstd_r = [Res(f"rstd{i}") for i in range(HT_R)]
        kTa = [sb(f"kTa{i}", [128, 4, 128], BF16) for i in range(KV_R)]
        va = [sb(f"va{i}", [128, 8, 65], BF16) for i in range(KV_R)]
        kTb = [sb(f"kTb{i}", [128, 2, 128], BF16) for i in range(KV_R)]
        vb = [sb(f"vb{i}", [128, 2, 65], BF16) for i in range(KV_R)]
        kTa_r = [Res(f"kTa{i}") for i in range(KV_R)]
        va_r = [Res(f"va{i}") for i in range(KV_R)]
        kTb_r = [Res(f"kTb{i}") for i in range(KV_R)]
        vb_r = [Res(f"vb{i}") for i in range(KV_R)]
        sq = sb("sq", [128, 1024], F32); sq_r = Res("sq")
        ssq = sb("ssq", [128, 32], F32); ssq_r = (Res("ssq_q"), Res("ssq_k"))
        nrm = sb("nrm", [128, 1024], BF16); nrm_r = Res("nrm")
        nrmk = sb("nrmk", [128, 640], BF16); nrmk_r = Res("nrmk")
        kbf = sb("kbf", [128, 128], F32); kbf_r = Res("kbf")
        kf = sb("kf", [128, 512], F32); kf_r = Res("kf")
        sqk = sb("sqk", [128, 640], BF16); sqk_r = Res("sqk")
        junk = sqk[:].bitcast(mybir.dt.int8)[:, 0:1024]
        qT2 = [sb(f"qT{i}", [128, 8, 128], BF16) for i in range(2)]; qT2_r = [(Res(f"qTa{i}"), Res(f"qTb{i}")) for i in range(2)]
        pT = [sb(f"pT{i}", [128, 640], BF16) for i in range(2)]; pT_r = [Res(f"pT{i}") for i in range(2)]
        onorm = [sb(f"onorm{i}", [128, 1024], F32) for i in range(2)]
        onorm_r = [[Res(f"onormA{i}"), Res(f"onormB{i}")] for i in range(2)]
        rden = sb("rden", [128, 8], F32); rden_r = Res("rden")
        tmp = [sb(f"tmp{i}", [128, 512], F32) for i in range(3)]; tmp_r = [Res(f"tmp{i}") for i in range(3)]
        ubuf = sb("ubuf", [128, 1024], BF16); u_r = Res("u")
        uT = sb("uT", [128, 8, 128], BF16); uT_r = (Res("uTa"), Res("uTb"))
        mbuf, m_r = ubuf, u_r
        mT, mT_r = uT, uT_r
        ps = st.enter_context(nc.psum_tensor("ps", [128, 4096], F32))
        bank_r = [Res(f"bank{b}", excl=True) for b in range(8)]
        B_S, B_O, B_PJ0, B_PJ1, B_T = 0, 4, 5, 6, 7

        def bank(b, lo=0, hi=512):
            return ps[:, b * 512 + lo: b * 512 + hi]

        def bank_bf(b):
            return ps[:, b * 512:(b + 1) * 512].bitcast(BF16)

        pj_ctr = [0]

        def next_pj():
            b = B_PJ0 + (pj_ctr[0] % 2)
            pj_ctr[0] += 1
            return b

        d_small = S.dma_sem("d_small")
        S.op("sp", lambda e: e.dma_start(out=small[:], in_=small_d[:, :]), writes=[small_r], dma=d_small)
        S.op("sp", lambda e: e.dma_start(out=xf[0][:], in_=x_d[0:128, :]), writes=[xf_r[0]], dma=xf_sem[0])
        S.op("dve", lambda e: e.tensor_copy(out=identb[:], in_=identf), reads=[small_r], writes=[identb_r])
        S.op("pool", lambda e: e.memset(misc[:], -0.5), writes=[misc_r])
        S.op("dve", lambda e: e.scalar_tensor_tensor(out=cab[:, 0:1], in0=cvec[:, 0:1], scalar=0.125, in1=cvec[:, 1:2],
                                                      op0=ALU.mult, op1=ALU.mult), reads=[small_r], writes=[misc_r])
        S.op("dve", lambda e: e.scalar_tensor_tensor(out=cab[:, 1:2], in0=cvec[:, 2:3], scalar=0.125, in1=cvec[:, 3:4],
                                                      op0=ALU.mult, op1=ALU.mult), reads=[small_r], writes=[misc_r])
        S.op("act", lambda e: e.activation(out=esink, in_=sinkv, func=AF.Exp), reads=[small_r], writes=[misc_r])
        for i in range(KV_R):
            S.op("pool", lambda e, i=i: e.memset(va[i][:, :, 64:65], 1.0), writes=[va_r[i]])
            S.op("pool", lambda e, i=i: e.memset(vb[i][:, :, 64:65], 1.0), writes=[vb_r[i]])

        stg_all = [(xr, [xr_r], xr_sem), (onorm[0], onorm_r[0], S.dma_sem("d_on0")), (onorm[1], onorm_r[1], S.dma_sem("d_on1")),
                   (qf, [qf_r], S.dma_sem("d_qf")), (sq, [sq_r], S.dma_sem("d_sq"))]
        stg_i = [0]
        cast_i = [0]

        def stage_load(src_ap, n, nslots):
            k = stg_i[0] % nslots
            stg_i[0] += 1
            t, r, sem = stg_all[k]
            S.op("sp", lambda e: e.dma_start(out=t[:, 0:n], in_=src_ap), writes=list(r), dma=sem)
            return t, list(r)

        def cast_scaled(dst_ap, dst_r, src_t, src_r, n, scale, scale_r=()):
            which = ("dve", "pool", "act")[cast_i[0] % 3]
            cast_i[0] += 1
            if which == "act":
                S.op("act", lambda e: e.activation(out=dst_ap, in_=src_t[:, 0:n], func=AF.Copy, scale=scale),
                     reads=[*src_r, *scale_r], writes=[dst_r])
            else:
                S.op(which, lambda e: e.tensor_scalar(out=dst_ap, in0=src_t[:, 0:n], scalar1=scale, scalar2=0.0,
                                                       op0=ALU.mult, op1=ALU.add),
                     reads=[*src_r, *scale_r], writes=[dst_r])

        def stage_win(pieces, nslots):
            for c in range(8):
                for j in pieces:
                    lo = j * 1024
                    n = min(1024, D_IN - lo)
                    t, r = stage_load(win_d[c * 128:(c + 1) * 128, lo:lo + n], n, nslots)
                    cast_scaled(win[:, c, lo:lo + n], win_r[j], t, r, n, gcol[:, c:c + 1], [small_r])
                    yield

        UAf = UA[:].rearrange("p h n -> p (h n)")
        TBf = TB[:].rearrange("p h n -> p (h n)")

        def stage_tables(nslots):
            for (dst, dst_r, bd, md, tot) in ((UAf, ua_r, uab_d, uam_d, 8 * N_UA * 128), (TBf, tb_r, tbb_d, tbm_d, 8 * 3 * 128)):
                for lo in range(0, tot, 1024):
                    n = min(1024, tot - lo)
                    t1_, r1_ = stage_load(bd[:, lo:lo + n], n, nslots)
                    t2_, r2_ = stage_load(md[:, lo:lo + n], n, nslots)
                    which = ("dve", "pool")[(lo // 1024) % 2]
                    S.op(which, lambda e, n=n, t1_=t1_, t2_=t2_: e.tensor_tensor(
                        out=t1_[:, 0:n], in0=t1_[:, 0:n], in1=t2_[:, 0:n], op=ALU.add),
                        reads=[*r2_], writes=[*r1_])
                    S.op("act", lambda e, dst=dst, lo=lo, n=n, t1_=t1_: e.activation(out=dst[:, lo:lo + n], in_=t1_[:, 0:n], func=AF.Exp),
                         reads=[*r1_], writes=[dst_r])
                    yield

        def stage_rest(nslots):
            yield from stage_tables(nslots)
            yield from stage_win([3, 4, 5], nslots)
            for c in range(4):
                t, r = stage_load(woa_d[c * 128:(c + 1) * 128, :], 1024, nslots)
                cast_scaled(woa[:, c, :], woa_r, t, r, 1024, 0.5)
                yield
                t, r = stage_load(wob_d[c * 128:(c + 1) * 128, :], 1024, nslots)
                cast_scaled(wob[:, c, :], wob_r, t, r, 1024, 0.5)
                yield
            for c in range(8):
                t, r = stage_load(wout_d[c * 128:(c + 1) * 128, :], 1024, nslots)
                cast_scaled(wout[:, c, :], wout_r, t, r, 1024, 0.5)
                yield

        tiles = []
        base = 0
        for T in seq_tiles:
            for i in range(T):
                tiles.append((base, i, T))
            base += T

        def load_xf(t):
            if t < NT:
                s2 = 0
                S.op("sp", lambda e: e.dma_start(out=xf[s2][:], in_=x_d[t * 128:(t + 1) * 128, :]), writes=[xf_r[s2]], dma=xf_sem[s2])

        def proj_chunk(t, col, n, b=None):
            s6 = t % HT_R
            if b is None:
                b = next_pj()

            def f(e):
                ins = None
                for c in range(8):
                    ins = e.matmul(bank(b, 0, n), lhsT=hT[s6][:, c, :], rhs=win[:, c, col:col + n], start=(c == 0), stop=(c == 7))
                return ins
            S.op("pe", f, reads=[hT_r[s6][0], hT_r[s6][1]] + [win_r[j] for j in range(col // 1024, (col + n - 1) // 1024 + 1)],
                 writes=[bank_r[b]])
            return b

        def rstd_from(ssq_ap, nheads, r):
            S.op("pool", lambda e: e.tensor_scalar(out=ssq_ap, in0=ssq_ap, scalar1=1.0 / 64, scalar2=EPS, op0=ALU.mult, op1=ALU.add),
                 reads=[r], writes=[r])
            S.op("pool", lambda e: e.tensor_tensor(out=ssq_ap, in0=ssq_ap, in1=cm05[:, 0:nheads], op=ALU.pow),
                 reads=[r, misc_r], writes=[r])

        def transposes_bf(src, src_r, nblk, bT):
            bb = bank_bf(bT)

            def f(e):
                ins = None
                for c in range(nblk):
                    ins = e.transpose(out=bb[:, c * 128:(c + 1) * 128], in_=src[:, c * 128:(c + 1) * 128], identity=identb[:])
                return ins
            S.op("pe", f, reads=[src_r, identb_r], writes=[bank_r[bT]])
            return bb

        def evac_T(dst, dst_r, bT):
            bb = bank_bf(bT)
            S.op("dve", lambda e: e.tensor_copy(out=dst[:].rearrange("p a b -> p (a b)"), in_=bb[:, 0:1024]),
                 reads=[bank_r[bT]], writes=[dst_r[0], dst_r[1]])

        pj_free = [B_PJ0, B_PJ1]
        t_free = [B_T]
        done_f = set()
        done_q = set()

        def acquire(pool):
            while not pool:
                yield
            return pool.pop(0)

        def proj_into(t, col, n, b):
            proj_chunk(t, col, n, b=b)

        def F(t):
            s2, s6, s8 = 0, t % HT_R, t % KV_R
            ssc = stat[:, s6:s6 + 1]
            rstd = stat[:, HT_R + s6:HT_R + s6 + 1]
            rstdh = stat[:, 2 * HT_R + s6:2 * HT_R + s6 + 1]
            S.op("act", lambda e: e.activation(out=junk, in_=xf[s2][:], func=AF.Square, accum_out=ssc),
                 reads=[xf_r[s2]], writes=[ss_r[s6], sqk_r], standalone=True)
            yield
            S.op("pool", lambda e: e.tensor_scalar(out=rstd, in0=ssc, scalar1=1.0 / D_MODEL, scalar2=EPS,
                                                   op0=ALU.mult, op1=ALU.add), reads=[ss_r[s6]], writes=[rstd_r[s6]])
            S.op("pool", lambda e: e.tensor_tensor(out=rstd, in0=rstd, in1=cm05[:, 0:1], op=ALU.pow),
                 reads=[rstd_r[s6], misc_r], writes=[rstd_r[s6]])
            S.op("pool", lambda e: e.tensor_scalar(out=rstdh, in0=rstd, scalar1=0.5, scalar2=0.0,
                                                   op0=ALU.mult, op1=ALU.add), reads=[rstd_r[s6]], writes=[rstd_r[s6]])
            for half in range(2):
                bT = yield from acquire(t_free)

                def f(e, half=half, bT=bT):
                    ins = None
                    for c in range(4):
                        cc = half * 4 + c
                        ins = e.transpose(out=bank(bT, c * 128, (c + 1) * 128), in_=xf[s2][:, cc * 128:(cc + 1) * 128], identity=identf)
                    return ins
                S.op("pe", f, reads=[xf_r[s2], small_r], writes=[bank_r[bT]])
                yield
                dst = hT[s6][:, half * 4:(half + 1) * 4, :].rearrange("p a b -> p (a b)")
                if half == 0:
                    S.op("act", lambda e, dst=dst, bT=bT: e.activation(out=dst, in_=bank(bT), func=AF.Copy),
                         reads=[bank_r[bT]], writes=[hT_r[s6][0]])
                else:
                    S.op("dve", lambda e, dst=dst, bT=bT: e.tensor_copy(out=dst, in_=bank(bT)),
                         reads=[bank_r[bT]], writes=[hT_r[s6][1]])
                t_free.append(bT)
            load_xf(t + 1)
            b2 = yield from acquire(pj_free)
            proj_into(t, C_VA, 512, b2)
            yield
            S.op("act", lambda e: e.activation(out=va[s8][:, :, 0:64], in_=bank(b2).rearrange("p (h d) -> p h d", d=64),
                                               func=AF.Copy, scale=rstd),
                 reads=[bank_r[b2], rstd_r[s6]], writes=[va_r[s8]])
            pj_free.append(b2)
            b3 = yield from acquire(pj_free)
            proj_into(t, C_KB, 256, b3)
            yield
            S.op("act", lambda e: e.activation(out=kbf[:], in_=bank(b3, 0, 128), func=AF.Copy, scale=rstd),
                 reads=[bank_r[b3], rstd_r[s6]], writes=[kbf_r])
            S.op("act", lambda e: e.activation(out=vb[s8][:, :, 0:64], in_=bank(b3, 128, 256).rearrange("p (h d) -> p h d", d=64),
                                               func=AF.Copy, scale=rstd),
                 reads=[bank_r[b3], rstd_r[s6]], writes=[vb_r[s8]])
            pj_free.append(b3)
            b = yield from acquire(pj_free)
            proj_into(t, C_KA, 512, b)
            yield
            S.op("act", lambda e: e.activation(out=kf[:], in_=bank(b), func=AF.Copy, scale=rstd),
                 reads=[bank_r[b], rstd_r[s6]], writes=[kf_r])
            pj_free.append(b)
            yield
            S.op("pool", lambda e: e.tensor_tensor(out=sqk[:, 0:512], in0=kf[:], in1=kf[:], op=ALU.mult), reads=[kf_r], writes=[sqk_r])
            S.op("pool", lambda e: e.tensor_tensor(out=sqk[:, 512:640], in0=kbf[:], in1=kbf[:], op=ALU.mult),
                 reads=[kbf_r], writes=[sqk_r])
            yield
            yield
            S.op("dve", lambda e: e.tensor_reduce(out=ssq[:, 16:26], in_=sqk[:, 0:640].rearrange("p (h d) -> p h d", d=64),
                                                  axis=AX.X, op=ALU.add), reads=[sqk_r], writes=[ssq_r[1]])
            yield
            rstd_from(ssq[:, 16:26], 10, ssq_r[1])
            yield
            yield
            S.op("pool", lambda e: e.tensor_tensor(out=nrmk[:, 0:512].rearrange("p (h d) -> p h d", d=64),
                                                   in0=kf[:].rearrange("p (h d) -> p h d", d=64),
                                                   in1=ssq[:, 16:24].unsqueeze(2).broadcast_to([128, 8, 64]), op=ALU.mult),
                 reads=[kf_r, ssq_r[1]], writes=[nrmk_r])
            S.op("pool", lambda e: e.tensor_tensor(out=nrmk[:, 512:640].rearrange("p (h d) -> p h d", d=64),
                                                   in0=kbf[:].rearrange("p (h d) -> p h d", d=64),
                                                   in1=ssq[:, 24:26].unsqueeze(2).broadcast_to([128, 2, 64]), op=ALU.mult),
                 reads=[kbf_r, ssq_r[1]], writes=[nrmk_r])
            yield
            yield
            bT = yield from acquire(t_free)
            bb = bank_bf(bT)
            transposes_bf(nrmk, nrmk_r, 5, bT)
            yield
            S.op("act", lambda e: e.activation(out=kTa[s8][:].rearrange("p a b -> p (a b)"), in_=bb[:, 0:512], func=AF.Copy,
                                               scale=cab[:, 0:1]),
                 reads=[bank_r[bT], misc_r], writes=[kTa_r[s8]])
            for kv in range(2):
                for dh in range(2):
                    src = bb[kv * 64:(kv + 1) * 64, 512:640]
                    dst = kTb[s8][dh * 64:(dh + 1) * 64, kv, :]
                    sc = cab[kv * 64:(kv + 1) * 64, 1:2]
                    if dh == 0:
                        S.op("act", lambda e, src=src, dst=dst, sc=sc: e.activation(out=dst, in_=src, func=AF.Copy, scale=sc),
                             reads=[bank_r[bT], misc_r], writes=[kTb_r[s8]])
                    else:
                        S.op("dve", lambda e, src=src, dst=dst, sc=sc: e.tensor_scalar(out=dst, in0=src, scalar1=sc, scalar2=None,
                                                                                       op0=ALU.mult),
                             reads=[bank_r[bT], misc_r], writes=[kTb_r[s8]])
            t_free.append(bT)
            done_f.add(t)
            yield

        def att_jobs(t, which):
            base, i, T = tiles[t]
            on, on_r = onorm[t % 2], onorm_r[t % 2]
            qT, qT_r = qT2[t % 2], qT2_r[t % 2]
            if which == "A":
                J, pieces = a_plan(i, T)
                tab, tab_r = UA, ua_r
            else:
                J = [j for j in (i - 1, i, i + 1) if 0 <= j < T]
                t0 = J[0] - i + 1
                pieces = [(t0, t0 + len(J))]
                tab, tab_r = TB, tb_r
            n = len(J)
            slots = [(base + j) % KV_R for j in J]
            br = 0 if which == "A" else 1

            def sinfo(k):
                sbanks = [B_S + 2 * k] + ([B_S + 2 * k + 1] if n > 4 else [])
                return sbanks, (B_S + 2 * k) * 512

            def qk(h, k):
                sbanks, soff = sinfo(k)
                half = h % 2
                qc = h // 2 if which == "A" else 4 + h // 2
                kr = [(kTa_r if which == "A" else kTb_r)[s] for s in slots]

                def fqk(e):
                    ins = None
                    for blk, s in enumerate(slots):
                        if which == "A":
                            lhs = kTa[s][half * 64:(half + 1) * 64, qc, :]
                        else:
                            lhs = kTb[s][half * 64:(half + 1) * 64, h // 4, :]
                        ins = e.matmul(ps[:, soff + blk * 128: soff + (blk + 1) * 128], lhsT=lhs,
                                       rhs=qT[half * 64:(half + 1) * 64, qc, :], start=True, stop=True)
                    return ins
                S.op("pe", fqk, reads=[*kr, qT_r[0], qT_r[1]], writes=[bank_r[b] for b in sbanks])

            def softmax(h, k):
                sbanks, soff = sinfo(k)
                S.op("act", lambda e: e.activation(out=pT[k][:, 0:n * 128], in_=ps[:, soff: soff + n * 128], func=AF.Exp),
                     reads=[bank_r[b] for b in sbanks], writes=[pT_r[k]])
                col = 0
                for (a0, a1) in pieces:
                    w = (a1 - a0) * 128
                    S.op("dve", lambda e, col=col, w=w, a0=a0, a1=a1: e.tensor_tensor(
                        out=pT[k][:, col:col + w], in0=pT[k][:, col:col + w],
                        in1=tab[:, h, a0 * 128:a1 * 128], op=ALU.mult),
                        reads=[pT_r[k], tab_r], writes=[pT_r[k]])
                    col += w

            def pv(h, k):
                vr = [(va_r if which == "A" else vb_r)[s] for s in slots]
                hh = h % 4

                def fpv(e):
                    ins = None
                    for blk, s in enumerate(slots):
                        rhs = va[s][:, h, :] if which == "A" else vb[s][:, h // 4, :]
                        ins = e.matmul(bank(B_O, hh * 65, (hh + 1) * 65), lhsT=pT[k][:, blk * 128:(blk + 1) * 128], rhs=rhs,
                                       start=(blk == 0), stop=(blk == n - 1))
                    return ins
                S.op("pe", fpv, reads=[pT_r[k], *vr], writes=[bank_r[B_O]])
                if hh == 3:
                    g = h // 4
                    ov = bank(B_O, 0, 260).rearrange("p (h d) -> p h d", d=65)
                    rd = rden[:, g * 4:(g + 1) * 4]
                    if which == "A":
                        S.op("dve", lambda e: e.reciprocal(out=rd.unsqueeze(2), in_=ov[:, :, 64:65]),
                             reads=[bank_r[B_O]], writes=[rden_r])
                    else:
                        S.op("dve", lambda e: e.tensor_tensor(out=rd.unsqueeze(2), in0=ov[:, :, 64:65],
                                                              in1=esink[:, g * 4:(g + 1) * 4].unsqueeze(2), op=ALU.add),
                             reads=[bank_r[B_O], misc_r], writes=[rden_r])
                        S.op("dve", lambda e: e.reciprocal(out=rd, in_=rd), reads=[rden_r], writes=[rden_r])
                    S.op("dve", lambda e: e.tensor_tensor(
                        out=on[:, br * 512 + g * 256: br * 512 + (g + 1) * 256].rearrange("p (h d) -> p h d", d=64),
                        in0=ov[:, :, 0:64], in1=rd.unsqueeze(2).broadcast_to([128, 4, 64]), op=ALU.mult),
                        reads=[bank_r[B_O], rden_r], writes=[on_r[br]])

            return [(lambda k, h=h: qk(h, k), lambda k, h=h: softmax(h, k), lambda k, h=h: pv(h, k)) for h in range(8)]

        def Gq(t):
            s6 = t % HT_R
            rstd = stat[:, HT_R + s6:HT_R + s6 + 1]
            for (col, lo) in ((C_QA, 0), (C_QB, 512)):
                b = yield from acquire(pj_free)
                proj_into(t, col, 512, b)
                yield
                S.op("act", lambda e, b=b, lo=lo: e.activation(out=qf[:, lo:lo + 512], in_=bank(b), func=AF.Copy, scale=rstd),
                     reads=[bank_r[b], rstd_r[s6]], writes=[qf_r])
                pj_free.append(b)
            yield
            S.op("pool", lambda e: e.tensor_tensor(out=sq[:], in0=qf[:], in1=qf[:], op=ALU.mult), reads=[qf_r], writes=[sq_r])
            yield
            yield
            S.op("dve", lambda e: e.tensor_reduce(out=ssq[:, 0:16], in_=sq[:].rearrange("p (h d) -> p h d", d=64),
                                                  axis=AX.X, op=ALU.add), reads=[sq_r], writes=[ssq_r[0]])
            yield
            rstd_from(ssq[:, 0:16], 16, ssq_r[0])
            yield
            yield
            S.op("pool", lambda e: e.tensor_tensor(out=nrm[:].rearrange("p (h d) -> p h d", d=64),
                                                   in0=qf[:].rearrange("p (h d) -> p h d", d=64),
                                                   in1=ssq[:, 0:16].unsqueeze(2).broadcast_to([128, 16, 64]), op=ALU.mult),
                 reads=[qf_r, ssq_r[0]], writes=[nrm_r])
            yield
            yield
            yield
            bT = yield from acquire(t_free)
            transposes_bf(nrm, nrm_r, 8, bT)
            yield
            evac_T(qT2[t % 2], qT2_r[t % 2], bT)
            t_free.append(bT)
            done_q.add(t)
            yield

        def at_prologue(t):
            jobs = att_jobs(t, "A")
            jobs[0][0](0)
            jobs[0][1](0)
            jobs[1][0](1)

        def At(t):
            jobs = att_jobs(t, "A") + att_jobs(t, "B")
            nxt = None
            for j in range(16):
                if j + 2 >= 16 and t + 1 < NT and nxt is None:
                    while not ((t + 1) in done_q and (min(t + LEAD, NT - 1)) in done_f):
                        yield
                    nxt = att_jobs(t + 1, "A")[0:2]
                    jobs = jobs + nxt
                if j + 1 < len(jobs):
                    jobs[j + 1][1]((j + 1) % 2)
                jobs[j][2](j % 2)
                if j + 2 < len(jobs):
                    jobs[j + 2][0](j % 2)
                yield

        def Gb(t):
            s6 = t % HT_R
            rstd = stat[:, HT_R + s6:HT_R + s6 + 1]
            rstdh = stat[:, 2 * HT_R + s6:2 * HT_R + s6 + 1]
            on, on_r = onorm[t % 2], onorm_r[t % 2]
            S.op("sp", lambda e: e.dma_start(out=xr[:], in_=x_d[t * 128:(t + 1) * 128, :]), writes=[xr_r], dma=xr_sem)
            for br, col in ((0, C_ZA), (1, C_ZB)):
                b = yield from acquire(pj_free)
                proj_into(t, col, 512, b)
                yield
                th, th_r = tmp[0], tmp_r[0]
                t1, t1_r = tmp[1], tmp_r[1]
                S.op("act", lambda e, b=b, th=th: e.activation(out=th[:], in_=bank(b), func=AF.Tanh, scale=rstdh),
                     reads=[bank_r[b], rstd_r[s6]], writes=[th_r])
                S.op("dve", lambda e, b=b, t1=t1, br=br: e.scalar_tensor_tensor(out=t1[:], in0=bank(b), scalar=rstd,
                                                                                in1=on[:, br * 512:(br + 1) * 512],
                                                                                op0=ALU.mult, op1=ALU.mult),
                     reads=[bank_r[b], rstd_r[s6], on_r[br]], writes=[t1_r])
                pj_free.append(b)
                yield
                S.op("dve", lambda e, th=th, t1=t1, br=br: e.scalar_tensor_tensor(out=ubuf[:, br * 512:(br + 1) * 512], in0=th[:], scalar=1.0,
                                                                                  in1=t1[:], op0=ALU.add, op1=ALU.mult),
                     reads=[th_r, t1_r], writes=[u_r])
                yield
            yield
            bT = yield from acquire(t_free)
            transposes_bf(ubuf, u_r, 8, bT)
            yield
            evac_T(uT, uT_r, bT)
            t_free.append(bT)
            yield
            for nh in range(2):
                for br, gcol_, wo, wo_r in ((0, C_GA, woa, woa_r), (1, C_GB, wob, wob_r)):
                    b = yield from acquire(pj_free)
                    proj_into(t, gcol_ + nh * 512, 512, b)
                    yield
                    th, th_r = tmp[0], tmp_r[0]
                    S.op("act", lambda e, b=b, th=th: e.activation(out=th[:], in_=bank(b), func=AF.Tanh, scale=rstdh),
                         reads=[bank_r[b], rstd_r[s6]], writes=[th_r])
                    pj_free.append(b)
                    b2 = yield from acquire(pj_free)

                    def fo(e, br=br, wo=wo, b2=b2, nh=nh):
                        ins = None
                        for c in range(4):
                            ins = e.matmul(bank(b2), lhsT=uT[:, br * 4 + c, :], rhs=wo[:, c, nh * 512:(nh + 1) * 512],
                                           start=(c == 0), stop=(c == 3))
                        return ins
                    S.op("pe", fo, reads=[uT_r[br], wo_r], writes=[bank_r[b2]])
                    yield
                    m1, m1_r = tmp[1 + br], tmp_r[1 + br]
                    S.op("dve", lambda e, th=th, m1=m1, b2=b2: e.scalar_tensor_tensor(out=m1[:], in0=th[:], scalar=1.0, in1=bank(b2),
                                                                                      op0=ALU.add, op1=ALU.mult),
                         reads=[th_r, bank_r[b2]], writes=[m1_r])
                    pj_free.append(b2)
                yield
                S.op("pool", lambda e, nh=nh: e.tensor_tensor(out=mbuf[:, nh * 512:(nh + 1) * 512], in0=tmp[1][:], in1=tmp[2][:], op=ALU.add),
                     reads=[tmp_r[1], tmp_r[2]], writes=[m_r])
                yield
            yield
            yield
            bT = yield from acquire(t_free)
            transposes_bf(mbuf, m_r, 8, bT)
            yield
            evac_T(mT, mT_r, bT)
            t_free.append(bT)
            yield
            for nh in range(2):
                b = yield from acquire(pj_free)

                def fw(e, b=b, nh=nh):
                    ins = None
                    for c in range(8):
                        ins = e.matmul(bank(b), lhsT=mT[:, c, :], rhs=wout[:, c, nh * 512:(nh + 1) * 512], start=(c == 0), stop=(c == 7))
                    return ins
                S.op("pe", fw, reads=[mT_r[0], mT_r[1], wout_r], writes=[bank_r[b]])
                yield
                S.op("dve", lambda e, b=b, nh=nh: e.tensor_tensor(out=xr[:, nh * 512:(nh + 1) * 512], in0=bank(b),
                                                                  in1=xr[:, nh * 512:(nh + 1) * 512], op=ALU.add),
                     reads=[bank_r[b], xr_r], writes=[xr_r])
                pj_free.append(b)
            last_store[0] = S.op("sp", lambda e: e.dma_start(out=y_d[t * 128:(t + 1) * 128, :], in_=xr[:]), reads=[xr_r], dma=st_sem)

        def dump(t):
            s6, s8 = t % HT_R, t % KV_R
            items = [("stat", stat[:], F32, [128, 3 * HT_R], []), ("hT", hT[s6][:].rearrange("p a b -> p (a b)"), BF16, [128, 1024], hT_r[s6]),
                     ("qT", qT2[t % 2][:].rearrange("p a b -> p (a b)"), BF16, [128, 1024], qT2_r[t % 2]),
                     ("kTa", kTa[s8][:].rearrange("p a b -> p (a b)"), BF16, [128, 512], [kTa_r[s8]]),
                     ("va", va[s8][:].rearrange("p a b -> p (a b)"), BF16, [128, 520], [va_r[s8]]),
                     ("kTb", kTb[s8][:].rearrange("p a b -> p (a b)"), BF16, [128, 256], [kTb_r[s8]]),
                     ("vb", vb[s8][:].rearrange("p a b -> p (a b)"), BF16, [128, 130], [vb_r[s8]]),
                     ("onorm", onorm[t % 2][:], F32, [128, 1024], onorm_r[t % 2]), ("ubuf", ubuf[:], BF16, [128, 1024], [u_r]),
                     ("mbuf", mbuf[:], BF16, [128, 1024], [m_r])]
            evs = []
            dsem = S.dma_sem("d_dbg")
            for name, ap, dt, shape, rs in items:
                d = nc.dram_tensor("dbg_" + name, shape, dt, kind="ExternalOutput").ap()
                evs.append(S.op("sp", lambda e, d=d, ap=ap: e.dma_start(out=d[:, :], in_=ap), reads=list(rs), dma=dsem))
            return evs[-1]

        last_store = [None]

        def count_steps(mk):
            S.dry = True
            saved = (pj_ctr[0], list(pj_free), list(t_free), set(done_f), set(done_q))
            n = sum(1 for _ in mk())
            pj_ctr[0] = saved[0]
            pj_free[:] = saved[1]
            t_free[:] = saved[2]
            done_f.clear(); done_f.update(saved[3])
            done_q.clear(); done_q.update(saved[4])
            S.dry = False
            return n + 1

        def run_all(makers):
            makers = [m for m in makers if m is not None]
            order = []
            for pri, (mk, span, dry_ok) in enumerate(makers):
                n = count_steps(mk) if dry_ok is True else (17 if dry_ok is False else dry_ok)
                for j in range(n):
                    order.append(((j + 0.5) / n * span, pri))
            order.sort()
            gens = [mk() for mk, _, _ in makers]
            alive = [True] * len(gens)
            trace_seq = []
            for _, pri in order:
                if alive[pri]:
                    trace_seq.append(pri)
                    try:
                        next(gens[pri])
                    except StopIteration:
                        alive[pri] = False
            guard = 0
            while any(alive):
                for pri in range(len(gens)):
                    if alive[pri]:
                        trace_seq.append(10 + pri)
                        try:
                            next(gens[pri])
                        except StopIteration:
                            alive[pri] = False
                guard += 1
                assert guard < 10000, "stream scheduling deadlock"
            if len(makers) == 4 and DEBUG_SEQ:
                print("SEQ", "".join("ABQF"[p] if p < 10 else "abqf"[p - 10] for p in trace_seq))

        LEAD = 4
        for _ in stage_win([0, 1, 2], 5):
            pass
        rest = stage_rest(3)
        stg_i[0] = 0

        def take(n):
            for _ in range(n):
                try:
                    next(rest)
                except StopIteration:
                    return
                yield

        for t in range(min(LEAD, NT)):
            run_all([(lambda t=t: F(t), 1.0, True), (lambda: take(14), 1.0, 15)])
        run_all([(lambda: Gq(0), 1.0, True), (lambda: take(14), 1.0, 15)])
        for _ in rest:
            pass
        at_prologue(0)
        dbg_ev = None
        for t in range(NT + 1):
            run_all([((lambda t=t: At(t)), 0.88, False) if t < NT else None,
                     ((lambda t=t: Gb(t - 1)), 1.0, True) if t >= 1 else None,
                     ((lambda t=t: Gq(t + 1)), 0.68, True) if t + 1 < NT else None,
                     ((lambda t=t: F(t + LEAD)), 0.68, True) if t + LEAD < NT else None])
            if dbg is not None and t - 1 == dbg:
                dbg_ev = dump(dbg)
        S.final_wait("sp", [last_store[0], dbg_ev])
        S.emit()
    return nc


def _shared_inputs(norm_g, w_in, qn_a, kn_a, rpb_a, qn_b, kn_b, sink_b, w_o_a, w_o_b, w_out, t5_table):
    f = lambda a: np.ascontiguousarray(np.asarray(a, dtype=np.float32))
    small = np.zeros((128, 160), np.float32)
    small[:, 0:128] = np.eye(128, dtype=np.float32)
    small[:, 128:136] = f(norm_g)[0].reshape(8, 128).T
    small[:, 136] = np.tile(f(qn_a)[0], 2)
    small[:, 137] = np.tile(f(kn_a)[0], 2)
    small[:, 138] = np.tile(f(qn_b)[0], 2)
    small[:, 139] = np.tile(f(kn_b)[0], 2)
    small[:, 140:148] = f(sink_b)[0][None, :]
    uab, uam = _a_tables(f(rpb_a)[0])
    tbb, tbm = _b_tables(f(t5_table))
    return {"w_in": f(w_in)[0], "w_oa": f(w_o_a)[0], "w_ob": f(w_o_b)[0], "w_out": f(w_out)[0], "small": small,
            "ua_bias": uab, "ua_mask": uam, "tb_bias": tbb, "tb_mask": tbm}


_PROG_CACHE = {}


def run_layer(seqs_per_core, shared, n_cores, dbg=None):
    seq_tiles = tuple(s.shape[0] // 128 for s in seqs_per_core[0])
    if dbg is not None:
        nc = build_program(list(seq_tiles), dbg=dbg)
    else:
        if seq_tiles not in _PROG_CACHE:
            _PROG_CACHE[seq_tiles] = build_program(list(seq_tiles))
        nc = _PROG_CACHE[seq_tiles]
    in_maps = []
    for c in range(n_cores):
        m = dict(shared)
        m["x"] = np.ascontiguousarray(np.concatenate(seqs_per_core[c], axis=0), dtype=np.float32)
        in_maps.append(m)
    res = run_bass_kernel_spmd(nc, in_maps, core_ids=list(range(n_cores)))
    if dbg is not None:
        return res.results
    return [r["y"] for r in res.results]


def kernel(x_prompt, x_sample, norm_g, w_in, qn_a, kn_a, rpb_a, qn_b, kn_b, sink_b, w_o_a, w_o_b, w_out, t5_table):
    x_prompt = np.asarray(x_prompt, dtype=np.float32)
    x_sample = np.asarray(x_sample, dtype=np.float32)
    shared = _shared_inputs(norm_g, w_in, qn_a, kn_a, rpb_a, qn_b, kn_b, sink_b, w_o_a, w_o_b, w_out, t5_table)
    n = 8
    seqs = [[x_prompt[c], x_sample[c]] for c in range(n)]
    ys = run_layer(seqs, shared, n)
    Lp = x_prompt.shape[1]
    y_prompt = np.stack([ys[c][:Lp] for c in range(n)], axis=0)
    y_sample = np.stack([ys[c][Lp:] for c in range(n)], axis=0)
    return (y_prompt.astype(np.float32), y_sample.astype(np.float32))
```

```python
import math
from contextlib import ExitStack

import numpy as np
import concourse.bass as bass
import concourse.mybir as mybir
from concourse.bass_utils import run_bass_kernel_spmd

F32 = mybir.dt.float32
BF16 = mybir.dt.bfloat16
AF = mybir.ActivationFunctionType
ALU = mybir.AluOpType
AX = mybir.AxisListType

D_MODEL = 1024
D_IN = 5376
EPS = 1e-6
NEG = -30000.0
C_QA, C_KA, C_VA, C_ZA, C_QB, C_KB, C_VB, C_ZB, C_GA, C_GB = 0, 512, 1024, 1536, 2048, 2560, 2688, 2816, 3328, 4352
DEBUG_SEQ = False
KV_R = 8
HT_R = 6
N_UA = 9
UA_ORDER = [(-3, False), (-2, False), (-2, True), (-1, False), (0, False), (1, False), (2, True), (2, False), (3, False)]


class Res:
    __slots__ = ("name", "w", "r", "excl")

    def __init__(self, name, excl=False):
        self.name = name
        self.w = None
        self.r = {}
        self.excl = excl


class Sched:
    SAME_ENGINE_SYNC = True

    def __init__(self, nc, stack):
        self.nc = nc
        self.stack = stack
        self.eng = {"pe": nc.tensor, "act": nc.scalar, "dve": nc.vector, "pool": nc.gpsimd, "sp": nc.sync}
        self.ops = {k: [] for k in self.eng}
        self.sems = {}
        self.cnt = {}
        for k in ("pe", "act", "dve", "pool"):
            self.sems[k] = stack.enter_context(nc.semaphore("s_" + k))
            self.cnt[k] = 0
        self.known = {k: {} for k in self.eng}

    def dma_sem(self, name):
        self.sems[name] = self.stack.enter_context(self.nc.semaphore(name))
        self.cnt[name] = 0
        return name

    dry = False

    def op(self, eng, fn, reads=(), writes=(), dma=None, standalone=False):
        if self.dry:
            return None
        deps = {}

        def add(ev):
            if ev is not None and deps.get(ev[0], 0) < ev[1]:
                deps[ev[0]] = ev[1]

        def addall(d):
            for k, v in d.items():
                if deps.get(k, 0) < v:
                    deps[k] = v

        for r in reads:
            add(r.w)
            if r.excl:
                addall(r.r)
        for w in writes:
            add(w.w)
            addall(w.r)
        waits = []
        kn = self.known[eng]
        for k, v in deps.items():
            if k == eng and (eng == "pe" or not self.SAME_ENGINE_SYNC):
                continue
            if kn.get(k, 0) >= v:
                continue
            waits.append((k, v))
            kn[k] = v
        if dma is None:
            self.cnt[eng] += 1
            ev = (eng, self.cnt[eng])
            inc = (eng, 1)
        else:
            self.cnt[dma] += 16
            ev = (dma, self.cnt[dma])
            inc = (dma, 16)
        self.ops[eng].append((fn, waits, inc, standalone))
        for r in reads:
            if r.excl:
                r.w = ev
                r.r = {}
            elif r.r.get(ev[0], 0) < ev[1]:
                r.r[ev[0]] = ev[1]
        for w in writes:
            w.w = ev
            w.r = {}
        return ev

    def final_wait(self, eng, events):
        self.ops[eng].append((None, [e for e in events if e is not None], None, True))

    def emit(self):
        sems = self.sems
        with self.nc.Block() as block:
            for name, deco in (("sp", block.sync), ("act", block.scalar), ("dve", block.vector),
                               ("pool", block.gpsimd), ("pe", block.tensor)):
                ops = self.ops[name]

                def body(e, ops=ops, name=name):
                    for fn, waits, inc, standalone in ops:
                        if fn is None:
                            for k, v in waits:
                                e.wait_ge(sems[k], v)
                            continue
                        attach = None
                        ws = waits
                        if name in ("act", "dve", "pool") and waits and not standalone:
                            attach = waits[-1]
                            ws = waits[:-1]
                        for k, v in ws:
                            e.wait_ge(sems[k], v)
                        ins = fn(e)
                        if attach is not None:
                            ins._wait_ge(sems[attach[0]], attach[1])
                        ins.then_inc(sems[inc[0]], inc[1])

                deco(body)


def _a_tables(rpb):
    p = np.arange(128)
    a, kc = p // 64, p % 64
    q = np.arange(128)
    b, qc = q // 64, q % 64
    cs = np.clip(qc - 8, 0, 48)
    colv = (kc[:, None] >= cs[None, :]) & (kc[:, None] < cs[None, :] + 16)
    dc = np.clip(kc[:, None] - qc[None, :] + 15, 0, 30)
    bias = np.zeros((128, 8, N_UA, 128), np.float32)
    mask = np.zeros((128, 8, N_UA, 128), np.float32)
    for t, (delta, gen) in enumerate(UA_ORDER):
        dr = 2 * delta + a[:, None] - b[None, :]
        rowv = (dr >= -4) & (dr <= 3) if gen else (np.abs(dr) <= 7)
        valid = rowv & colv
        dri = np.clip(dr + 7, 0, 14)
        for h in range(8):
            bias[:, h, t, :] = rpb[h][dri, dc]
        mask[:, :, t, :] = np.where(valid, 0.0, NEG)[:, None, :]
    return bias.reshape(128, -1), mask.reshape(128, -1)


def _t5_bucket_np(rel):
    half, max_exact = 16, 8
    n = np.abs(rel)
    try:
        import jax
        import jax.numpy as jnp
        with jax.default_device(jax.devices("cpu")[0]):
            nn = jnp.asarray(n.astype(np.int32))
            large = max_exact + (jnp.log(jnp.maximum(nn, 1).astype(jnp.float32) / max_exact)
                                 / math.log(128 / max_exact) * (half - max_exact)).astype(jnp.int32)
            large = np.asarray(jnp.minimum(large, half - 1))
    except Exception:
        lg = (np.log(np.maximum(n, 1).astype(np.float32) / np.float32(max_exact)) / np.float32(math.log(128 / max_exact))
              * np.float32(half - max_exact))
        large = np.minimum(max_exact + lg.astype(np.int32), half - 1)
    return np.where(rel > 0, half, 0) + np.where(n < max_exact, n, large)


def _b_tables(t5):
    p = np.arange(128)[:, None]
    q = np.arange(128)[None, :]
    bias = np.zeros((128, 8, 3, 128), np.float32)
    mask = np.zeros((128, 8, 3, 128), np.float32)
    for t, delta in enumerate((-1, 0, 1)):
        rel = 128 * delta + p - q
        bk = _t5_bucket_np(np.clip(rel, -128, 128))
        valid = np.abs(rel) <= 128
        for h in range(8):
            bias[:, h, t, :] = t5[bk, h]
        mask[:, :, t, :] = np.where(valid, 0.0, NEG)[:, None, :]
    return bias.reshape(128, -1), mask.reshape(128, -1)


def a_plan(i, T):
    if T < 6:
        raise ValueError("sequence too short")
    if 2 <= i <= T - 3:
        return list(range(i - 2, i + 3)), [(2, 7)]
    if i == 0:
        return [0, 1, 2, 3], [(4, 6), (7, 9)]
    if i == 1:
        return [0, 1, 2, 3], [(3, 6), (7, 8)]
    if i == T - 2:
        return list(range(T - 4, T)), [(1, 2), (3, 6)]
    return list(range(T - 4, T)), [(0, 2), (3, 5)]


def build_program(seq_tiles, dbg=None):
    NT = sum(seq_tiles)
    nc = bass.Bass("TRN2", target_bir_lowering=False, dynamic_dma_scratch_size=256)

    def dram(name, shape, kind):
        return nc.dram_tensor(name, shape, F32, kind=kind).ap()

    x_d = dram("x", [NT * 128, D_MODEL], "ExternalInput")
    y_d = dram("y", [NT * 128, D_MODEL], "ExternalOutput")
    win_d = dram("w_in", [D_MODEL, D_IN], "ExternalInput")
    woa_d = dram("w_oa", [512, D_MODEL], "ExternalInput")
    wob_d = dram("w_ob", [512, D_MODEL], "ExternalInput")
    wout_d = dram("w_out", [D_MODEL, D_MODEL], "ExternalInput")
    small_d = dram("small", [128, 160], "ExternalInput")
    uab_d = dram("ua_bias", [128, 8 * N_UA * 128], "ExternalInput")
    uam_d = dram("ua_mask", [128, 8 * N_UA * 128], "ExternalInput")
    tbb_d = dram("tb_bias", [128, 8 * 3 * 128], "ExternalInput")
    tbm_d = dram("tb_mask", [128, 8 * 3 * 128], "ExternalInput")

    with ExitStack() as st:
        S = Sched(nc, st)

        def sb(name, shape, dt):
            return st.enter_context(nc.sbuf_tensor("sb_" + name, shape, dt))

        win = sb("win", [128, 8, D_IN], BF16); win_r = [Res(f"win{j}") for j in range(6)]
        woa = sb("woa", [128, 4, D_MODEL], BF16); woa_r = Res("woa")
        wob = sb("wob", [128, 4, D_MODEL], BF16); wob_r = Res("wob")
        wout = sb("wout", [128, 8, D_MODEL], BF16); wout_r = Res("wout")
        UA = sb("UA", [128, 8, N_UA * 128], BF16); ua_r = Res("UA")
        TB = sb("TB", [128, 8, 3 * 128], BF16); tb_r = Res("TB")
        small = sb("small", [128, 160], F32); small_r = Res("small")
        identf = small[:, 0:128]
        gcol = small[:, 128:136]
        cvec = small[:, 136:140]
        sinkv = small[:, 140:148]
        identb = sb("identb", [128, 128], BF16); identb_r = Res("identb")
        misc = sb("misc", [128, 64], F32); misc_r = Res("misc")
        cab = misc[:, 0:2]
        esink = misc[:, 2:10]
        cm05 = misc[:, 16:32]
        xf = [sb(f"xf{i}", [128, D_MODEL], F32) for i in range(1)]
        xf_r = [Res(f"xf{i}") for i in range(1)]
        xf_sem = [S.dma_sem(f"d_xf{i}") for i in range(1)]
        qf = sb("qf", [128, 1024], F32); qf_r = Res("qf")
        xr = sb("xr", [128, D_MODEL], F32); xr_r = Res("xr"); xr_sem = S.dma_sem("d_xr"); st_sem = S.dma_sem("d_st")
        hT = [sb(f"hT{i}", [128, 8, 128], BF16) for i in range(HT_R)]
        hT_r = [(Res(f"hTa{i}"), Res(f"hTb{i}")) for i in range(HT_R)]
        stat = sb("stat", [128, 3 * HT_R], F32)
        ss_r = [Res(f"ss{i}") for i in range(HT_R)]
        rstd_r = [Res(f"rstd{i}") for i in range(HT_R)]
        kTa = [sb(f"kTa{i}", [128, 4, 128], BF16) for i in range(KV_R)]
        va = [sb(f"va{i}", [128, 8, 65], BF16) for i in range(KV_R)]
        kTb = [sb(f"kTb{i}", [128, 2, 128], BF16) for i in range(KV_R)]
        vb = [sb(f"vb{i}", [128, 2, 65], BF16) for i in range(KV_R)]
        kTa_r = [Res(f"kTa{i}") for i in range(KV_R)]
        va_r = [Res(f"va{i}") for i in range(KV_R)]
        kTb_r = [Res(f"kTb{i}") for i in range(KV_R)]
        vb_r = [Res(f"vb{i}") for i in range(KV_R)]
        sq = sb("sq", [128, 1024], F32); sq_r = Res("sq")
        ssq = sb("ssq", [128, 32], F32); ssq_r = (Res("ssq_q"), Res("ssq_k"))
        nrm = sb("nrm", [128, 1024], BF16); nrm_r = Res("nrm")
        nrmk = sb("nrmk", [128, 640], BF16); nrmk_r = Res("nrmk")
        kbf = sb("kbf", [128, 128], F32); kbf_r = Res("kbf")
        kf = sb("kf", [128, 512], F32); kf_r = Res("kf")
        sqk = sb("sqk", [128, 640], BF16); sqk_r = Res("sqk")
        junk = sqk[:].bitcast(mybir.dt.int8)[:, 0:1024]
        qT2 = [sb(f"qT{i}", [128, 8, 128], BF16) for i in range(2)]; qT2_r = [(Res(f"qTa{i}"), Res(f"qTb{i}")) for i in range(2)]
        pT = [sb(f"pT{i}", [128, 640], BF16) for i in range(2)]; pT_r = [Res(f"pT{i}") for i in range(2)]
        onorm = [sb(f"onorm{i}", [128, 1024], F32) for i in range(2)]
        onorm_r = [[Res(f"onormA{i}"), Res(f"onormB{i}")] for i in range(2)]
        rden = sb("rden", [128, 8], F32); rden_r = Res("rden")
        tmp = [sb(f"tmp{i}", [128, 512], F32) for i in range(3)]; tmp_r = [Res(f"tmp{i}") for i in range(3)]
        ubuf = sb("ubuf", [128, 1024], BF16); u_r = Res("u")
        uT = sb("uT", [128, 8, 128], BF16); uT_r = (Res("uTa"), Res("uTb"))
        mbuf, m_r = ubuf, u_r
        mT, mT_r = uT, uT_r
        ps = st.enter_context(nc.psum_tensor("ps", [128, 4096], F32))
        bank_r = [Res(f"bank{b}", excl=True) for b in range(8)]
        B_S, B_O, B_PJ0, B_PJ1, B_T = 0, 4, 5, 6, 7

        def bank(b, lo=0, hi=512):
            return ps[:, b * 512 + lo: b * 512 + hi]

        def bank_bf(b):
            return ps[:, b * 512:(b + 1) * 512].bitcast(BF16)

        pj_ctr = [0]

        def next_pj():
            b = B_PJ0 + (pj_ctr[0] % 2)
            pj_ctr[0] += 1
            return b

        d_small = S.dma_sem("d_small")
        S.op("sp", lambda e: e.dma_start(out=small[:], in_=small_d[:, :]), writes=[small_r], dma=d_small)
        S.op("sp", lambda e: e.dma_start(out=xf[0][:], in_=x_d[0:128, :]), writes=[xf_r[0]], dma=xf_sem[0])
        S.op("dve", lambda e: e.tensor_copy(out=identb[:], in_=identf), reads=[small_r], writes=[identb_r])
        S.op("pool", lambda e: e.memset(misc[:], -0.5), writes=[misc_r])
        S.op("dve", lambda e: e.scalar_tensor_tensor(out=cab[:, 0:1], in0=cvec[:, 0:1], scalar=0.125, in1=cvec[:, 1:2],
                                                      op0=ALU.mult, op1=ALU.mult), reads=[small_r], writes=[misc_r])
        S.op("dve", lambda e: e.scalar_tensor_tensor(out=cab[:, 1:2], in0=cvec[:, 2:3], scalar=0.125, in1=cvec[:, 3:4],
                                                      op0=ALU.mult, op1=ALU.mult), reads=[small_r], writes=[misc_r])
        S.op("act", lambda e: e.activation(out=esink, in_=sinkv, func=AF.Exp), reads=[small_r], writes=[misc_r])
        for i in range(KV_R):
            S.op("pool", lambda e, i=i: e.memset(va[i][:, :, 64:65], 1.0), writes=[va_r[i]])
            S.op("pool", lambda e, i=i: e.memset(vb[i][:, :, 64:65], 1.0), writes=[vb_r[i]])

        stg_all = [(xr, [xr_r], xr_sem), (onorm[0], onorm_r[0], S.dma_sem("d_on0")), (onorm[1], onorm_r[1], S.dma_sem("d_on1")),
                   (qf, [qf_r], S.dma_sem("d_qf")), (sq, [sq_r], S.dma_sem("d_sq"))]
        stg_i = [0]
        cast_i = [0]

        def stage_load(src_ap, n, nslots):
            k = stg_i[0] % nslots
            stg_i[0] += 1
            t, r, sem = stg_all[k]
            S.op("sp", lambda e: e.dma_start(out=t[:, 0:n], in_=src_ap), writes=list(r), dma=sem)
            return t, list(r)

        def cast_scaled(dst_ap, dst_r, src_t, src_r, n, scale, scale_r=()):
            which = ("dve", "pool", "act")[cast_i[0] % 3]
            cast_i[0] += 1
            if which == "act":
                S.op("act", lambda e: e.activation(out=dst_ap, in_=src_t[:, 0:n], func=AF.Copy, scale=scale),
                     reads=[*src_r, *scale_r], writes=[dst_r])
            else:
                S.op(which, lambda e: e.tensor_scalar(out=dst_ap, in0=src_t[:, 0:n], scalar1=scale, scalar2=0.0,
                                                       op0=ALU.mult, op1=ALU.add),
                     reads=[*src_r, *scale_r], writes=[dst_r])

        def stage_win(pieces, nslots):
            for c in range(8):
                for j in pieces:
                    lo = j * 1024
                    n = min(1024, D_IN - lo)
                    t, r = stage_load(win_d[c * 128:(c + 1) * 128, lo:lo + n], n, nslots)
                    cast_scaled(win[:, c, lo:lo + n], win_r[j], t, r, n, gcol[:, c:c + 1], [small_r])
                    yield

        UAf = UA[:].rearrange("p h n -> p (h n)")
        TBf = TB[:].rearrange("p h n -> p (h n)")

        def stage_tables(nslots):
            for (dst, dst_r, bd, md, tot) in ((UAf, ua_r, uab_d, uam_d, 8 * N_UA * 128), (TBf, tb_r, tbb_d, tbm_d, 8 * 3 * 128)):
                for lo in range(0, tot, 1024):
                    n = min(1024, tot - lo)
                    t1_, r1_ = stage_load(bd[:, lo:lo + n], n, nslots)
                    t2_, r2_ = stage_load(md[:, lo:lo + n], n, nslots)
                    which = ("dve", "pool")[(lo // 1024) % 2]
                    S.op(which, lambda e, n=n, t1_=t1_, t2_=t2_: e.tensor_tensor(
                        out=t1_[:, 0:n], in0=t1_[:, 0:n], in1=t2_[:, 0:n], op=ALU.add),
                        reads=[*r2_], writes=[*r1_])
                    S.op("act", lambda e, dst=dst, lo=lo, n=n, t1_=t1_: e.activation(out=dst[:, lo:lo + n], in_=t1_[:, 0:n], func=AF.Exp),
                         reads=[*r1_], writes=[dst_r])
                    yield

        def stage_rest(nslots):
            yield from stage_tables(nslots)
            yield from stage_win([3, 4, 5], nslots)
            for c in range(4):
                t, r = stage_load(woa_d[c * 128:(c + 1) * 128, :], 1024, nslots)
                cast_scaled(woa[:, c, :], woa_r, t, r, 1024, 0.5)
                yield
                t, r = stage_load(wob_d[c * 128:(c + 1) * 128, :], 1024, nslots)
                cast_scaled(wob[:, c, :], wob_r, t, r, 1024, 0.5)
                yield
            for c in range(8):
                t, r = stage_load(wout_d[c * 128:(c + 1) * 128, :], 1024, nslots)
                cast_scaled(wout[:, c, :], wout_r, t, r, 1024, 0.5)
                yield

        tiles = []
        base = 0
        for T in seq_tiles:
            for i in range(T):
                tiles.append((base, i, T))
            base += T

        def load_xf(t):
            if t < NT:
                s2 = 0
                S.op("sp", lambda e: e.dma_start(out=xf[s2][:], in_=x_d[t * 128:(t + 1) * 128, :]), writes=[xf_r[s2]], dma=xf_sem[s2])

        def proj_chunk(t, col, n, b=None):
            s6 = t % HT_R
            if b is None:
                b = next_pj()

            def f(e):
                ins = None
                for c in range(8):
                    ins = e.matmul(bank(b, 0, n), lhsT=hT[s6][:, c, :], rhs=win[:, c, col:col + n], start=(c == 0), stop=(c == 7))
                return ins
            S.op("pe", f, reads=[hT_r[s6][0], hT_r[s6][1]] + [win_r[j] for j in range(col // 1024, (col + n - 1) // 1024 + 1)],
                 writes=[bank_r[b]])
            return b

        def rstd_from(ssq_ap, nheads, r):
            S.op("pool", lambda e: e.tensor_scalar(out=ssq_ap, in0=ssq_ap, scalar1=1.0 / 64, scalar2=EPS, op0=ALU.mult, op1=ALU.add),
                 reads=[r], writes=[r])
            S.op("pool", lambda e: e.tensor_tensor(out=ssq_ap, in0=ssq_ap, in1=cm05[:, 0:nheads], op=ALU.pow),
                 reads=[r, misc_r], writes=[r])

        def transposes_bf(src, src_r, nblk, bT):
            bb = bank_bf(bT)

            def f(e):
                ins = None
                for c in range(nblk):
                    ins = e.transpose(out=bb[:, c * 128:(c + 1) * 128], in_=src[:, c * 128:(c + 1) * 128], identity=identb[:])
                return ins
            S.op("pe", f, reads=[src_r, identb_r], writes=[bank_r[bT]])
            return bb

        def evac_T(dst, dst_r, bT):
            bb = bank_bf(bT)
            S.op("dve", lambda e: e.tensor_copy(out=dst[:].rearrange("p a b -> p (a b)"), in_=bb[:, 0:1024]),
                 reads=[bank_r[bT]], writes=[dst_r[0], dst_r[1]])

        pj_free = [B_PJ0, B_PJ1]
        t_free = [B_T]
        done_f = set()
        done_q = set()

        def acquire(pool):
            while not pool:
                yield
            return pool.pop(0)

        def proj_into(t, col, n, b):
            proj_chunk(t, col, n, b=b)

        def F(t):
            s2, s6, s8 = 0, t % HT_R, t % KV_R
            ssc = stat[:, s6:s6 + 1]
            rstd = stat[:, HT_R + s6:HT_R + s6 + 1]
            rstdh = stat[:, 2 * HT_R + s6:2 * HT_R + s6 + 1]
            S.op("act", lambda e: e.activation(out=junk, in_=xf[s2][:], func=AF.Square, accum_out=ssc),
                 reads=[xf_r[s2]], writes=[ss_r[s6], sqk_r], standalone=True)
            yield
            S.op("pool", lambda e: e.tensor_scalar(out=rstd, in0=ssc, scalar1=1.0 / D_MODEL, scalar2=EPS,
                                                   op0=ALU.mult, op1=ALU.add), reads=[ss_r[s6]], writes=[rstd_r[s6]])
            S.op("pool", lambda e: e.tensor_tensor(out=rstd, in0=rstd, in1=cm05[:, 0:1], op=ALU.pow),
                 reads=[rstd_r[s6], misc_r], writes=[rstd_r[s6]])
            S.op("pool", lambda e: e.tensor_scalar(out=rstdh, in0=rstd, scalar1=0.5, scalar2=0.0,
                                                   op0=ALU.mult, op1=ALU.add), reads=[rstd_r[s6]], writes=[rstd_r[s6]])
            for half in range(2):
                bT = yield from acquire(t_free)

                def f(e, half=half, bT=bT):
                    ins = None
                    for c in range(4):
                        cc = half * 4 + c
                        ins = e.transpose(out=bank(bT, c * 128, (c + 1) * 128), in_=xf[s2][:, cc * 128:(cc + 1) * 128], identity=identf)
                    return ins
                S.op("pe", f, reads=[xf_r[s2], small_r], writes=[bank_r[bT]])
                yield
                dst = hT[s6][:, half * 4:(half + 1) * 4, :].rearrange("p a b -> p (a b)")
                if half == 0:
                    S.op("act", lambda e, dst=dst, bT=bT: e.activation(out=dst, in_=bank(bT), func=AF.Copy),
                         reads=[bank_r[bT]], writes=[hT_r[s6][0]])
                else:
                    S.op("dve", lambda e, dst=dst, bT=bT: e.tensor_copy(out=dst, in_=bank(bT)),
                         reads=[bank_r[bT]], writes=[hT_r[s6][1]])
                t_free.append(bT)
            load_xf(t + 1)
            b2 = yield from acquire(pj_free)
            proj_into(t, C_VA, 512, b2)
            yield
            S.op("act", lambda e: e.activation(out=va[s8][:, :, 0:64], in_=bank(b2).rearrange("p (h d) -> p h d", d=64),
                                               func=AF.Copy, scale=rstd),
                 reads=[bank_r[b2], rstd_r[s6]], writes=[va_r[s8]])
            pj_free.append(b2)
            b3 = yield from acquire(pj_free)
            proj_into(t, C_KB, 256, b3)
            yield
            S.op("act", lambda e: e.activation(out=kbf[:], in_=bank(b3, 0, 128), func=AF.Copy, scale=rstd),
                 reads=[bank_r[b3], rstd_r[s6]], writes=[kbf_r])
            S.op("act", lambda e: e.activation(out=vb[s8][:, :, 0:64], in_=bank(b3, 128, 256).rearrange("p (h d) -> p h d", d=64),
                                               func=AF.Copy, scale=rstd),
                 reads=[bank_r[b3], rstd_r[s6]], writes=[vb_r[s8]])
            pj_free.append(b3)
            b = yield from acquire(pj_free)
            proj_into(t, C_KA, 512, b)
            yield
            S.op("act", lambda e: e.activation(out=kf[:], in_=bank(b), func=AF.Copy, scale=rstd),
                 reads=[bank_r[b], rstd_r[s6]], writes=[kf_r])
            pj_free.append(b)
            yield
            S.op("pool", lambda e: e.tensor_tensor(out=sqk[:, 0:512], in0=kf[:], in1=kf[:], op=ALU.mult), reads=[kf_r], writes=[sqk_r])
            S.op("pool", lambda e: e.tensor_tensor(out=sqk[:, 512:640], in0=kbf[:], in1=kbf[:], op=ALU.mult),
                 reads=[kbf_r], writes=[sqk_r])
            yield
            yield
            S.op("dve", lambda e: e.tensor_reduce(out=ssq[:, 16:26], in_=sqk[:, 0:640].rearrange("p (h d) -> p h d", d=64),
                                                  axis=AX.X, op=ALU.add), reads=[sqk_r], writes=[ssq_r[1]])
            yield
            rstd_from(ssq[:, 16:26], 10, ssq_r[1])
            yield
            yield
            S.op("pool", lambda e: e.tensor_tensor(out=nrmk[:, 0:512].rearrange("p (h d) -> p h d", d=64),
                                                   in0=kf[:].rearrange("p (h d) -> p h d", d=64),
                                                   in1=ssq[:, 16:24].unsqueeze(2).broadcast_to([128, 8, 64]), op=ALU.mult),
                 reads=[kf_r, ssq_r[1]], writes=[nrmk_r])
            S.op("pool", lambda e: e.tensor_tensor(out=nrmk[:, 512:640].rearrange("p (h d) -> p h d", d=64),
                                                   in0=kbf[:].rearrange("p (h d) -> p h d", d=64),
                                                   in1=ssq[:, 24:26].unsqueeze(2).broadcast_to([128, 2, 64]), op=ALU.mult),
                 reads=[kbf_r, ssq_r[1]], writes=[nrmk_r])
            yield
            yield
            bT = yield from acquire(t_free)
            bb = bank_bf(bT)
            transposes_bf(nrmk, nrmk_r, 5, bT)
            yield
            S.op("act", lambda e: e.activation(out=kTa[s8][:].rearrange("p a b -> p (a b)"), in_=bb[:, 0:512], func=AF.Copy,
                                               scale=cab[:, 0:1]),
                 reads=[bank_r[bT], misc_r], writes=[kTa_r[s8]])
            for kv in range(2):
                for dh in range(2):
                    src = bb[kv * 64:(kv + 1) * 64, 512:640]
                    dst = kTb[s8][dh * 64:(dh + 1) * 64, kv, :]
                    sc = cab[kv * 64:(kv + 1) * 64, 1:2]
                    if dh == 0:
                        S.op("act", lambda e, src=src, dst=dst, sc=sc: e.activation(out=dst, in_=src, func=AF.Copy, scale=sc),
                             reads=[bank_r[bT], misc_r], writes=[kTb_r[s8]])
                    else:
                        S.op("dve", lambda e, src=src, dst=dst, sc=sc: e.tensor_scalar(out=dst, in0=src, scalar1=sc, scalar2=None,
                                                                                       op0=ALU.mult),
                             reads=[bank_r[bT], misc_r], writes=[kTb_r[s8]])
            t_free.append(bT)
            done_f.add(t)
            yield

        def att_jobs(t, which):
            base, i, T = tiles[t]
            on, on_r = onorm[t % 2], onorm_r[t % 2]
            qT, qT_r = qT2[t % 2], qT2_r[t % 2]
            if which == "A":
                J, pieces = a_plan(i, T)
                tab, tab_r = UA, ua_r
            else:
                J = [j for j in (i - 1, i, i + 1) if 0 <= j < T]
                t0 = J[0] - i + 1
                pieces = [(t0, t0 + len(J))]
                tab, tab_r = TB, tb_r
            n = len(J)
            slots = [(base + j) % KV_R for j in J]
            br = 0 if which == "A" else 1

            def sinfo(k):
                sbanks = [B_S + 2 * k] + ([B_S + 2 * k + 1] if n > 4 else [])
                return sbanks, (B_S + 2 * k) * 512

            def qk(h, k):
                sbanks, soff = sinfo(k)
                half = h % 2
                qc = h // 2 if which == "A" else 4 + h // 2
                kr = [(kTa_r if which == "A" else kTb_r)[s] for s in slots]

                def fqk(e):
                    ins = None
                    for blk, s in enumerate(slots):
                        if which == "A":
                            lhs = kTa[s][half * 64:(half + 1) * 64, qc, :]
                        else:
                            lhs = kTb[s][half * 64:(half + 1) * 64, h // 4, :]
                        ins = e.matmul(ps[:, soff + blk * 128: soff + (blk + 1) * 128], lhsT=lhs,
                                       rhs=qT[half * 64:(half + 1) * 64, qc, :], start=True, stop=True)
                    return ins
                S.op("pe", fqk, reads=[*kr, qT_r[0], qT_r[1]], writes=[bank_r[b] for b in sbanks])

            def softmax(h, k):
                sbanks, soff = sinfo(k)
                S.op("act", lambda e: e.activation(out=pT[k][:, 0:n * 128], in_=ps[:, soff: soff + n * 128], func=AF.Exp),
                     reads=[bank_r[b] for b in sbanks], writes=[pT_r[k]])
                col = 0
                for (a0, a1) in pieces:
                    w = (a1 - a0) * 128
                    S.op("dve", lambda e, col=col, w=w, a0=a0, a1=a1: e.tensor_tensor(
                        out=pT[k][:, col:col + w], in0=pT[k][:, col:col + w],
                        in1=tab[:, h, a0 * 128:a1 * 128], op=ALU.mult),
                        reads=[pT_r[k], tab_r], writes=[pT_r[k]])
                    col += w

            def pv(h, k):
                vr = [(va_r if which == "A" else vb_r)[s] for s in slots]
                hh = h % 4

                def fpv(e):
                    ins = None
                    for blk, s in enumerate(slots):
                        rhs = va[s][:, h, :] if which == "A" else vb[s][:, h // 4, :]
                        ins = e.matmul(bank(B_O, hh * 65, (hh + 1) * 65), lhsT=pT[k][:, blk * 128:(blk + 1) * 128], rhs=rhs,
                                       start=(blk == 0), stop=(blk == n - 1))
                    return ins
                S.op("pe", fpv, reads=[pT_r[k], *vr], writes=[bank_r[B_O]])
                if hh == 3:
                    g = h // 4
                    ov = bank(B_O, 0, 260).rearrange("p (h d) -> p h d", d=65)
                    rd = rden[:, g * 4:(g + 1) * 4]
                    if which == "A":
                        S.op("dve", lambda e: e.reciprocal(out=rd.unsqueeze(2), in_=ov[:, :, 64:65]),
                             reads=[bank_r[B_O]], writes=[rden_r])
                    else:
                        S.op("dve", lambda e: e.tensor_tensor(out=rd.unsqueeze(2), in0=ov[:, :, 64:65],
                                                              in1=esink[:, g * 4:(g + 1) * 4].unsqueeze(2), op=ALU.add),
                             reads=[bank_r[B_O], misc_r], writes=[rden_r])
                        S.op("dve", lambda e: e.reciprocal(out=rd, in_=rd), reads=[rden_r], writes=[rden_r])
                    S.op("dve", lambda e: e.tensor_tensor(
                        out=on[:, br * 512 + g * 256: br * 512 + (g + 1) * 256].rearrange("p (h d) -> p h d", d=64),
                        in0=ov[:, :, 0:64], in1=rd.unsqueeze(2).broadcast_to([128, 4, 64]), op=ALU.mult),
                        reads=[bank_r[B_O], rden_r], writes=[on_r[br]])

            return [(lambda k, h=h: qk(h, k), lambda k, h=h: softmax(h, k), lambda k, h=h: pv(h, k)) for h in range(8)]

        def Gq(t):
            s6 = t % HT_R
            rstd = stat[:, HT_R + s6:HT_R + s6 + 1]
            for (col, lo) in ((C_QA, 0), (C_QB, 512)):
                b = yield from acquire(pj_free)
                proj_into(t, col, 512, b)
                yield
                S.op("act", lambda e, b=b, lo=lo: e.activation(out=qf[:, lo:lo + 512], in_=bank(b), func=AF.Copy, scale=rstd),
                     reads=[bank_r[b], rstd_r[s6]], writes=[qf_r])
                pj_free.append(b)
            yield
            S.op("pool", lambda e: e.tensor_tensor(out=sq[:], in0=qf[:], in1=qf[:], op=ALU.mult), reads=[qf_r], writes=[sq_r])
            yield
            yield
            S.op("dve", lambda e: e.tensor_reduce(out=ssq[:, 0:16], in_=sq[:].rearrange("p (h d) -> p h d", d=64),
                                                  axis=AX.X, op=ALU.add), reads=[sq_r], writes=[ssq_r[0]])
            yield
            rstd_from(ssq[:, 0:16], 16, ssq_r[0])
            yield
            yield
            S.op("pool", lambda e: e.tensor_tensor(out=nrm[:].rearrange("p (h d) -> p h d", d=64),
                                                   in0=qf[:].rearrange("p (h d) -> p h d", d=64),
                                                   in1=ssq[:, 0:16].unsqueeze(2).broadcast_to([128, 16, 64]), op=ALU.mult),
                 reads=[qf_r, ssq_r[0]], writes=[nrm_r])
            yield
            yield
            yield
            bT = yield from acquire(t_free)
            transposes_bf(nrm, nrm_r, 8, bT)
            yield
            evac_T(qT2[t % 2], qT2_r[t % 2], bT)
            t_free.append(bT)
            done_q.add(t)
            yield

        def at_prologue(t):
            jobs = att_jobs(t, "A")
            jobs[0][0](0)
            jobs[0][1](0)
            jobs[1][0](1)

        def At(t):
            jobs = att_jobs(t, "A") + att_jobs(t, "B")
            nxt = None
            for j in range(16):
                if j + 2 >= 16 and t + 1 < NT and nxt is None:
                    while not ((t + 1) in done_q and (min(t + LEAD, NT - 1)) in done_f):
                        yield
                    nxt = att_jobs(t + 1, "A")[0:2]
                    jobs = jobs + nxt
                if j + 1 < len(jobs):
                    jobs[j + 1][1]((j + 1) % 2)
                jobs[j][2](j % 2)
                if j + 2 < len(jobs):
                    jobs[j + 2][0](j % 2)
                yield

        def Gb(t):
            s6 = t % HT_R
            rstd = stat[:, HT_R + s6:HT_R + s6 + 1]
            rstdh = stat[:, 2 * HT_R + s6:2 * HT_R + s6 + 1]
            on, on_r = onorm[t % 2], onorm_r[t % 2]
            S.op("sp", lambda e: e.dma_start(out=xr[:], in_=x_d[t * 128:(t + 1) * 128, :]), writes=[xr_r], dma=xr_sem)
            for br, col in ((0, C_ZA), (1, C_ZB)):
                b = yield from acquire(pj_free)
                proj_into(t, col, 512, b)
                yield
                th, th_r = tmp[0], tmp_r[0]
                t1, t1_r = tmp[1], tmp_r[1]
                S.op("act", lambda e, b=b, th=th: e.activation(out=th[:], in_=bank(b), func=AF.Tanh, scale=rstdh),
                     reads=[bank_r[b], rstd_r[s6]], writes=[th_r])
                S.op("dve", lambda e, b=b, t1=t1, br=br: e.scalar_tensor_tensor(out=t1[:], in0=bank(b), scalar=rstd,
                                                                                in1=on[:, br * 512:(br + 1) * 512],
                                                                                op0=ALU.mult, op1=ALU.mult),
                     reads=[bank_r[b], rstd_r[s6], on_r[br]], writes=[t1_r])
                pj_free.append(b)
                yield
                S.op("dve", lambda e, th=th, t1=t1, br=br: e.scalar_tensor_tensor(out=ubuf[:, br * 512:(br + 1) * 512], in0=th[:], scalar=1.0,
                                                                                  in1=t1[:], op0=ALU.add, op1=ALU.mult),
                     reads=[th_r, t1_r], writes=[u_r])
                yield
            bT = yield from acquire(t_free)
            transposes_bf(ubuf, u_r, 8, bT)
            yield
            evac_T(uT, uT_r, bT)
            t_free.append(bT)
            yield
            for nh in range(2):
                for br, gcol_, wo, wo_r in ((0, C_GA, woa, woa_r), (1, C_GB, wob, wob_r)):
                    b = yield from acquire(pj_free)
                    proj_into(t, gcol_ + nh * 512, 512, b)
                    yield
                    th, th_r = tmp[0], tmp_r[0]
                    S.op("act", lambda e, b=b, th=th: e.activation(out=th[:], in_=bank(b), func=AF.Tanh, scale=rstdh),
                         reads=[bank_r[b], rstd_r[s6]], writes=[th_r])
                    pj_free.append(b)
                    b2 = yield from acquire(pj_free)

                    def fo(e, br=br, wo=wo, b2=b2, nh=nh):
                        ins = None
                        for c in range(4):
                            ins = e.matmul(bank(b2), lhsT=uT[:, br * 4 + c, :], rhs=wo[:, c, nh * 512:(nh + 1) * 512],
                                           start=(c == 0), stop=(c == 3))
                        return ins
                    S.op("pe", fo, reads=[uT_r[br], wo_r], writes=[bank_r[b2]])
                    yield
                    m1, m1_r = tmp[1 + br], tmp_r[1 + br]
                    S.op("dve", lambda e, th=th, m1=m1, b2=b2: e.scalar_tensor_tensor(out=m1[:], in0=th[:], scalar=1.0, in1=bank(b2),
                                                                                      op0=ALU.add, op1=ALU.mult),
                         reads=[th_r, bank_r[b2]], writes=[m1_r])
                    pj_free.append(b2)
                yield
                S.op("pool", lambda e, nh=nh: e.tensor_tensor(out=mbuf[:, nh * 512:(nh + 1) * 512], in0=tmp[1][:], in1=tmp[2][:], op=ALU.add),
                     reads=[tmp_r[1], tmp_r[2]], writes=[m_r])
                yield
            yield
            bT = yield from acquire(t_free)
            transposes_bf(mbuf, m_r, 8, bT)
            yield
            evac_T(mT, mT_r, bT)
            t_free.append(bT)
            yield
            for nh in range(2):
                b = yield from acquire(pj_free)

                def fw(e, b=b, nh=nh):
                    ins = None
                    for c in range(8):
                        ins = e.matmul(bank(b), lhsT=mT[:, c, :], rhs=wout[:, c, nh * 512:(nh + 1) * 512], start=(c == 0), stop=(c == 7))
                    return ins
                S.op("pe", fw, reads=[mT_r[0], mT_r[1], wout_r], writes=[bank_r[b]])
                yield
                S.op("dve", lambda e, b=b, nh=nh: e.tensor_tensor(out=xr[:, nh * 512:(nh + 1) * 512], in0=bank(b),
                                                                  in1=xr[:, nh * 512:(nh + 1) * 512], op=ALU.add),
                     reads=[bank_r[b], xr_r], writes=[xr_r])
                pj_free.append(b)
            last_store[0] = S.op("sp", lambda e: e.dma_start(out=y_d[t * 128:(t + 1) * 128, :], in_=xr[:]), reads=[xr_r], dma=st_sem)

        def dump(t):
            s6, s8 = t % HT_R, t % KV_R
            items = [("stat", stat[:], F32, [128, 3 * HT_R], []), ("hT", hT[s6][:].rearrange("p a b -> p (a b)"), BF16, [128, 1024], hT_r[s6]),
                     ("qT", qT2[t % 2][:].rearrange("p a b -> p (a b)"), BF16, [128, 1024], qT2_r[t % 2]),
                     ("kTa", kTa[s8][:].rearrange("p a b -> p (a b)"), BF16, [128, 512], [kTa_r[s8]]),
                     ("va", va[s8][:].rearrange("p a b -> p (a b)"), BF16, [128, 520], [va_r[s8]]),
                     ("kTb", kTb[s8][:].rearrange("p a b -> p (a b)"), BF16, [128, 256], [kTb_r[s8]]),
                     ("vb", vb[s8][:].rearrange("p a b -> p (a b)"), BF16, [128, 130], [vb_r[s8]]),
                     ("onorm", onorm[t % 2][:], F32, [128, 1024], onorm_r[t % 2]), ("ubuf", ubuf[:], BF16, [128, 1024], [u_r]),
                     ("mbuf", mbuf[:], BF16, [128, 1024], [m_r])]
            evs = []
            dsem = S.dma_sem("d_dbg")
            for name, ap, dt, shape, rs in items:
                d = nc.dram_tensor("dbg_" + name, shape, dt, kind="ExternalOutput").ap()
                evs.append(S.op("sp", lambda e, d=d, ap=ap: e.dma_start(out=d[:, :], in_=ap), reads=list(rs), dma=dsem))
            return evs[-1]

        last_store = [None]

        def count_steps(mk):
            S.dry = True
            saved = (pj_ctr[0], list(pj_free), list(t_free), set(done_f), set(done_q))
            n = sum(1 for _ in mk())
            pj_ctr[0] = saved[0]
            pj_free[:] = saved[1]
            t_free[:] = saved[2]
            done_f.clear(); done_f.update(saved[3])
            done_q.clear(); done_q.update(saved[4])
            S.dry = False
            return n + 1

        def run_all(makers):
            makers = [m for m in makers if m is not None]
            order = []
            for pri, (mk, span, dry_ok) in enumerate(makers):
                n = count_steps(mk) if dry_ok is True else (17 if dry_ok is False else dry_ok)
                for j in range(n):
                    order.append(((j + 0.5) / n * span, pri))
            order.sort()
            gens = [mk() for mk, _, _ in makers]
            alive = [True] * len(gens)
            trace_seq = []
            for _, pri in order:
                if alive[pri]:
                    trace_seq.append(pri)
                    try:
                        next(gens[pri])
                    except StopIteration:
                        alive[pri] = False
            guard = 0
            while any(alive):
                for pri in range(len(gens)):
                    if alive[pri]:
                        trace_seq.append(10 + pri)
                        try:
                            next(gens[pri])
                        except StopIteration:
                            alive[pri] = False
                guard += 1
                assert guard < 10000, "stream scheduling deadlock"
            if len(makers) == 4 and DEBUG_SEQ:
                print("SEQ", "".join("ABQF"[p] if p < 10 else "abqf"[p - 10] for p in trace_seq))

        LEAD = 4
        for _ in stage_win([0, 1, 2], 5):
            pass
        rest = stage_rest(3)
        stg_i[0] = 0

        def take(n):
            for _ in range(n):
                try:
                    next(rest)
                except StopIteration:
                    return
                yield

        for t in range(min(LEAD, NT)):
            run_all([(lambda t=t: F(t), 1.0, True), (lambda: take(14), 1.0, 15)])
        run_all([(lambda: Gq(0), 1.0, True), (lambda: take(14), 1.0, 15)])
        for _ in rest:
            pass
        at_prologue(0)
        dbg_ev = None
        for t in range(NT + 1):
            run_all([((lambda t=t: At(t)), 0.88, False) if t < NT else None,
                     ((lambda t=t: Gb(t - 1)), 1.0, True) if t >= 1 else None,
                     ((lambda t=t: Gq(t + 1)), 0.66, True) if t + 1 < NT else None,
                     ((lambda t=t: F(t + LEAD)), 0.70, True) if t + LEAD < NT else None])
            if dbg is not None and t - 1 == dbg:
                dbg_ev = dump(dbg)
        S.final_wait("sp", [last_store[0], dbg_ev])
        S.emit()
    return nc


def _shared_inputs(norm_g, w_in, qn_a, kn_a, rpb_a, qn_b, kn_b, sink_b, w_o_a, w_o_b, w_out, t5_table):
    f = lambda a: np.ascontiguousarray(np.asarray(a, dtype=np.float32))
    small = np.zeros((128, 160), np.float32)
    small[:, 0:128] = np.eye(128, dtype=np.float32)
    small[:, 128:136] = f(norm_g)[0].reshape(8, 128).T
    small[:, 136] = np.tile(f(qn_a)[0], 2)
    small[:, 137] = np.tile(f(kn_a)[0], 2)
    small[:, 138] = np.tile(f(qn_b)[0], 2)
    small[:, 139] = np.tile(f(kn_b)[0], 2)
    small[:, 140:148] = f(sink_b)[0][None, :]
    uab, uam = _a_tables(f(rpb_a)[0])
    tbb, tbm = _b_tables(f(t5_table))
    return {"w_in": f(w_in)[0], "w_oa": f(w_o_a)[0], "w_ob": f(w_o_b)[0], "w_out": f(w_out)[0], "small": small,
            "ua_bias": uab, "ua_mask": uam, "tb_bias": tbb, "tb_mask": tbm}


_PROG_CACHE = {}


def run_layer(seqs_per_core, shared, n_cores, dbg=None):
    seq_tiles = tuple(s.shape[0] // 128 for s in seqs_per_core[0])
    if dbg is not None:
        nc = build_program(list(seq_tiles), dbg=dbg)
    else:
        if seq_tiles not in _PROG_CACHE:
            _PROG_CACHE[seq_tiles] = build_program(list(seq_tiles))
        nc = _PROG_CACHE[seq_tiles]
    in_maps = []
    for c in range(n_cores):
        m = dict(shared)
        m["x"] = np.ascontiguousarray(np.concatenate(seqs_per_core[c], axis=0), dtype=np.float32)
        in_maps.append(m)
    res = run_bass_kernel_spmd(nc, in_maps, core_ids=list(range(n_cores)))
    if dbg is not None:
        return res.results
    return [r["y"] for r in res.results]


def kernel(x_prompt, x_sample, norm_g, w_in, qn_a, kn_a, rpb_a, qn_b, kn_b, sink_b, w_o_a, w_o_b, w_out, t5_table):
    x_prompt = np.asarray(x_prompt, dtype=np.float32)
    x_sample = np.asarray(x_sample, dtype=np.float32)
    shared = _shared_inputs(norm_g, w_in, qn_a, kn_a, rpb_a, qn_b, kn_b, sink_b, w_o_a, w_o_b, w_out, t5_table)
    n = 8
    seqs = [[x_prompt[c], x_sample[c]] for c in range(n)]
    ys = run_layer(seqs, shared, n)
    Lp = x_prompt.shape[1]
    y_prompt = np.stack([ys[c][:Lp] for c in range(n)], axis=0)
    y_sample = np.stack([ys[c][Lp:] for c in range(n)], axis=0)
    return (y_prompt.astype(np.float32), y_sample.astype(np.float32))
```

```python
import math
from contextlib import ExitStack

import numpy as np
import concourse.bass as bass
import concourse.mybir as mybir
from concourse.bass_utils import run_bass_kernel_spmd

F32 = mybir.dt.float32
BF16 = mybir.dt.bfloat16
AF = mybir.ActivationFunctionType
ALU = mybir.AluOpType
AX = mybir.AxisListType

D_MODEL = 1024
D_IN = 5376
EPS = 1e-6
NEG = -30000.0
C_QA, C_KA, C_VA, C_ZA, C_QB, C_KB, C_VB, C_ZB, C_GA, C_GB = 0, 512, 1024, 1536, 2048, 2560, 2688, 2816, 3328, 4352
DEBUG_SEQ = False
KV_R = 8
HT_R = 6
N_UA = 9
UA_ORDER = [(-3, False), (-2, False), (-2, True), (-1, False), (0, False), (1, False), (2, True), (2, False), (3, False)]


class Res:
    __slots__ = ("name", "w", "r", "excl")

    def __init__(self, name, excl=False):
        self.name = name
        self.w = None
        self.r = {}
        self.excl = excl


class Sched:
    SAME_ENGINE_SYNC = True

    def __init__(self, nc, stack):
        self.nc = nc
        self.stack = stack
        self.eng = {"pe": nc.tensor, "act": nc.scalar, "dve": nc.vector, "pool": nc.gpsimd, "sp": nc.sync}
        self.ops = {k: [] for k in self.eng}
        self.sems = {}
        self.cnt = {}
        for k in ("pe", "act", "dve", "pool"):
            self.sems[k] = stack.enter_context(nc.semaphore("s_" + k))
            self.cnt[k] = 0
        self.known = {k: {} for k in self.eng}

    def dma_sem(self, name):
        self.sems[name] = self.stack.enter_context(self.nc.semaphore(name))
        self.cnt[name] = 0
        return name

    dry = False

    def op(self, eng, fn, reads=(), writes=(), dma=None, standalone=False):
        if self.dry:
            return None
        deps = {}

        def add(ev):
            if ev is not None and deps.get(ev[0], 0) < ev[1]:
                deps[ev[0]] = ev[1]

        def addall(d):
            for k, v in d.items():
                if deps.get(k, 0) < v:
                    deps[k] = v

        for r in reads:
            add(r.w)
            if r.excl:
                addall(r.r)
        for w in writes:
            add(w.w)
            addall(w.r)
        waits = []
        kn = self.known[eng]
        for k, v in deps.items():
            if k == eng and (eng == "pe" or not self.SAME_ENGINE_SYNC):
                continue
            if kn.get(k, 0) >= v:
                continue
            waits.append((k, v))
            kn[k] = v
        if dma is None:
            self.cnt[eng] += 1
            ev = (eng, self.cnt[eng])
            inc = (eng, 1)
        else:
            self.cnt[dma] += 16
            ev = (dma, self.cnt[dma])
            inc = (dma, 16)
        self.ops[eng].append((fn, waits, inc, standalone))
        for r in reads:
            if r.excl:
                r.w = ev
                r.r = {}
            elif r.r.get(ev[0], 0) < ev[1]:
                r.r[ev[0]] = ev[1]
        for w in writes:
            w.w = ev
            w.r = {}
        return ev

    def final_wait(self, eng, events):
        self.ops[eng].append((None, [e for e in events if e is not None], None, True))

    def emit(self):
        sems = self.sems
        with self.nc.Block() as block:
            for name, deco in (("sp", block.sync), ("act", block.scalar), ("dve", block.vector),
                               ("pool", block.gpsimd), ("pe", block.tensor)):
                ops = self.ops[name]

                def body(e, ops=ops, name=name):
                    for fn, waits, inc, standalone in ops:
                        if fn is None:
                            for k, v in waits:
                                e.wait_ge(sems[k], v)
                            continue
                        attach = None
                        ws = waits
                        if name in ("act", "dve", "pool") and waits and not standalone:
                            attach = waits[-1]
                            ws = waits[:-1]
                        for k, v in ws:
                            e.wait_ge(sems[k], v)
                        ins = fn(e)
                        if attach is not None:
                            ins._wait_ge(sems[attach[0]], attach[1])
                        ins.then_inc(sems[inc[0]], inc[1])

                deco(body)


def _a_tables(rpb):
    p = np.arange(128)
    a, kc = p // 64, p % 64
    q = np.arange(128)
    b, qc = q // 64, q % 64
    cs = np.clip(qc - 8, 0, 48)
    colv = (kc[:, None] >= cs[None, :]) & (kc[:, None] < cs[None, :] + 16)
    dc = np.clip(kc[:, None] - qc[None, :] + 15, 0, 30)
    bias = np.zeros((128, 8, N_UA, 128), np.float32)
    mask = np.zeros((128, 8, N_UA, 128), np.float32)
    for t, (delta, gen) in enumerate(UA_ORDER):
        dr = 2 * delta + a[:, None] - b[None, :]
        rowv = (dr >= -4) & (dr <= 3) if gen else (np.abs(dr) <= 7)
        valid = rowv & colv
        dri = np.clip(dr + 7, 0, 14)
        for h in range(8):
            bias[:, h, t, :] = rpb[h][dri, dc]
        mask[:, :, t, :] = np.where(valid, 0.0, NEG)[:, None, :]
    return bias.reshape(128, -1), mask.reshape(128, -1)


def _t5_bucket_np(rel):
    half, max_exact = 16, 8
    n = np.abs(rel)
    try:
        import jax
        import jax.numpy as jnp
        with jax.default_device(jax.devices("cpu")[0]):
            nn = jnp.asarray(n.astype(np.int32))
            large = max_exact + (jnp.log(jnp.maximum(nn, 1).astype(jnp.float32) / max_exact)
                                 / math.log(128 / max_exact) * (half - max_exact)).astype(jnp.int32)
            large = np.asarray(jnp.minimum(large, half - 1))
    except Exception:
        lg = (np.log(np.maximum(n, 1).astype(np.float32) / np.float32(max_exact)) / np.float32(math.log(128 / max_exact))
              * np.float32(half - max_exact))
        large = np.minimum(max_exact + lg.astype(np.int32), half - 1)
    return np.where(rel > 0, half, 0) + np.where(n < max_exact, n, large)


def _b_tables(t5):
    p = np.arange(128)[:, None]
    q = np.arange(128)[None, :]
    bias = np.zeros((128, 8, 3, 128), np.float32)
    mask = np.zeros((128, 8, 3, 128), np.float32)
    for t, delta in enumerate((-1, 0, 1)):
        rel = 128 * delta + p - q
        bk = _t5_bucket_np(np.clip(rel, -128, 128))
        valid = np.abs(rel) <= 128
        for h in range(8):
            bias[:, h, t, :] = t5[bk, h]
        mask[:, :, t, :] = np.where(valid, 0.0, NEG)[:, None, :]
    return bias.reshape(128, -1), mask.reshape(128, -1)


def a_plan(i, T):
    if T < 6:
        raise ValueError("sequence too short")
    if 2 <= i <= T - 3:
        return list(range(i - 2, i + 3)), [(2, 7)]
    if i == 0:
        return [0, 1, 2, 3], [(4, 6), (7, 9)]
    if i == 1:
        return [0, 1, 2, 3], [(3, 6), (7, 8)]
    if i == T - 2:
        return list(range(T - 4, T)), [(1, 2), (3, 6)]
    return list(range(T - 4, T)), [(0, 2), (3, 5)]


def build_program(seq_tiles, dbg=None):
    NT = sum(seq_tiles)
    nc = bass.Bass("TRN2", target_bir_lowering=False, dynamic_dma_scratch_size=256)

    def dram(name, shape, kind):
        return nc.dram_tensor(name, shape, F32, kind=kind).ap()

    x_d = dram("x", [NT * 128, D_MODEL], "ExternalInput")
    y_d = dram("y", [NT * 128, D_MODEL], "ExternalOutput")
    win_d = dram("w_in", [D_MODEL, D_IN], "ExternalInput")
    woa_d = dram("w_oa", [512, D_MODEL], "ExternalInput")
    wob_d = dram("w_ob", [512, D_MODEL], "ExternalInput")
    wout_d = dram("w_out", [D_MODEL, D_MODEL], "ExternalInput")
    small_d = dram("small", [128, 160], "ExternalInput")
    uab_d = dram("ua_bias", [128, 8 * N_UA * 128], "ExternalInput")
    uam_d = dram("ua_mask", [128, 8 * N_UA * 128], "ExternalInput")
    tbb_d = dram("tb_bias", [128, 8 * 3 * 128], "ExternalInput")
    tbm_d = dram("tb_mask", [128, 8 * 3 * 128], "ExternalInput")

    with ExitStack() as st:
        S = Sched(nc, st)

        def sb(name, shape, dt):
            return st.enter_context(nc.sbuf_tensor("sb_" + name, shape, dt))

        win = sb("win", [128, 8, D_IN], BF16); win_r = [Res(f"win{j}") for j in range(6)]
        woa = sb("woa", [128, 4, D_MODEL], BF16); woa_r = Res("woa")
        wob = sb("wob", [128, 4, D_MODEL], BF16); wob_r = Res("wob")
        wout = sb("wout", [128, 8, D_MODEL], BF16); wout_r = Res("wout")
        UA = sb("UA", [128, 8, N_UA * 128], BF16); ua_r = Res("UA")
        TB = sb("TB", [128, 8, 3 * 128], BF16); tb_r = Res("TB")
        small = sb("small", [128, 160], F32); small_r = Res("small")
        identf = small[:, 0:128]
        gcol = small[:, 128:136]
        cvec = small[:, 136:140]
        sinkv = small[:, 140:148]
        identb = sb("identb", [128, 128], BF16); identb_r = Res("identb")
        misc = sb("misc", [128, 64], F32); misc_r = Res("misc")
        cab = misc[:, 0:2]
        esink = misc[:, 2:10]
        cm05 = misc[:, 16:32]
        xf = [sb(f"xf{i}", [128, D_MODEL], F32) for i in range(1)]
        xf_r = [Res(f"xf{i}") for i in range(1)]
        xf_sem = [S.dma_sem(f"d_xf{i}") for i in range(1)]
        qf = sb("qf", [128, 1024], F32); qf_r = Res("qf")
        xr = sb("xr", [128, D_MODEL], F32); xr_r = Res("xr"); xr_sem = S.dma_sem("d_xr"); st_sem = S.dma_sem("d_st")
        hT = [sb(f"hT{i}", [128, 8, 128], BF16) for i in range(HT_R)]
        hT_r = [(Res(f"hTa{i}"), Res(f"hTb{i}")) for i in range(HT_R)]
        stat = sb("stat", [128, 3 * HT_R], F32)
        ss_r = [Res(f"ss{i}") for i in range(HT_R)]
        rstd_r = [Res(f"rstd{i}") for i in range(HT_R)]
        kTa = [sb(f"kTa{i}", [128, 4, 128], BF16) for i in range(KV_R)]
        va = [sb(f"va{i}", [128, 8, 65], BF16) for i in range(KV_R)]
        kTb = [sb(f"kTb{i}", [128, 2, 128], BF16) for i in range(KV_R)]
        vb = [sb(f"vb{i}", [128, 2, 65], BF16) for i in range(KV_R)]
        kTa_r = [Res(f"kTa{i}") for i in range(KV_R)]
        va_r = [Res(f"va{i}") for i in range(KV_R)]
        kTb_r = [Res(f"kTb{i}") for i in range(KV_R)]
        vb_r = [Res(f"vb{i}") for i in range(KV_R)]
        sq = sb("sq", [128, 1024], F32); sq_r = Res("sq")
        ssq = sb("ssq", [128, 32], F32); ssq_r = (Res("ssq_q"), Res("ssq_k"))
        nrm = sb("nrm", [128, 1024], BF16); nrm_r = Res("nrm")
        nrmk = sb("nrmk", [128, 640], BF16); nrmk_r = Res("nrmk")
        kbf = sb("kbf", [128, 128], F32); kbf_r = Res("kbf")
        kf = sb("kf", [128, 512], F32); kf_r = Res("kf")
        sqk = sb("sqk", [128, 640], BF16); sqk_r = Res("sqk")
        junk = sqk[:].bitcast(mybir.dt.int8)[:, 0:1024]
        qT2 = [sb(f"qT{i}", [128, 8, 128], BF16) for i in range(2)]; qT2_r = [(Res(f"qTa{i}"), Res(f"qTb{i}")) for i in range(2)]
        pT = [sb(f"pT{i}", [128, 640], BF16) for i in range(2)]; pT_r = [Res(f"pT{i}") for i in range(2)]
        onorm = [sb(f"onorm{i}", [128, 1024], F32) for i in range(2)]
        onorm_r = [[Res(f"onormA{i}"), Res(f"onormB{i}")] for i in range(2)]
        rden = sb("rden", [128, 8], F32); rden_r = Res("rden")
        tmp = [sb(f"tmp{i}", [128, 512], F32) for i in range(3)]; tmp_r = [Res(f"tmp{i}") for i in range(3)]
        ubuf = sb("ubuf", [128, 1024], BF16); u_r = Res("u")
        uT = sb("uT", [128, 8, 128], BF16); uT_r = (Res("uTa"), Res("uTb"))
        mbuf, m_r = ubuf, u_r
        mT, mT_r = uT, uT_r
        ps = st.enter_context(nc.psum_tensor("ps", [128, 4096], F32))
        bank_r = [Res(f"bank{b}", excl=True) for b in range(8)]
        B_S, B_O, B_PJ0, B_PJ1, B_T = 0, 4, 5, 6, 7

        def bank(b, lo=0, hi=512):
            return ps[:, b * 512 + lo: b * 512 + hi]

        def bank_bf(b):
            return ps[:, b * 512:(b + 1) * 512].bitcast(BF16)

        pj_ctr = [0]

        def next_pj():
            b = B_PJ0 + (pj_ctr[0] % 2)
            pj_ctr[0] += 1
            return b

        d_small = S.dma_sem("d_small")
        S.op("sp", lambda e: e.dma_start(out=small[:], in_=small_d[:, :]), writes=[small_r], dma=d_small)
        S.op("sp", lambda e: e.dma_start(out=xf[0][:], in_=x_d[0:128, :]), writes=[xf_r[0]], dma=xf_sem[0])
        S.op("dve", lambda e: e.tensor_copy(out=identb[:], in_=identf), reads=[small_r], writes=[identb_r])
        S.op("pool", lambda e: e.memset(misc[:], -0.5), writes=[misc_r])
        S.op("dve", lambda e: e.scalar_tensor_tensor(out=cab[:, 0:1], in0=cvec[:, 0:1], scalar=0.125, in1=cvec[:, 1:2],
                                                      op0=ALU.mult, op1=ALU.mult), reads=[small_r], writes=[misc_r])
        S.op("dve", lambda e: e.scalar_tensor_tensor(out=cab[:, 1:2], in0=cvec[:, 2:3], scalar=0.125, in1=cvec[:, 3:4],
                                                      op0=ALU.mult, op1=ALU.mult), reads=[small_r], writes=[misc_r])
        S.op("act", lambda e: e.activation(out=esink, in_=sinkv, func=AF.Exp), reads=[small_r], writes=[misc_r])
        for i in range(KV_R):
            S.op("pool", lambda e, i=i: e.memset(va[i][:, :, 64:65], 1.0), writes=[va_r[i]])
            S.op("pool", lambda e, i=i: e.memset(vb[i][:, :, 64:65], 1.0), writes=[vb_r[i]])

        stg_all = [(xr, [xr_r], xr_sem), (onorm[0], onorm_r[0], S.dma_sem("d_on0")), (onorm[1], onorm_r[1], S.dma_sem("d_on1")),
                   (qf, [qf_r], S.dma_sem("d_qf")), (sq, [sq_r], S.dma_sem("d_sq"))]
        stg_i = [0]
        cast_i = [0]

        def stage_load(src_ap, n, nslots):
            k = stg_i[0] % nslots
            stg_i[0] += 1
            t, r, sem = stg_all[k]
            S.op("sp", lambda e: e.dma_start(out=t[:, 0:n], in_=src_ap), writes=list(r), dma=sem)
            return t, list(r)

        def cast_scaled(dst_ap, dst_r, src_t, src_r, n, scale, scale_r=()):
            which = ("dve", "pool", "act")[cast_i[0] % 3]
            cast_i[0] += 1
            if which == "act":
                S.op("act", lambda e: e.activation(out=dst_ap, in_=src_t[:, 0:n], func=AF.Copy, scale=scale),
                     reads=[*src_r, *scale_r], writes=[dst_r])
            else:
                S.op(which, lambda e: e.tensor_scalar(out=dst_ap, in0=src_t[:, 0:n], scalar1=scale, scalar2=0.0,
                                                       op0=ALU.mult, op1=ALU.add),
                     reads=[*src_r, *scale_r], writes=[dst_r])

        def stage_win(pieces, nslots):
            for c in range(8):
                for j in pieces:
                    lo = j * 1024
                    n = min(1024, D_IN - lo)
                    t, r = stage_load(win_d[c * 128:(c + 1) * 128, lo:lo + n], n, nslots)
                    cast_scaled(win[:, c, lo:lo + n], win_r[j], t, r, n, gcol[:, c:c + 1], [small_r])
                    yield

        UAf = UA[:].rearrange("p h n -> p (h n)")
        TBf = TB[:].rearrange("p h n -> p (h n)")

        def stage_tables(nslots):
            for (dst, dst_r, bd, md, tot) in ((UAf, ua_r, uab_d, uam_d, 8 * N_UA * 128), (TBf, tb_r, tbb_d, tbm_d, 8 * 3 * 128)):
                for lo in range(0, tot, 1024):
                    n = min(1024, tot - lo)
                    t1_, r1_ = stage_load(bd[:, lo:lo + n], n, nslots)
                    t2_, r2_ = stage_load(md[:, lo:lo + n], n, nslots)
                    which = ("dve", "pool")[(lo // 1024) % 2]
                    S.op(which, lambda e, n=n, t1_=t1_, t2_=t2_: e.tensor_tensor(
                        out=t1_[:, 0:n], in0=t1_[:, 0:n], in1=t2_[:, 0:n], op=ALU.add),
                        reads=[*r2_], writes=[*r1_])
                    S.op("act", lambda e, dst=dst, lo=lo, n=n, t1_=t1_: e.activation(out=dst[:, lo:lo + n], in_=t1_[:, 0:n], func=AF.Exp),
                         reads=[*r1_], writes=[dst_r])
                    yield

        def stage_rest(nslots):
            yield from stage_tables(nslots)
            yield from stage_win([3, 4, 5], nslots)
            for c in range(4):
                t, r = stage_load(woa_d[c * 128:(c + 1) * 128, :], 1024, nslots)
                cast_scaled(woa[:, c, :], woa_r, t, r, 1024, 0.5)
                yield
                t, r = stage_load(wob_d[c * 128:(c + 1) * 128, :], 1024, nslots)
                cast_scaled(wob[:, c, :], wob_r, t, r, 1024, 0.5)
                yield
            for c in range(8):
                t, r = stage_load(wout_d[c * 128:(c + 1) * 128, :], 1024, nslots)
                cast_scaled(wout[:, c, :], wout_r, t, r, 1024, 0.5)
                yield

        tiles = []
        base = 0
        for T in seq_tiles:
            for i in range(T):
                tiles.append((base, i, T))
            base += T

        def load_xf(t):
            if t < NT:
                s2 = 0
                S.op("sp", lambda e: e.dma_start(out=xf[s2][:], in_=x_d[t * 128:(t + 1) * 128, :]), writes=[xf_r[s2]], dma=xf_sem[s2])

        def proj_chunk(t, col, n, b=None):
            s6 = t % HT_R
            if b is None:
                b = next_pj()

            def f(e):
                ins = None
                for c in range(8):
                    ins = e.matmul(bank(b, 0, n), lhsT=hT[s6][:, c, :], rhs=win[:, c, col:col + n], start=(c == 0), stop=(c == 7))
                return ins
            S.op("pe", f, reads=[hT_r[s6][0], hT_r[s6][1]] + [win_r[j] for j in range(col // 1024, (col + n - 1) // 1024 + 1)],
                 writes=[bank_r[b]])
            return b

        def rstd_from(ssq_ap, nheads, r):
            S.op("pool", lambda e: e.tensor_scalar(out=ssq_ap, in0=ssq_ap, scalar1=1.0 / 64, scalar2=EPS, op0=ALU.mult, op1=ALU.add),
                 reads=[r], writes=[r])
            S.op("pool", lambda e: e.tensor_tensor(out=ssq_ap, in0=ssq_ap, in1=cm05[:, 0:nheads], op=ALU.pow),
                 reads=[r, misc_r], writes=[r])

        def transposes_bf(src, src_r, nblk, bT):
            bb = bank_bf(bT)

            def f(e):
                ins = None
                for c in range(nblk):
                    ins = e.transpose(out=bb[:, c * 128:(c + 1) * 128], in_=src[:, c * 128:(c + 1) * 128], identity=identb[:])
                return ins
            S.op("pe", f, reads=[src_r, identb_r], writes=[bank_r[bT]])
            return bb

        def evac_T(dst, dst_r, bT):
            bb = bank_bf(bT)
            S.op("dve", lambda e: e.tensor_copy(out=dst[:].rearrange("p a b -> p (a b)"), in_=bb[:, 0:1024]),
                 reads=[bank_r[bT]], writes=[dst_r[0], dst_r[1]])

        pj_free = [B_PJ0, B_PJ1]
        t_free = [B_T]
        done_f = set()
        done_q = set()

        def acquire(pool):
            while not pool:
                yield
            return pool.pop(0)

        def proj_into(t, col, n, b):
            proj_chunk(t, col, n, b=b)

        def F(t):
            s2, s6, s8 = 0, t % HT_R, t % KV_R
            ssc = stat[:, s6:s6 + 1]
            rstd = stat[:, HT_R + s6:HT_R + s6 + 1]
            rstdh = stat[:, 2 * HT_R + s6:2 * HT_R + s6 + 1]
            S.op("act", lambda e: e.activation(out=junk, in_=xf[s2][:], func=AF.Square, accum_out=ssc),
                 reads=[xf_r[s2]], writes=[ss_r[s6], sqk_r], standalone=True)
            yield
            S.op("pool", lambda e: e.tensor_scalar(out=rstd, in0=ssc, scalar1=1.0 / D_MODEL, scalar2=EPS,
                                                   op0=ALU.mult, op1=ALU.add), reads=[ss_r[s6]], writes=[rstd_r[s6]])
            S.op("pool", lambda e: e.tensor_tensor(out=rstd, in0=rstd, in1=cm05[:, 0:1], op=ALU.pow),
                 reads=[rstd_r[s6], misc_r], writes=[rstd_r[s6]])
            S.op("pool", lambda e: e.tensor_scalar(out=rstdh, in0=rstd, scalar1=0.5, scalar2=0.0,
                                                   op0=ALU.mult, op1=ALU.add), reads=[rstd_r[s6]], writes=[rstd_r[s6]])
            for half in range(2):
                bT = yield from acquire(t_free)

                def f(e, half=half, bT=bT):
                    ins = None
                    for c in range(4):
                        cc = half * 4 + c
                        ins = e.transpose(out=bank(bT, c * 128, (c + 1) * 128), in_=xf[s2][:, cc * 128:(cc + 1) * 128], identity=identf)
                    return ins
                S.op("pe", f, reads=[xf_r[s2], small_r], writes=[bank_r[bT]])
                yield
                dst = hT[s6][:, half * 4:(half + 1) * 4, :].rearrange("p a b -> p (a b)")
                if half == 0:
                    S.op("act", lambda e, dst=dst, bT=bT: e.activation(out=dst, in_=bank(bT), func=AF.Copy),
                         reads=[bank_r[bT]], writes=[hT_r[s6][0]])
                else:
                    S.op("dve", lambda e, dst=dst, bT=bT: e.tensor_copy(out=dst, in_=bank(bT)),
                         reads=[bank_r[bT]], writes=[hT_r[s6][1]])
                t_free.append(bT)
            load_xf(t + 1)
            b2 = yield from acquire(pj_free)
            proj_into(t, C_VA, 512, b2)
            yield
            S.op("act", lambda e: e.activation(out=va[s8][:, :, 0:64], in_=bank(b2).rearrange("p (h d) -> p h d", d=64),
                                               func=AF.Copy, scale=rstd),
                 reads=[bank_r[b2], rstd_r[s6]], writes=[va_r[s8]])
            pj_free.append(b2)
            b3 = yield from acquire(pj_free)
            proj_into(t, C_KB, 256, b3)
            yield
            S.op("act", lambda e: e.activation(out=kbf[:], in_=bank(b3, 0, 128), func=AF.Copy, scale=rstd),
                 reads=[bank_r[b3], rstd_r[s6]], writes=[kbf_r])
            S.op("act", lambda e: e.activation(out=vb[s8][:, :, 0:64], in_=bank(b3, 128, 256).rearrange("p (h d) -> p h d", d=64),
                                               func=AF.Copy, scale=rstd),
                 reads=[bank_r[b3], rstd_r[s6]], writes=[vb_r[s8]])
            pj_free.append(b3)
            b = yield from acquire(pj_free)
            proj_into(t, C_KA, 512, b)
            yield
            S.op("act", lambda e: e.activation(out=kf[:], in_=bank(b), func=AF.Copy, scale=rstd),
                 reads=[bank_r[b], rstd_r[s6]], writes=[kf_r])
            pj_free.append(b)
            yield
            S.op("pool", lambda e: e.tensor_tensor(out=sqk[:, 0:512], in0=kf[:], in1=kf[:], op=ALU.mult), reads=[kf_r], writes=[sqk_r])
            S.op("pool", lambda e: e.tensor_tensor(out=sqk[:, 512:640], in0=kbf[:], in1=kbf[:], op=ALU.mult),
                 reads=[kbf_r], writes=[sqk_r])
            yield
            yield
            S.op("dve", lambda e: e.tensor_reduce(out=ssq[:, 16:26], in_=sqk[:, 0:640].rearrange("p (h d) -> p h d", d=64),
                                                  axis=AX.X, op=ALU.add), reads=[sqk_r], writes=[ssq_r[1]])
            yield
            rstd_from(ssq[:, 16:26], 10, ssq_r[1])
            yield
            yield
            S.op("pool", lambda e: e.tensor_tensor(out=nrmk[:, 0:512].rearrange("p (h d) -> p h d", d=64),
                                                   in0=kf[:].rearrange("p (h d) -> p h d", d=64),
                                                   in1=ssq[:, 16:24].unsqueeze(2).broadcast_to([128, 8, 64]), op=ALU.mult),
                 reads=[kf_r, ssq_r[1]], writes=[nrmk_r])
            S.op("pool", lambda e: e.tensor_tensor(out=nrmk[:, 512:640].rearrange("p (h d) -> p h d", d=64),
                                                   in0=kbf[:].rearrange("p (h d) -> p h d", d=64),
                                                   in1=ssq[:, 24:26].unsqueeze(2).broadcast_to([128, 2, 64]), op=ALU.mult),
                 reads=[kbf_r, ssq_r[1]], writes=[nrmk_r])
            yield
            yield
            bT = yield from acquire(t_free)
            bb = bank_bf(bT)
            transposes_bf(nrmk, nrmk_r, 5, bT)
            yield
            S.op("act", lambda e: e.activation(out=kTa[s8][:].rearrange("p a b -> p (a b)"), in_=bb[:, 0:512], func=AF.Copy,
                                               scale=cab[:, 0:1]),
                 reads=[bank_r[bT], misc_r], writes=[kTa_r[s8]])
            for kv in range(2):
                for dh in range(2):
                    src = bb[kv * 64:(kv + 1) * 64, 512:640]
                    dst = kTb[s8][dh * 64:(dh + 1) * 64, kv, :]
                    sc = cab[kv * 64:(kv + 1) * 64, 1:2]
                    if dh == 0:
                        S.op("act", lambda e, src=src, dst=dst, sc=sc: e.activation(out=dst, in_=src, func=AF.Copy, scale=sc),
                             reads=[bank_r[bT], misc_r], writes=[kTb_r[s8]])
                    else:
                        S.op("dve", lambda e, src=src, dst=dst, sc=sc: e.tensor_scalar(out=dst, in0=src, scalar1=sc, scalar2=None,
                                                                                       op0=ALU.mult),
                             reads=[bank_r[bT], misc_r], writes=[kTb_r[s8]])
            t_free.append(bT)
            done_f.add(t)
            yield

        def att_jobs(t, which):
            base, i, T = tiles[t]
            on, on_r = onorm[t % 2], onorm_r[t % 2]
            qT, qT_r = qT2[t % 2], qT2_r[t % 2]
            if which == "A":
                J, pieces = a_plan(i, T)
                tab, tab_r = UA, ua_r
            else:
                J = [j for j in (i - 1, i, i + 1) if 0 <= j < T]
                t0 = J[0] - i + 1
                pieces = [(t0, t0 + len(J))]
                tab, tab_r = TB, tb_r
            n = len(J)
            slots = [(base + j) % KV_R for j in J]
            br = 0 if which == "A" else 1

            def sinfo(k):
                sbanks = [B_S + 2 * k] + ([B_S + 2 * k + 1] if n > 4 else [])
                return sbanks, (B_S + 2 * k) * 512

            def qk(h, k):
                sbanks, soff = sinfo(k)
                half = h % 2
                qc = h // 2 if which == "A" else 4 + h // 2
                kr = [(kTa_r if which == "A" else kTb_r)[s] for s in slots]

                def fqk(e):
                    ins = None
                    for blk, s in enumerate(slots):
                        if which == "A":
                            lhs = kTa[s][half * 64:(half + 1) * 64, qc, :]
                        else:
                            lhs = kTb[s][half * 64:(half + 1) * 64, h // 4, :]
                        ins = e.matmul(ps[:, soff + blk * 128: soff + (blk + 1) * 128], lhsT=lhs,
                                       rhs=qT[half * 64:(half + 1) * 64, qc, :], start=True, stop=True)
                    return ins
                S.op("pe", fqk, reads=[*kr, qT_r[0], qT_r[1]], writes=[bank_r[b] for b in sbanks])

            def softmax(h, k):
                sbanks, soff = sinfo(k)
                S.op("act", lambda e: e.activation(out=pT[k][:, 0:n * 128], in_=ps[:, soff: soff + n * 128], func=AF.Exp),
                     reads=[bank_r[b] for b in sbanks], writes=[pT_r[k]])
                col = 0
                for (a0, a1) in pieces:
                    w = (a1 - a0) * 128
                    S.op("dve", lambda e, col=col, w=w, a0=a0, a1=a1: e.tensor_tensor(
                        out=pT[k][:, col:col + w], in0=pT[k][:, col:col + w],
                        in1=tab[:, h, a0 * 128:a1 * 128], op=ALU.mult),
                        reads=[pT_r[k], tab_r], writes=[pT_r[k]])
                    col += w

            def pv(h, k):
                vr = [(va_r if which == "A" else vb_r)[s] for s in slots]
                hh = h % 4

                def fpv(e):
                    ins = None
                    for blk, s in enumerate(slots):
                        rhs = va[s][:, h, :] if which == "A" else vb[s][:, h // 4, :]
                        ins = e.matmul(bank(B_O, hh * 65, (hh + 1) * 65), lhsT=pT[k][:, blk * 128:(blk + 1) * 128], rhs=rhs,
                                       start=(blk == 0), stop=(blk == n - 1))
                    return ins
                S.op("pe", fpv, reads=[pT_r[k], *vr], writes=[bank_r[B_O]])
                if hh == 3:
                    g = h // 4
                    ov = bank(B_O, 0, 260).rearrange("p (h d) -> p h d", d=65)
                    rd = rden[:, g * 4:(g + 1) * 4]
                    if which == "A":
                        S.op("dve", lambda e: e.reciprocal(out=rd.unsqueeze(2), in_=ov[:, :, 64:65]),
                             reads=[bank_r[B_O]], writes=[rden_r])
                    else:
                        S.op("dve", lambda e: e.tensor_tensor(out=rd.unsqueeze(2), in0=ov[:, :, 64:65],
                                                              in1=esink[:, g * 4:(g + 1) * 4].unsqueeze(2), op=ALU.add),
                             reads=[bank_r[B_O], misc_r], writes=[rden_r])
                        S.op("dve", lambda e: e.reciprocal(out=rd, in_=rd), reads=[rden_r], writes=[rden_r])
                    S.op("dve", lambda e: e.tensor_tensor(
                        out=on[:, br * 512 + g * 256: br * 512 + (g + 1) * 256].rearrange("p (h d) -> p h d", d=64),
                        in0=ov[:, :, 0:64], in1=rd.unsqueeze(2).broadcast_to([128, 4, 64]), op=ALU.mult),
                        reads=[bank_r[B_O], rden_r], writes=[on_r[br]])

            return [(lambda k, h=h: qk(h, k), lambda k, h=h: softmax(h, k), lambda k, h=h: pv(h, k)) for h in range(8)]

        def Gq(t):
            s6 = t % HT_R
            rstd = stat[:, HT_R + s6:HT_R + s6 + 1]
            for (col, lo) in ((C_QA, 0), (C_QB, 512)):
                b = yield from acquire(pj_free)
                proj_into(t, col, 512, b)
                yield
                S.op("act", lambda e, b=b, lo=lo: e.activation(out=qf[:, lo:lo + 512], in_=bank(b), func=AF.Copy, scale=rstd),
                     reads=[bank_r[b], rstd_r[s6]], writes=[qf_r])
                pj_free.append(b)
            yield
            S.op("pool", lambda e: e.tensor_tensor(out=sq[:], in0=qf[:], in1=qf[:], op=ALU.mult), reads=[qf_r], writes=[sq_r])
            yield
            yield
            S.op("dve", lambda e: e.tensor_reduce(out=ssq[:, 0:16], in_=sq[:].rearrange("p (h d) -> p h d", d=64),
                                                  axis=AX.X, op=ALU.add), reads=[sq_r], writes=[ssq_r[0]])
            yield
            rstd_from(ssq[:, 0:16], 16, ssq_r[0])
            yield
            yield
            S.op("pool", lambda e: e.tensor_tensor(out=nrm[:].rearrange("p (h d) -> p h d", d=64),
                                                   in0=qf[:].rearrange("p (h d) -> p h d", d=64),
                                                   in1=ssq[:, 0:16].unsqueeze(2).broadcast_to([128, 16, 64]), op=ALU.mult),
                 reads=[qf_r, ssq_r[0]], writes=[nrm_r])
            yield
            yield
            yield
            bT = yield from acquire(t_free)
            transposes_bf(nrm, nrm_r, 8, bT)
            yield
            evac_T(qT2[t % 2], qT2_r[t % 2], bT)
            t_free.append(bT)
            done_q.add(t)
            yield

        def at_prologue(t):
            jobs = att_jobs(t, "A")
            jobs[0][0](0)
            jobs[0][1](0)
            jobs[1][0](1)

        def At(t):
            jobs = att_jobs(t, "A") + att_jobs(t, "B")
            nxt = None
            for j in range(16):
                if j + 2 >= 16 and t + 1 < NT and nxt is None:
                    while not ((t + 1) in done_q and (min(t + LEAD, NT - 1)) in done_f):
                        yield
                    nxt = att_jobs(t + 1, "A")[0:2]
                    jobs = jobs + nxt
                if j + 1 < len(jobs):
                    jobs[j + 1][1]((j + 1) % 2)
                jobs[j][2](j % 2)
                if j + 2 < len(jobs):
                    jobs[j + 2][0](j % 2)
                yield

        def Gb(t):
            s6 = t % HT_R
            rstd = stat[:, HT_R + s6:HT_R + s6 + 1]
            rstdh = stat[:, 2 * HT_R + s6:2 * HT_R + s6 + 1]
            on, on_r = onorm[t % 2], onorm_r[t % 2]
            S.op("sp", lambda e: e.dma_start(out=xr[:], in_=x_d[t * 128:(t + 1) * 128, :]), writes=[xr_r], dma=xr_sem)
            for br, col in ((0, C_ZA), (1, C_ZB)):
                b = yield from acquire(pj_free)
                proj_into(t, col, 512, b)
                yield
                th, th_r = tmp[0], tmp_r[0]
                t1, t1_r = tmp[1], tmp_r[1]
                S.op("act", lambda e, b=b, th=th: e.activation(out=th[:], in_=bank(b), func=AF.Tanh, scale=rstdh),
                     reads=[bank_r[b], rstd_r[s6]], writes=[th_r])
                S.op("dve", lambda e, b=b, t1=t1, br=br: e.scalar_tensor_tensor(out=t1[:], in0=bank(b), scalar=rstd,
                                                                                in1=on[:, br * 512:(br + 1) * 512],
                                                                                op0=ALU.mult, op1=ALU.mult),
                     reads=[bank_r[b], rstd_r[s6], on_r[br]], writes=[t1_r])
                pj_free.append(b)
                yield
                S.op("dve", lambda e, th=th, t1=t1, br=br: e.scalar_tensor_tensor(out=ubuf[:, br * 512:(br + 1) * 512], in0=th[:], scalar=1.0,
                                                                                  in1=t1[:], op0=ALU.add, op1=ALU.mult),
                     reads=[th_r, t1_r], writes=[u_r])
                yield
            bT = yield from acquire(t_free)
            transposes_bf(ubuf, u_r, 8, bT)
            yield
            evac_T(uT, uT_r, bT)
            t_free.append(bT)
            yield
            for nh in range(2):
                for br, gcol_, wo, wo_r in ((0, C_GA, woa, woa_r), (1, C_GB, wob, wob_r)):
                    b = yield from acquire(pj_free)
                    proj_into(t, gcol_ + nh * 512, 512, b)
                    yield
                    th, th_r = tmp[0], tmp_r[0]
                    S.op("act", lambda e, b=b, th=th: e.activation(out=th[:], in_=bank(b), func=AF.Tanh, scale=rstdh),
                         reads=[bank_r[b], rstd_r[s6]], writes=[th_r])
                    pj_free.append(b)
                    b2 = yield from acquire(pj_free)

                    def fo(e, br=br, wo=wo, b2=b2, nh=nh):
                        ins = None
                        for c in range(4):
                            ins = e.matmul(bank(b2), lhsT=uT[:, br * 4 + c, :], rhs=wo[:, c, nh * 512:(nh + 1) * 512],
                                           start=(c == 0), stop=(c == 3))
                        return ins
                    S.op("pe", fo, reads=[uT_r[br], wo_r], writes=[bank_r[b2]])
                    yield
                    m1, m1_r = tmp[1 + br], tmp_r[1 + br]
                    S.op("dve", lambda e, th=th, m1=m1, b2=b2: e.scalar_tensor_tensor(out=m1[:], in0=th[:], scalar=1.0, in1=bank(b2),
                                                                                      op0=ALU.add, op1=ALU.mult),
                         reads=[th_r, bank_r[b2]], writes=[m1_r])
                    pj_free.append(b2)
                yield
                S.op("pool", lambda e, nh=nh: e.tensor_tensor(out=mbuf[:, nh * 512:(nh + 1) * 512], in0=tmp[1][:], in1=tmp[2][:], op=ALU.add),
                     reads=[tmp_r[1], tmp_r[2]], writes=[m_r])
                yield
            yield
            bT = yield from acquire(t_free)
            transposes_bf(mbuf, m_r, 8, bT)
            yield
            evac_T(mT, mT_r, bT)
            t_free.append(bT)
            yield
            for nh in range(2):
                b = yield from acquire(pj_free)

                def fw(e, b=b, nh=nh):
                    ins = None
                    for c in range(8):
                        ins = e.matmul(bank(b), lhsT=mT[:, c, :], rhs=wout[:, c, nh * 512:(nh + 1) * 512], start=(c == 0), stop=(c == 7))
                    return ins
                S.op("pe", fw, reads=[mT_r[0], mT_r[1], wout_r], writes=[bank_r[b]])
                yield
                S.op("dve", lambda e, b=b, nh=nh: e.tensor_tensor(out=xr[:, nh * 512:(nh + 1) * 512], in0=bank(b),
                                                                  in1=xr[:, nh * 512:(nh + 1) * 512], op=ALU.add),
                     reads=[bank_r[b], xr_r], writes=[xr_r])
                pj_free.append(b)
            last_store[0] = S.op("sp", lambda e: e.dma_start(out=y_d[t * 128:(t + 1) * 128, :], in_=xr[:]), reads=[xr_r], dma=st_sem)

        def dump(t):
            s6, s8 = t % HT_R, t % KV_R
            items = [("stat", stat[:], F32, [128, 3 * HT_R], []), ("hT", hT[s6][:].rearrange("p a b -> p (a b)"), BF16, [128, 1024], hT_r[s6]),
                     ("qT", qT2[t % 2][:].rearrange("p a b -> p (a b)"), BF16, [128, 1024], qT2_r[t % 2]),
                     ("kTa", kTa[s8][:].rearrange("p a b -> p (a b)"), BF16, [128, 512], [kTa_r[s8]]),
                     ("va", va[s8][:].rearrange("p a b -> p (a b)"), BF16, [128, 520], [va_r[s8]]),
                     ("kTb", kTb[s8][:].rearrange("p a b -> p (a b)"), BF16, [128, 256], [kTb_r[s8]]),
                     ("vb", vb[s8][:].rearrange("p a b -> p (a b)"), BF16, [128, 130], [vb_r[s8]]),
                     ("onorm", onorm[t % 2][:], F32, [128, 1024], onorm_r[t % 2]), ("ubuf", ubuf[:], BF16, [128, 1024], [u_r]),
                     ("mbuf", mbuf[:], BF16, [128, 1024], [m_r])]
            evs = []
            dsem = S.dma_sem("d_dbg")
            for name, ap, dt, shape, rs in items:
                d = nc.dram_tensor("dbg_" + name, shape, dt, kind="ExternalOutput").ap()
                evs.append(S.op("sp", lambda e, d=d, ap=ap: e.dma_start(out=d[:, :], in_=ap), reads=list(rs), dma=dsem))
            return evs[-1]

        last_store = [None]

        def count_steps(mk):
            S.dry = True
            saved = (pj_ctr[0], list(pj_free), list(t_free), set(done_f), set(done_q))
            n = sum(1 for _ in mk())
            pj_ctr[0] = saved[0]
            pj_free[:] = saved[1]
            t_free[:] = saved[2]
            done_f.clear(); done_f.update(saved[3])
            done_q.clear(); done_q.update(saved[4])
            S.dry = False
            return n + 1

        def run_all(makers):
            makers = [m for m in makers if m is not None]
            order = []
            for pri, (mk, span, dry_ok) in enumerate(makers):
                n = count_steps(mk) if dry_ok is True else (17 if dry_ok is False else dry_ok)
                for j in range(n):
                    order.append(((j + 0.5) / n * span, pri))
            order.sort()
            gens = [mk() for mk, _, _ in makers]
            alive = [True] * len(gens)
            trace_seq = []
            for _, pri in order:
                if alive[pri]:
                    trace_seq.append(pri)
                    try:
                        next(gens[pri])
                    except StopIteration:
                        alive[pri] = False
            guard = 0
            while any(alive):
                for pri in range(len(gens)):
                    if alive[pri]:
                        trace_seq.append(10 + pri)
                        try:
                            next(gens[pri])
                        except StopIteration:
                            alive[pri] = False
                guard += 1
                assert guard < 10000, "stream scheduling deadlock"
            if len(makers) == 4 and DEBUG_SEQ:
                print("SEQ", "".join("ABQF"[p] if p < 10 else "abqf"[p - 10] for p in trace_seq))

        LEAD = 4
        for _ in stage_win([0, 1, 2], 5):
            pass
        rest = stage_rest(3)
        stg_i[0] = 0

        def take(n):
            for _ in range(n):
                try:
                    next(rest)
                except StopIteration:
                    return
                yield

        for t in range(min(LEAD, NT)):
            run_all([(lambda t=t: F(t), 1.0, True), (lambda: take(14), 1.0, 15)])
        run_all([(lambda: Gq(0), 1.0, True), (lambda: take(14), 1.0, 15)])
        for _ in rest:
            pass
        at_prologue(0)
        dbg_ev = None
        for t in range(NT + 1):
            run_all([((lambda t=t: At(t)), 0.91, False) if t < NT else None,
                     ((lambda t=t: Gb(t - 1)), 1.0, True) if t >= 1 else None,
                     ((lambda t=t: Gq(t + 1)), 0.68, True) if t + 1 < NT else None,
                     ((lambda t=t: F(t + LEAD)), 0.68, True) if t + LEAD < NT else None])
            if dbg is not None and t - 1 == dbg:
                dbg_ev = dump(dbg)
        S.final_wait("sp", [last_store[0], dbg_ev])
        S.emit()
    return nc


def _shared_inputs(norm_g, w_in, qn_a, kn_a, rpb_a, qn_b, kn_b, sink_b, w_o_a, w_o_b, w_out, t5_table):
    f = lambda a: np.ascontiguousarray(np.asarray(a, dtype=np.float32))
    small = np.zeros((128, 160), np.float32)
    small[:, 0:128] = np.eye(128, dtype=np.float32)
    small[:, 128:136] = f(norm_g)[0].reshape(8, 128).T
    small[:, 136] = np.tile(f(qn_a)[0], 2)
    small[:, 137] = np.tile(f(kn_a)[0], 2)
    small[:, 138] = np.tile(f(qn_b)[0], 2)
    small[:, 139] = np.tile(f(kn_b)[0], 2)
    small[:, 140:148] = f(sink_b)[0][None, :]
    uab, uam = _a_tables(f(rpb_a)[0])
    tbb, tbm = _b_tables(f(t5_table))
    return {"w_in": f(w_in)[0], "w_oa": f(w_o_a)[0], "w_ob": f(w_o_b)[0], "w_out": f(w_out)[0], "small": small,
            "ua_bias": uab, "ua_mask": uam, "tb_bias": tbb, "tb_mask": tbm}


_PROG_CACHE = {}


def run_layer(seqs_per_core, shared, n_cores, dbg=None):
    seq_tiles = tuple(s.shape[0] // 128 for s in seqs_per_core[0])
    if dbg is not None:
        nc = build_program(list(seq_tiles), dbg=dbg)
    else:
        if seq_tiles not in _PROG_CACHE:
            _PROG_CACHE[seq_tiles] = build_program(list(seq_tiles))
        nc = _PROG_CACHE[seq_tiles]
    in_maps = []
    for c in range(n_cores):
        m = dict(shared)
        m["x"] = np.ascontiguousarray(np.concatenate(seqs_per_core[c], axis=0), dtype=np.float32)
        in_maps.append(m)
    res = run_bass_kernel_spmd(nc, in_maps, core_ids=list(range(n_cores)))
    if dbg is not None:
        return res.results
    return [r["y"] for r in res.results]


def kernel(x_prompt, x_sample, norm_g, w_in, qn_a, kn_a, rpb_a, qn_b, kn_b, sink_b, w_o_a, w_o_b, w_out, t5_table):
    x_prompt = np.asarray(x_prompt, dtype=np.float32)
    x_sample = np.asarray(x_sample, dtype=np.float32)
    shared = _shared_inputs(norm_g, w_in, qn_a, kn_a, rpb_a, qn_b, kn_b, sink_b, w_o_a, w_o_b, w_out, t5_table)
    n = 8
    seqs = [[x_prompt[c], x_sample[c]] for c in range(n)]
    ys = run_layer(seqs, shared, n)
    Lp = x_prompt.shape[1]
    y_prompt = np.stack([ys[c][:Lp] for c in range(n)], axis=0)
    y_sample = np.stack([ys[c][Lp:] for c in range(n)], axis=0)
    return (y_prompt.astype(np.float32), y_sample.astype(np.float32))
```

```python
import math
from contextlib import ExitStack

import numpy as np
import concourse.bass as bass
import concourse.mybir as mybir
from concourse.bass_utils import run_bass_kernel_spmd

F32 = mybir.dt.float32
BF16 = mybir.dt.bfloat16
AF = mybir.ActivationFunctionType
ALU = mybir.AluOpType
AX = mybir.AxisListType

D_MODEL = 1024
D_IN = 5376
EPS = 1e-6
NEG = -30000.0
C_QA, C_KA, C_VA, C_ZA, C_QB, C_KB, C_VB, C_ZB, C_GA, C_GB = 0, 512, 1024, 1536, 2048, 2560, 2688, 2816, 3328, 4352
DEBUG_SEQ = False
KV_R = 8
HT_R = 6
N_UA = 9
UA_ORDER = [(-3, False), (-2, False), (-2, True), (-1, False), (0, False), (1, False), (2, True), (2, False), (3, False)]


class Res:
    __slots__ = ("name", "w", "r", "excl")

    def __init__(self, name, excl=False):
        self.name = name
        self.w = None
        self.r = {}
        self.excl = excl


class Sched:
    SAME_ENGINE_SYNC = True

    def __init__(self, nc, stack):
        self.nc = nc
        self.stack = stack
        self.eng = {"pe": nc.tensor, "act": nc.scalar, "dve": nc.vector, "pool": nc.gpsimd, "sp": nc.sync}
        self.ops = {k: [] for k in self.eng}
        self.sems = {}
        self.cnt = {}
        for k in ("pe", "act", "dve", "pool"):
            self.sems[k] = stack.enter_context(nc.semaphore("s_" + k))
            self.cnt[k] = 0
        self.known = {k: {} for k in self.eng}

    def dma_sem(self, name):
        self.sems[name] = self.stack.enter_context(self.nc.semaphore(name))
        self.cnt[name] = 0
        return name

    dry = False

    def op(self, eng, fn, reads=(), writes=(), dma=None, standalone=False):
        if self.dry:
            return None
        deps = {}

        def add(ev):
            if ev is not None and deps.get(ev[0], 0) < ev[1]:
                deps[ev[0]] = ev[1]

        def addall(d):
            for k, v in d.items():
                if deps.get(k, 0) < v:
                    deps[k] = v

        for r in reads:
            add(r.w)
            if r.excl:
                addall(r.r)
        for w in writes:
            add(w.w)
            addall(w.r)
        waits = []
        kn = self.known[eng]
        for k, v in deps.items():
            if k == eng and (eng == "pe" or not self.SAME_ENGINE_SYNC):
                continue
            if kn.get(k, 0) >= v:
                continue
            waits.append((k, v))
            kn[k] = v
        if dma is None:
            self.cnt[eng] += 1
            ev = (eng, self.cnt[eng])
            inc = (eng, 1)
        else:
            self.cnt[dma] += 16
            ev = (dma, self.cnt[dma])
            inc = (dma, 16)
        self.ops[eng].append((fn, waits, inc, standalone))
        for r in reads:
            if r.excl:
                r.w = ev
                r.r = {}
            elif r.r.get(ev[0], 0) < ev[1]:
                r.r[ev[0]] = ev[1]
        for w in writes:
            w.w = ev
            w.r = {}
        return ev

    def final_wait(self, eng, events):
        self.ops[eng].append((None, [e for e in events if e is not None], None, True))

    def emit(self):
        sems = self.sems
        with self.nc.Block() as block:
            for name, deco in (("sp", block.sync), ("act", block.scalar), ("dve", block.vector),
                               ("pool", block.gpsimd), ("pe", block.tensor)):
                ops = self.ops[name]

                def body(e, ops=ops, name=name):
                    for fn, waits, inc, standalone in ops:
                        if fn is None:
                            for k, v in waits:
                                e.wait_ge(sems[k], v)
                            continue
                        attach = None
                        ws = waits
                        if name in ("act", "dve", "pool") and waits and not standalone:
                            attach = waits[-1]
                            ws = waits[:-1]
                        for k, v in ws:
                            e.wait_ge(sems[k], v)
                        ins = fn(e)
                        if attach is not None:
                            ins._wait_ge(sems[attach[0]], attach[1])
                        ins.then_inc(sems[inc[0]], inc[1])

                deco(body)


def _a_tables(rpb):
    p = np.arange(128)
    a, kc = p // 64, p % 64
    q = np.arange(128)
    b, qc = q // 64, q % 64
    cs = np.clip(qc - 8, 0, 48)
    colv = (kc[:, None] >= cs[None, :]) & (kc[:, None] < cs[None, :] + 16)
    dc = np.clip(kc[:, None] - qc[None, :] + 15, 0, 30)
    bias = np.zeros((128, 8, N_UA, 128), np.float32)
    mask = np.zeros((128, 8, N_UA, 128), np.float32)
    for t, (delta, gen) in enumerate(UA_ORDER):
        dr = 2 * delta + a[:, None] - b[None, :]
        rowv = (dr >= -4) & (dr <= 3) if gen else (np.abs(dr) <= 7)
        valid = rowv & colv
        dri = np.clip(dr + 7, 0, 14)
        for h in range(8):
            bias[:, h, t, :] = rpb[h][dri, dc]
        mask[:, :, t, :] = np.where(valid, 0.0, NEG)[:, None, :]
    return bias.reshape(128, -1), mask.reshape(128, -1)


def _t5_bucket_np(rel):
    half, max_exact = 16, 8
    n = np.abs(rel)
    try:
        import jax
        import jax.numpy as jnp
        with jax.default_device(jax.devices("cpu")[0]):
            nn = jnp.asarray(n.astype(np.int32))
            large = max_exact + (jnp.log(jnp.maximum(nn, 1).astype(jnp.float32) / max_exact)
                                 / math.log(128 / max_exact) * (half - max_exact)).astype(jnp.int32)
            large = np.asarray(jnp.minimum(large, half - 1))
    except Exception:
        lg = (np.log(np.maximum(n, 1).astype(np.float32) / np.float32(max_exact)) / np.float32(math.log(128 / max_exact))
              * np.float32(half - max_exact))
        large = np.minimum(max_exact + lg.astype(np.int32), half - 1)
    return np.where(rel > 0, half, 0) + np.where(n < max_exact, n, large)


def _b_tables(t5):
    p = np.arange(128)[:, None]
    q = np.arange(128)[None, :]
    bias = np.zeros((128, 8, 3, 128), np.float32)
    mask = np.zeros((128, 8, 3, 128), np.float32)
    for t, delta in enumerate((-1, 0, 1)):
        rel = 128 * delta + p - q
        bk = _t5_bucket_np(np.clip(rel, -128, 128))
        valid = np.abs(rel) <= 128
        for h in range(8):
            bias[:, h, t, :] = t5[bk, h]
        mask[:, :, t, :] = np.where(valid, 0.0, NEG)[:, None, :]
    return bias.reshape(128, -1), mask.reshape(128, -1)


def a_plan(i, T):
    if T < 6:
        raise ValueError("sequence too short")
    if 2 <= i <= T - 3:
        return list(range(i - 2, i + 3)), [(2, 7)]
    if i == 0:
        return [0, 1, 2, 3], [(4, 6), (7, 9)]
    if i == 1:
        return [0, 1, 2, 3], [(3, 6), (7, 8)]
    if i == T - 2:
        return list(range(T - 4, T)), [(1, 2), (3, 6)]
    return list(range(T - 4, T)), [(0, 2), (3, 5)]


def build_program(seq_tiles, dbg=None):
    NT = sum(seq_tiles)
    nc = bass.Bass("TRN2", target_bir_lowering=False, dynamic_dma_scratch_size=256)

    def dram(name, shape, kind):
        return nc.dram_tensor(name, shape, F32, kind=kind).ap()

    x_d = dram("x", [NT * 128, D_MODEL], "ExternalInput")
    y_d = dram("y", [NT * 128, D_MODEL], "ExternalOutput")
    win_d = dram("w_in", [D_MODEL, D_IN], "ExternalInput")
    woa_d = dram("w_oa", [512, D_MODEL], "ExternalInput")
    wob_d = dram("w_ob", [512, D_MODEL], "ExternalInput")
    wout_d = dram("w_out", [D_MODEL, D_MODEL], "ExternalInput")
    small_d = dram("small", [128, 160], "ExternalInput")
    uab_d = dram("ua_bias", [128, 8 * N_UA * 128], "ExternalInput")
    uam_d = dram("ua_mask", [128, 8 * N_UA * 128], "ExternalInput")
    tbb_d = dram("tb_bias", [128, 8 * 3 * 128], "ExternalInput")
    tbm_d = dram("tb_mask", [128, 8 * 3 * 128], "ExternalInput")

    with ExitStack() as st:
        S = Sched(nc, st)

        def sb(name, shape, dt):
            return st.enter_context(nc.sbuf_tensor("sb_" + name, shape, dt))

        win = sb("win", [128, 8, D_IN], BF16); win_r = [Res(f"win{j}") for j in range(6)]
        woa = sb("woa", [128, 4, D_MODEL], BF16); woa_r = Res("woa")
        wob = sb("wob", [128, 4, D_MODEL], BF16); wob_r = Res("wob")
        wout = sb("wout", [128, 8, D_MODEL], BF16); wout_r = Res("wout")
        UA = sb("UA", [128, 8, N_UA * 128], BF16); ua_r = Res("UA")
        TB = sb("TB", [128, 8, 3 * 128], BF16); tb_r = Res("TB")
        small = sb("small", [128, 160], F32); small_r = Res("small")
        identf = small[:, 0:128]
        gcol = small[:, 128:136]
        cvec = small[:, 136:140]
        sinkv = small[:, 140:148]
        identb = sb("identb", [128, 128], BF16); identb_r = Res("identb")
        misc = sb("misc", [128, 64], F32); misc_r = Res("misc")
        cab = misc[:, 0:2]
        esink = misc[:, 2:10]
        cm05 = misc[:, 16:32]
        xf = [sb(f"xf{i}", [128, D_MODEL], F32) for i in range(1)]
        xf_r = [Res(f"xf{i}") for i in range(1)]
        xf_sem = [S.dma_sem(f"d_xf{i}") for i in range(1)]
        qf = sb("qf", [128, 1024], F32); qf_r = Res("qf")
        xr = sb("xr", [128, D_MODEL], F32); xr_r = Res("xr"); xr_sem = S.dma_sem("d_xr"); st_sem = S.dma_sem("d_st")
        hT = [sb(f"hT{i}", [128, 8, 128], BF16) for i in range(HT_R)]
        hT_r = [(Res(f"hTa{i}"), Res(f"hTb{i}")) for i in range(HT_R)]
        stat = sb("stat", [128, 3 * HT_R], F32)
        ss_r = [Res(f"ss{i}") for i in range(HT_R)]
        rstd_r = [Res(f"rstd{i}") for i in range(HT_R)]
        kTa = [sb(f"kTa{i}", [128, 4, 128], BF16) for i in range(KV_R)]
        va = [sb(f"va{i}", [128, 8, 65], BF16) for i in range(KV_R)]
        kTb = [sb(f"kTb{i}", [128, 2, 128], BF16) for i in range(KV_R)]
        vb = [sb(f"vb{i}", [128, 2, 65], BF16) for i in range(KV_R)]
        kTa_r = [Res(f"kTa{i}") for i in range(KV_R)]
        va_r = [Res(f"va{i}") for i in range(KV_R)]
        kTb_r = [Res(f"kTb{i}") for i in range(KV_R)]
        vb_r = [Res(f"vb{i}") for i in range(KV_R)]
        sq = sb("sq", [128, 1024], F32); sq_r = Res("sq")
        ssq = sb("ssq", [128, 32], F32); ssq_r = (Res("ssq_q"), Res("ssq_k"))
        nrm = sb("nrm", [128, 1024], BF16); nrm_r = Res("nrm")
        nrmk = sb("nrmk", [128, 640], BF16); nrmk_r = Res("nrmk")
        kbf = sb("kbf", [128, 128], F32); kbf_r = Res("kbf")
        kf = sb("kf", [128, 512], F32); kf_r = Res("kf")
        sqk = sb("sqk", [128, 640], BF16); sqk_r = Res("sqk")
        junk = sqk[:].bitcast(mybir.dt.int8)[:, 0:1024]
        qT2 = [sb(f"qT{i}", [128, 8, 128], BF16) for i in range(2)]; qT2_r = [(Res(f"qTa{i}"), Res(f"qTb{i}")) for i in range(2)]
        pT = [sb(f"pT{i}", [128, 640], BF16) for i in range(2)]; pT_r = [Res(f"pT{i}") for i in range(2)]
        onorm = [sb(f"onorm{i}", [128, 1024], F32) for i in range(2)]
        onorm_r = [[Res(f"onormA{i}"), Res(f"onormB{i}")] for i in range(2)]
        rden = sb("rden", [128, 8], F32); rden_r = Res("rden")
        tmp = [sb(f"tmp{i}", [128, 512], F32) for i in range(3)]; tmp_r = [Res(f"tmp{i}") for i in range(3)]
        ubuf = sb("ubuf", [128, 1024], BF16); u_r = Res("u")
        uT = sb("uT", [128, 8, 128], BF16); uT_r = (Res("uTa"), Res("uTb"))
        mbuf, m_r = ubuf, u_r
        mT, mT_r = uT, uT_r
        ps = st.enter_context(nc.psum_tensor("ps", [128, 4096], F32))
        bank_r = [Res(f"bank{b}", excl=True) for b in range(8)]
        B_S, B_O, B_PJ0, B_PJ1, B_T = 0, 4, 5, 6, 7

        def bank(b, lo=0, hi=512):
            return ps[:, b * 512 + lo: b * 512 + hi]

        def bank_bf(b):
            return ps[:, b * 512:(b + 1) * 512].bitcast(BF16)

        pj_ctr = [0]

        def next_pj():
            b = B_PJ0 + (pj_ctr[0] % 2)
            pj_ctr[0] += 1
            return b

        d_small = S.dma_sem("d_small")
        S.op("sp", lambda e: e.dma_start(out=small[:], in_=small_d[:, :]), writes=[small_r], dma=d_small)
        S.op("sp", lambda e: e.dma_start(out=xf[0][:], in_=x_d[0:128, :]), writes=[xf_r[0]], dma=xf_sem[0])
        S.op("dve", lambda e: e.tensor_copy(out=identb[:], in_=identf), reads=[small_r], writes=[identb_r])
        S.op("pool", lambda e: e.memset(misc[:], -0.5), writes=[misc_r])
        S.op("dve", lambda e: e.scalar_tensor_tensor(out=cab[:, 0:1], in0=cvec[:, 0:1], scalar=0.125, in1=cvec[:, 1:2],
                                                      op0=ALU.mult, op1=ALU.mult), reads=[small_r], writes=[misc_r])
        S.op("dve", lambda e: e.scalar_tensor_tensor(out=cab[:, 1:2], in0=cvec[:, 2:3], scalar=0.125, in1=cvec[:, 3:4],
                                                      op0=ALU.mult, op1=ALU.mult), reads=[small_r], writes=[misc_r])
        S.op("act", lambda e: e.activation(out=esink, in_=sinkv, func=AF.Exp), reads=[small_r], writes=[misc_r])
        for i in range(KV_R):
            S.op("pool", lambda e, i=i: e.memset(va[i][:, :, 64:65], 1.0), writes=[va_r[i]])
            S.op("pool", lambda e, i=i: e.memset(vb[i][:, :, 64:65], 1.0), writes=[vb_r[i]])

        stg_all = [(xr, [xr_r], xr_sem), (onorm[0], onorm_r[0], S.dma_sem("d_on0")), (onorm[1], onorm_r[1], S.dma_sem("d_on1")),
                   (qf, [qf_r], S.dma_sem("d_qf")), (sq, [sq_r], S.dma_sem("d_sq"))]
        stg_i = [0]
        cast_i = [0]

        def stage_load(src_ap, n, nslots):
            k = stg_i[0] % nslots
            stg_i[0] += 1
            t, r, sem = stg_all[k]
            S.op("sp", lambda e: e.dma_start(out=t[:, 0:n], in_=src_ap), writes=list(r), dma=sem)
            return t, list(r)

        def cast_scaled(dst_ap, dst_r, src_t, src_r, n, scale, scale_r=()):
            which = ("dve", "pool", "act")[cast_i[0] % 3]
            cast_i[0] += 1
            if which == "act":
                S.op("act", lambda e: e.activation(out=dst_ap, in_=src_t[:, 0:n], func=AF.Copy, scale=scale),
                     reads=[*src_r, *scale_r], writes=[dst_r])
            else:
                S.op(which, lambda e: e.tensor_scalar(out=dst_ap, in0=src_t[:, 0:n], scalar1=scale, scalar2=0.0,
                                                       op0=ALU.mult, op1=ALU.add),
                     reads=[*src_r, *scale_r], writes=[dst_r])

        def stage_win(pieces, nslots):
            for c in range(8):
                for j in pieces:
                    lo = j * 1024
                    n = min(1024, D_IN - lo)
                    t, r = stage_load(win_d[c * 128:(c + 1) * 128, lo:lo + n], n, nslots)
                    cast_scaled(win[:, c, lo:lo + n], win_r[j], t, r, n, gcol[:, c:c + 1], [small_r])
                    yield

        UAf = UA[:].rearrange("p h n -> p (h n)")
        TBf = TB[:].rearrange("p h n -> p (h n)")

        def stage_tables(nslots):
            for (dst, dst_r, bd, md, tot) in ((UAf, ua_r, uab_d, uam_d, 8 * N_UA * 128), (TBf, tb_r, tbb_d, tbm_d, 8 * 3 * 128)):
                for lo in range(0, tot, 1024):
                    n = min(1024, tot - lo)
                    t1_, r1_ = stage_load(bd[:, lo:lo + n], n, nslots)
                    t2_, r2_ = stage_load(md[:, lo:lo + n], n, nslots)
                    which = ("dve", "pool")[(lo // 1024) % 2]
                    S.op(which, lambda e, n=n, t1_=t1_, t2_=t2_: e.tensor_tensor(
                        out=t1_[:, 0:n], in0=t1_[:, 0:n], in1=t2_[:, 0:n], op=ALU.add),
                        reads=[*r2_], writes=[*r1_])
                    S.op("act", lambda e, dst=dst, lo=lo, n=n, t1_=t1_: e.activation(out=dst[:, lo:lo + n], in_=t1_[:, 0:n], func=AF.Exp),
                         reads=[*r1_], writes=[dst_r])
                    yield

        def stage_rest(nslots):
            yield from stage_tables(nslots)
            yield from stage_win([3, 4, 5], nslots)
            for c in range(4):
                t, r = stage_load(woa_d[c * 128:(c + 1) * 128, :], 1024, nslots)
                cast_scaled(woa[:, c, :], woa_r, t, r, 1024, 0.5)
                yield
                t, r = stage_load(wob_d[c * 128:(c + 1) * 128, :], 1024, nslots)
                cast_scaled(wob[:, c, :], wob_r, t, r, 1024, 0.5)
                yield
            for c in range(8):
                t, r = stage_load(wout_d[c * 128:(c + 1) * 128, :], 1024, nslots)
                cast_scaled(wout[:, c, :], wout_r, t, r, 1024, 0.5)
                yield

        tiles = []
        base = 0
        for T in seq_tiles:
            for i in range(T):
                tiles.append((base, i, T))
            base += T

        def load_xf(t):
            if t < NT:
                s2 = 0
                S.op("sp", lambda e: e.dma_start(out=xf[s2][:], in_=x_d[t * 128:(t + 1) * 128, :]), writes=[xf_r[s2]], dma=xf_sem[s2])

        def proj_chunk(t, col, n, b=None):
            s6 = t % HT_R
            if b is None:
                b = next_pj()

            def f(e):
                ins = None
                for c in range(8):
                    ins = e.matmul(bank(b, 0, n), lhsT=hT[s6][:, c, :], rhs=win[:, c, col:col + n], start=(c == 0), stop=(c == 7))
                return ins
            S.op("pe", f, reads=[hT_r[s6][0], hT_r[s6][1]] + [win_r[j] for j in range(col // 1024, (col + n - 1) // 1024 + 1)],
                 writes=[bank_r[b]])
            return b

        def rstd_from(ssq_ap, nheads, r):
            S.op("pool", lambda e: e.tensor_scalar(out=ssq_ap, in0=ssq_ap, scalar1=1.0 / 64, scalar2=EPS, op0=ALU.mult, op1=ALU.add),
                 reads=[r], writes=[r])
            S.op("pool", lambda e: e.tensor_tensor(out=ssq_ap, in0=ssq_ap, in1=cm05[:, 0:nheads], op=ALU.pow),
                 reads=[r, misc_r], writes=[r])

        def transposes_bf(src, src_r, nblk, bT):
            bb = bank_bf(bT)

            def f(e):
                ins = None
                for c in range(nblk):
                    ins = e.transpose(out=bb[:, c * 128:(c + 1) * 128], in_=src[:, c * 128:(c + 1) * 128], identity=identb[:])
                return ins
            S.op("pe", f, reads=[src_r, identb_r], writes=[bank_r[bT]])
            return bb

        def evac_T(dst, dst_r, bT):
            bb = bank_bf(bT)
            S.op("dve", lambda e: e.tensor_copy(out=dst[:].rearrange("p a b -> p (a b)"), in_=bb[:, 0:1024]),
                 reads=[bank_r[bT]], writes=[dst_r[0], dst_r[1]])

        pj_free = [B_PJ0, B_PJ1]
        t_free = [B_T]
        done_f = set()
        done_q = set()

        def acquire(pool):
            while not pool:
                yield
            return pool.pop(0)

        def proj_into(t, col, n, b):
            proj_chunk(t, col, n, b=b)

        def F(t):
            s2, s6, s8 = 0, t % HT_R, t % KV_R
            ssc = stat[:, s6:s6 + 1]
            rstd = stat[:, HT_R + s6:HT_R + s6 + 1]
            rstdh = stat[:, 2 * HT_R + s6:2 * HT_R + s6 + 1]
            S.op("act", lambda e: e.activation(out=junk, in_=xf[s2][:], func=AF.Square, accum_out=ssc),
                 reads=[xf_r[s2]], writes=[ss_r[s6], sqk_r], standalone=True)
            yield
            S.op("pool", lambda e: e.tensor_scalar(out=rstd, in0=ssc, scalar1=1.0 / D_MODEL, scalar2=EPS,
                                                   op0=ALU.mult, op1=ALU.add), reads=[ss_r[s6]], writes=[rstd_r[s6]])
            S.op("pool", lambda e: e.tensor_tensor(out=rstd, in0=rstd, in1=cm05[:, 0:1], op=ALU.pow),
                 reads=[rstd_r[s6], misc_r], writes=[rstd_r[s6]])
            S.op("pool", lambda e: e.tensor_scalar(out=rstdh, in0=rstd, scalar1=0.5, scalar2=0.0,
                                                   op0=ALU.mult, op1=ALU.add), reads=[rstd_r[s6]], writes=[rstd_r[s6]])
            for half in range(2):
                bT = yield from acquire(t_free)

                def f(e, half=half, bT=bT):
                    ins = None
                    for c in range(4):
                        cc = half * 4 + c
                        ins = e.transpose(out=bank(bT, c * 128, (c + 1) * 128), in_=xf[s2][:, cc * 128:(cc + 1) * 128], identity=identf)
                    return ins
                S.op("pe", f, reads=[xf_r[s2], small_r], writes=[bank_r[bT]])
                yield
                dst = hT[s6][:, half * 4:(half + 1) * 4, :].rearrange("p a b -> p (a b)")
                if half == 0:
                    S.op("act", lambda e, dst=dst, bT=bT: e.activation(out=dst, in_=bank(bT), func=AF.Copy),
                         reads=[bank_r[bT]], writes=[hT_r[s6][0]])
                else:
                    S.op("dve", lambda e, dst=dst, bT=bT: e.tensor_copy(out=dst, in_=bank(bT)),
                         reads=[bank_r[bT]], writes=[hT_r[s6][1]])
                t_free.append(bT)
            load_xf(t + 1)
            b2 = yield from acquire(pj_free)
            proj_into(t, C_VA, 512, b2)
            yield
            S.op("act", lambda e: e.activation(out=va[s8][:, :, 0:64], in_=bank(b2).rearrange("p (h d) -> p h d", d=64),
                                               func=AF.Copy, scale=rstd),
                 reads=[bank_r[b2], rstd_r[s6]], writes=[va_r[s8]])
            pj_free.append(b2)
            b3 = yield from acquire(pj_free)
            proj_into(t, C_KB, 256, b3)
            yield
            S.op("act", lambda e: e.activation(out=kbf[:], in_=bank(b3, 0, 128), func=AF.Copy, scale=rstd),
                 reads=[bank_r[b3], rstd_r[s6]], writes=[kbf_r])
            S.op("act", lambda e: e.activation(out=vb[s8][:, :, 0:64], in_=bank(b3, 128, 256).rearrange("p (h d) -> p h d", d=64),
                                               func=AF.Copy, scale=rstd),
                 reads=[bank_r[b3], rstd_r[s6]], writes=[vb_r[s8]])
            pj_free.append(b3)
            b = yield from acquire(pj_free)
            proj_into(t, C_KA, 512, b)
            yield
            S.op("act", lambda e: e.activation(out=kf[:], in_=bank(b), func=AF.Copy, scale=rstd),
                 reads=[bank_r[b], rstd_r[s6]], writes=[kf_r])
            pj_free.append(b)
            yield
            S.op("pool", lambda e: e.tensor_tensor(out=sqk[:, 0:512], in0=kf[:], in1=kf[:], op=ALU.mult), reads=[kf_r], writes=[sqk_r])
            S.op("pool", lambda e: e.tensor_tensor(out=sqk[:, 512:640], in0=kbf[:], in1=kbf[:], op=ALU.mult),
                 reads=[kbf_r], writes=[sqk_r])
            yield
            yield
            S.op("dve", lambda e: e.tensor_reduce(out=ssq[:, 16:26], in_=sqk[:, 0:640].rearrange("p (h d) -> p h d", d=64),
                                                  axis=AX.X, op=ALU.add), reads=[sqk_r], writes=[ssq_r[1]])
            yield
            rstd_from(ssq[:, 16:26], 10, ssq_r[1])
            yield
            yield
            S.op("pool", lambda e: e.tensor_tensor(out=nrmk[:, 0:512].rearrange("p (h d) -> p h d", d=64),
                                                   in0=kf[:].rearrange("p (h d) -> p h d", d=64),
                                                   in1=ssq[:, 16:24].unsqueeze(2).broadcast_to([128, 8, 64]), op=ALU.mult),
                 reads=[kf_r, ssq_r[1]], writes=[nrmk_r])
            S.op("pool", lambda e: e.tensor_tensor(out=nrmk[:, 512:640].rearrange("p (h d) -> p h d", d=64),
                                                   in0=kbf[:].rearrange("p (h d) -> p h d", d=64),
                                                   in1=ssq[:, 24:26].unsqueeze(2).broadcast_to([128, 2, 64]), op=ALU.mult),
                 reads=[kbf_r, ssq_r[1]], writes=[nrmk_r])
            yield
            yield
            bT = yield from acquire(t_free)
            bb = bank_bf(bT)
            transposes_bf(nrmk, nrmk_r, 5, bT)
            yield
            S.op("act", lambda e: e.activation(out=kTa[s8][:].rearrange("p a b -> p (a b)"), in_=bb[:, 0:512], func=AF.Copy,
                                               scale=cab[:, 0:1]),
                 reads=[bank_r[bT], misc_r], writes=[kTa_r[s8]])
            for kv in range(2):
                for dh in range(2):
                    src = bb[kv * 64:(kv + 1) * 64, 512:640]
                    dst = kTb[s8][dh * 64:(dh + 1) * 64, kv, :]
                    sc = cab[kv * 64:(kv + 1) * 64, 1:2]
                    if dh == 0:
                        S.op("act", lambda e, src=src, dst=dst, sc=sc: e.activation(out=dst, in_=src, func=AF.Copy, scale=sc),
                             reads=[bank_r[bT], misc_r], writes=[kTb_r[s8]])
                    else:
                        S.op("dve", lambda e, src=src, dst=dst, sc=sc: e.tensor_scalar(out=dst, in0=src, scalar1=sc, scalar2=None,
                                                                                       op0=ALU.mult),
                             reads=[bank_r[bT], misc_r], writes=[kTb_r[s8]])
            t_free.append(bT)
            done_f.add(t)
            yield

        def att_jobs(t, which):
            base, i, T = tiles[t]
            on, on_r = onorm[t % 2], onorm_r[t % 2]
            qT, qT_r = qT2[t % 2], qT2_r[t % 2]
            if which == "A":
                J, pieces = a_plan(i, T)
                tab, tab_r = UA, ua_r
            else:
                J = [j for j in (i - 1, i, i + 1) if 0 <= j < T]
                t0 = J[0] - i + 1
                pieces = [(t0, t0 + len(J))]
                tab, tab_r = TB, tb_r
            n = len(J)
            slots = [(base + j) % KV_R for j in J]
            br = 0 if which == "A" else 1

            def sinfo(k):
                sbanks = [B_S + 2 * k] + ([B_S + 2 * k + 1] if n > 4 else [])
                return sbanks, (B_S + 2 * k) * 512

            def qk(h, k):
                sbanks, soff = sinfo(k)
                half = h % 2
                qc = h // 2 if which == "A" else 4 + h // 2
                kr = [(kTa_r if which == "A" else kTb_r)[s] for s in slots]

                def fqk(e):
                    ins = None
                    for blk, s in enumerate(slots):
                        if which == "A":
                            lhs = kTa[s][half * 64:(half + 1) * 64, qc, :]
                        else:
                            lhs = kTb[s][half * 64:(half + 1) * 64, h // 4, :]
                        ins = e.matmul(ps[:, soff + blk * 128: soff + (blk + 1) * 128], lhsT=lhs,
                                       rhs=qT[half * 64:(half + 1) * 64, qc, :], start=True, stop=True)
                    return ins
                S.op("pe", fqk, reads=[*kr, qT_r[0], qT_r[1]], writes=[bank_r[b] for b in sbanks])

            def softmax(h, k):
                sbanks, soff = sinfo(k)
                S.op("act", lambda e: e.activation(out=pT[k][:, 0:n * 128], in_=ps[:, soff: soff + n * 128], func=AF.Exp),
                     reads=[bank_r[b] for b in sbanks], writes=[pT_r[k]])
                col = 0
                for (a0, a1) in pieces:
                    w = (a1 - a0) * 128
                    S.op("dve", lambda e, col=col, w=w, a0=a0, a1=a1: e.tensor_tensor(
                        out=pT[k][:, col:col + w], in0=pT[k][:, col:col + w],
                        in1=tab[:, h, a0 * 128:a1 * 128], op=ALU.mult),
                        reads=[pT_r[k], tab_r], writes=[pT_r[k]])
                    col += w

            def pv(h, k):
                vr = [(va_r if which == "A" else vb_r)[s] for s in slots]
                hh = h % 4

                def fpv(e):
                    ins = None
                    for blk, s in enumerate(slots):
                        rhs = va[s][:, h, :] if which == "A" else vb[s][:, h // 4, :]
                        ins = e.matmul(bank(B_O, hh * 65, (hh + 1) * 65), lhsT=pT[k][:, blk * 128:(blk + 1) * 128], rhs=rhs,
                                       start=(blk == 0), stop=(blk == n - 1))
                    return ins
                S.op("pe", fpv, reads=[pT_r[k], *vr], writes=[bank_r[B_O]])
                if hh == 3:
                    g = h // 4
                    ov = bank(B_O, 0, 260).rearrange("p (h d) -> p h d", d=65)
                    rd = rden[:, g * 4:(g + 1) * 4]
                    if which == "A":
                        S.op("dve", lambda e: e.reciprocal(out=rd.unsqueeze(2), in_=ov[:, :, 64:65]),
                             reads=[bank_r[B_O]], writes=[rden_r])
                    else:
                        S.op("dve", lambda e: e.tensor_tensor(out=rd.unsqueeze(2), in0=ov[:, :, 64:65],
                                                              in1=esink[:, g * 4:(g + 1) * 4].unsqueeze(2), op=ALU.add),
                             reads=[bank_r[B_O], misc_r], writes=[rden_r])
                        S.op("dve", lambda e: e.reciprocal(out=rd, in_=rd), reads=[rden_r], writes=[rden_r])
                    S.op("dve", lambda e: e.tensor_tensor(
                        out=on[:, br * 512 + g * 256: br * 512 + (g + 1) * 256].rearrange("p (h d) -> p h d", d=64),
                        in0=ov[:, :, 0:64], in1=rd.unsqueeze(2).broadcast_to([128, 4, 64]), op=ALU.mult),
                        reads=[bank_r[B_O], rden_r], writes=[on_r[br]])

            return [(lambda k, h=h: qk(h, k), lambda k, h=h: softmax(h, k), lambda k, h=h: pv(h, k)) for h in range(8)]

        def Gq(t):
            s6 = t % HT_R
            rstd = stat[:, HT_R + s6:HT_R + s6 + 1]
            for (col, lo) in ((C_QA, 0), (C_QB, 512)):
                b = yield from acquire(pj_free)
                proj_into(t, col, 512, b)
                yield
                S.op("act", lambda e, b=b, lo=lo: e.activation(out=qf[:, lo:lo + 512], in_=bank(b), func=AF.Copy, scale=rstd),
                     reads=[bank_r[b], rstd_r[s6]], writes=[qf_r])
                pj_free.append(b)
            yield
            S.op("pool", lambda e: e.tensor_tensor(out=sq[:], in0=qf[:], in1=qf[:], op=ALU.mult), reads=[qf_r], writes=[sq_r])
            yield
            yield
            S.op("dve", lambda e: e.tensor_reduce(out=ssq[:, 0:16], in_=sq[:].rearrange("p (h d) -> p h d", d=64),
                                                  axis=AX.X, op=ALU.add), reads=[sq_r], writes=[ssq_r[0]])
            yield
            rstd_from(ssq[:, 0:16], 16, ssq_r[0])
            yield
            yield
            S.op("pool", lambda e: e.tensor_tensor(out=nrm[:].rearrange("p (h d) -> p h d", d=64),
                                                   in0=qf[:].rearrange("p (h d) -> p h d", d=64),
                                                   in1=ssq[:, 0:16].unsqueeze(2).broadcast_to([128, 16, 64]), op=ALU.mult),
                 reads=[qf_r, ssq_r[0]], writes=[nrm_r])
            yield
            yield
            yield
            bT = yield from acquire(t_free)
            transposes_bf(nrm, nrm_r, 8, bT)
            yield
            evac_T(qT2[t % 2], qT2_r[t % 2], bT)
            t_free.append(bT)
            done_q.add(t)
            yield

        def at_prologue(t):
            jobs = att_jobs(t, "A")
            jobs[0][0](0)
            jobs[0][1](0)
            jobs[1][0](1)

        def At(t):
            jobs = att_jobs(t, "A") + att_jobs(t, "B")
            nxt = None
            for j in range(16):
                if j + 2 >= 16 and t + 1 < NT and nxt is None:
                    while not ((t + 1) in done_q and (min(t + LEAD, NT - 1)) in done_f):
                        yield
                    nxt = att_jobs(t + 1, "A")[0:2]
                    jobs = jobs + nxt
                if j + 1 < len(jobs):
                    jobs[j + 1][1]((j + 1) % 2)
                jobs[j][2](j % 2)
                if j + 2 < len(jobs):
                    jobs[j + 2][0](j % 2)
                yield

        def Gb(t):
            s6 = t % HT_R
            rstd = stat[:, HT_R + s6:HT_R + s6 + 1]
            rstdh = stat[:, 2 * HT_R + s6:2 * HT_R + s6 + 1]
            on, on_r = onorm[t % 2], onorm_r[t % 2]
            S.op("sp", lambda e: e.dma_start(out=xr[:], in_=x_d[t * 128:(t + 1) * 128, :]), writes=[xr_r], dma=xr_sem)
            for br, col in ((0, C_ZA), (1, C_ZB)):
                b = yield from acquire(pj_free)
                proj_into(t, col, 512, b)
                yield
                th, th_r = tmp[0], tmp_r[0]
                t1, t1_r = tmp[1], tmp_r[1]
                S.op("act", lambda e, b=b, th=th: e.activation(out=th[:], in_=bank(b), func=AF.Tanh, scale=rstdh),
                     reads=[bank_r[b], rstd_r[s6]], writes=[th_r])
                S.op("dve", lambda e, b=b, t1=t1, br=br: e.scalar_tensor_tensor(out=t1[:], in0=bank(b), scalar=rstd,
                                                                                in1=on[:, br * 512:(br + 1) * 512],
                                                                                op0=ALU.mult, op1=ALU.mult),
                     reads=[bank_r[b], rstd_r[s6], on_r[br]], writes=[t1_r])
                pj_free.append(b)
                yield
                S.op("dve", lambda e, th=th, t1=t1, br=br: e.scalar_tensor_tensor(out=ubuf[:, br * 512:(br + 1) * 512], in0=th[:], scalar=1.0,
                                                                                  in1=t1[:], op0=ALU.add, op1=ALU.mult),
                     reads=[th_r, t1_r], writes=[u_r])
                yield
            bT = yield from acquire(t_free)
            transposes_bf(ubuf, u_r, 8, bT)
            yield
            evac_T(uT, uT_r, bT)
            t_free.append(bT)
            yield
            for nh in range(2):
                for br, gcol_, wo, wo_r in ((0, C_GA, woa, woa_r), (1, C_GB, wob, wob_r)):
                    b = yield from acquire(pj_free)
                    proj_into(t, gcol_ + nh * 512, 512, b)
                    yield
                    th, th_r = tmp[0], tmp_r[0]
                    S.op("act", lambda e, b=b, th=th: e.activation(out=th[:], in_=bank(b), func=AF.Tanh, scale=rstdh),
                         reads=[bank_r[b], rstd_r[s6]], writes=[th_r])
                    pj_free.append(b)
                    b2 = yield from acquire(pj_free)

                    def fo(e, br=br, wo=wo, b2=b2, nh=nh):
                        ins = None
                        for c in range(4):
                            ins = e.matmul(bank(b2), lhsT=uT[:, br * 4 + c, :], rhs=wo[:, c, nh * 512:(nh + 1) * 512],
                                           start=(c == 0), stop=(c == 3))
                        return ins
                    S.op("pe", fo, reads=[uT_r[br], wo_r], writes=[bank_r[b2]])
                    yield
                    m1, m1_r = tmp[1 + br], tmp_r[1 + br]
                    S.op("dve", lambda e, th=th, m1=m1, b2=b2: e.scalar_tensor_tensor(out=m1[:], in0=th[:], scalar=1.0, in1=bank(b2),
                                                                                      op0=ALU.add, op1=ALU.mult),
                         reads=[th_r, bank_r[b2]], writes=[m1_r])
                    pj_free.append(b2)
                yield
                S.op("pool", lambda e, nh=nh: e.tensor_tensor(out=mbuf[:, nh * 512:(nh + 1) * 512], in0=tmp[1][:], in1=tmp[2][:], op=ALU.add),
                     reads=[tmp_r[1], tmp_r[2]], writes=[m_r])
                yield
            yield
            bT = yield from acquire(t_free)
            transposes_bf(mbuf, m_r, 8, bT)
            yield
            evac_T(mT, mT_r, bT)
            t_free.append(bT)
            yield
            for nh in range(2):
                b = yield from acquire(pj_free)

                def fw(e, b=b, nh=nh):
                    ins = None
                    for c in range(8):
                        ins = e.matmul(bank(b), lhsT=mT[:, c, :], rhs=wout[:, c, nh * 512:(nh + 1) * 512], start=(c == 0), stop=(c == 7))
                    return ins
                S.op("pe", fw, reads=[mT_r[0], mT_r[1], wout_r], writes=[bank_r[b]])
                yield
                S.op("dve", lambda e, b=b, nh=nh: e.tensor_tensor(out=xr[:, nh * 512:(nh + 1) * 512], in0=bank(b),
                                                                  in1=xr[:, nh * 512:(nh + 1) * 512], op=ALU.add),
                     reads=[bank_r[b], xr_r], writes=[xr_r])
                pj_free.append(b)
            last_store[0] = S.op("sp", lambda e: e.dma_start(out=y_d[t * 128:(t + 1) * 128, :], in_=xr[:]), reads=[xr_r], dma=st_sem)

        def dump(t):
            s6, s8 = t % HT_R, t % KV_R
            items = [("stat", stat[:], F32, [128, 3 * HT_R], []), ("hT", hT[s6][:].rearrange("p a b -> p (a b)"), BF16, [128, 1024], hT_r[s6]),
                     ("qT", qT2[t % 2][:].rearrange("p a b -> p (a b)"), BF16, [128, 1024], qT2_r[t % 2]),
                     ("kTa", kTa[s8][:].rearrange("p a b -> p (a b)"), BF16, [128, 512], [kTa_r[s8]]),
                     ("va", va[s8][:].rearrange("p a b -> p (a b)"), BF16, [128, 520], [va_r[s8]]),
                     ("kTb", kTb[s8][:].rearrange("p a b -> p (a b)"), BF16, [128, 256], [kTb_r[s8]]),
                     ("vb", vb[s8][:].rearrange("p a b -> p (a b)"), BF16, [128, 130], [vb_r[s8]]),
                     ("onorm", onorm[t % 2][:], F32, [128, 1024], onorm_r[t % 2]), ("ubuf", ubuf[:], BF16, [128, 1024], [u_r]),
                     ("mbuf", mbuf[:], BF16, [128, 1024], [m_r])]
            evs = []
            dsem = S.dma_sem("d_dbg")
            for name, ap, dt, shape, rs in items:
                d = nc.dram_tensor("dbg_" + name, shape, dt, kind="ExternalOutput").ap()
                evs.append(S.op("sp", lambda e, d=d, ap=ap: e.dma_start(out=d[:, :], in_=ap), reads=list(rs), dma=dsem))
            return evs[-1]

        last_store = [None]

        def count_steps(mk):
            S.dry = True
            saved = (pj_ctr[0], list(pj_free), list(t_free), set(done_f), set(done_q))
            n = sum(1 for _ in mk())
            pj_ctr[0] = saved[0]
            pj_free[:] = saved[1]
            t_free[:] = saved[2]
            done_f.clear(); done_f.update(saved[3])
            done_q.clear(); done_q.update(saved[4])
            S.dry = False
            return n + 1

        def run_all(makers):
            makers = [m for m in makers if m is not None]
            order = []
            for pri, (mk, span, dry_ok) in enumerate(makers):
                n = count_steps(mk) if dry_ok is True else (17 if dry_ok is False else dry_ok)
                for j in range(n):
                    order.append(((j + 0.5) / n * span, pri))
            order.sort()
            gens = [mk() for mk, _, _ in makers]
            alive = [True] * len(gens)
            trace_seq = []
            for _, pri in order:
                if alive[pri]:
                    trace_seq.append(pri)
                    try:
                        next(gens[pri])
                    except StopIteration:
                        alive[pri] = False
            guard = 0
            while any(alive):
                for pri in range(len(gens)):
                    if alive[pri]:
                        trace_seq.append(10 + pri)
                        try:
                            next(gens[pri])
                        except StopIteration:
                            alive[pri] = False
                guard += 1
                assert guard < 10000, "stream scheduling deadlock"
            if len(makers) == 4 and DEBUG_SEQ:
                print("SEQ", "".join("ABQF"[p] if p < 10 else "abqf"[p - 10] for p in trace_seq))

        LEAD = 4
        for _ in stage_win([0, 1, 2], 5):
            pass
        rest = stage_rest(3)
        stg_i[0] = 0

        def take(n):
            for _ in range(n):
                try:
                    next(rest)
                except StopIteration:
                    return
                yield

        for t in range(min(LEAD, NT)):
            run_all([(lambda t=t: F(t), 1.0, True), (lambda: take(14), 1.0, 15)])
        run_all([(lambda: Gq(0), 1.0, True), (lambda: take(14), 1.0, 15)])
        for _ in rest:
            pass
        at_prologue(0)
        dbg_ev = None
        for t in range(NT + 1):
            run_all([((lambda t=t: At(t)), 0.86, False) if t < NT else None,
                     ((lambda t=t: Gb(t - 1)), 1.0, True) if t >= 1 else None,
                     ((lambda t=t: Gq(t + 1)), 0.68, True) if t + 1 < NT else None,
                     ((lambda t=t: F(t + LEAD)), 0.68, True) if t + LEAD < NT else None])
            if dbg is not None and t - 1 == dbg:
                dbg_ev = dump(dbg)
        S.final_wait("sp", [last_store[0], dbg_ev])
        S.emit()
    return nc


def _shared_inputs(norm_g, w_in, qn_a, kn_a, rpb_a, qn_b, kn_b, sink_b, w_o_a, w_o_b, w_out, t5_table):
    f = lambda a: np.ascontiguousarray(np.asarray(a, dtype=np.float32))
    small = np.zeros((128, 160), np.float32)
    small[:, 0:128] = np.eye(128, dtype=np.float32)
    small[:, 128:136] = f(norm_g)[0].reshape(8, 128).T
    small[:, 136] = np.tile(f(qn_a)[0], 2)
    small[:, 137] = np.tile(f(kn_a)[0], 2)
    small[:, 138] = np.tile(f(qn_b)[0], 2)
    small[:, 139] = np.tile(f(kn_b)[0], 2)
    small[:, 140:148] = f(sink_b)[0][None, :]
    uab, uam = _a_tables(f(rpb_a)[0])
    tbb, tbm = _b_tables(f(t5_table))
    return {"w_in": f(w_in)[0], "w_oa": f(w_o_a)[0], "w_ob": f(w_o_b)[0], "w_out": f(w_out)[0], "small": small,
            "ua_bias": uab, "ua_mask": uam, "tb_bias": tbb, "tb_mask": tbm}


_PROG_CACHE = {}


def run_layer(seqs_per_core, shared, n_cores, dbg=None):
    seq_tiles = tuple(s.shape[0] // 128 for s in seqs_per_core[0])
    if dbg is not None:
        nc = build_program(list(seq_tiles), dbg=dbg)
    else:
        if seq_tiles not in _PROG_CACHE:
            _PROG_CACHE[seq_tiles] = build_program(list(seq_tiles))
        nc = _PROG_CACHE[seq_tiles]
    in_maps = []
    for c in range(n_cores):
        m = dict(shared)
        m["x"] = np.ascontiguousarray(np.concatenate(seqs_per_core[c], axis=0), dtype=np.float32)
        in_maps.append(m)
    res = run_bass_kernel_spmd(nc, in_maps, core_ids=list(range(n_cores)))
    if dbg is not None:
        return res.results
    return [r["y"] for r in res.results]


def kernel(x_prompt, x_sample, norm_g, w_in, qn_a, kn_a, rpb_a, qn_b, kn_b, sink_b, w_o_a, w_o_b, w_out, t5_table):
    x_prompt = np.asarray(x_prompt, dtype=np.float32)
    x_sample = np.asarray(x_sample, dtype=np.float32)
    shared = _shared_inputs(norm_g, w_in, qn_a, kn_a, rpb_a, qn_b, kn_b, sink_b, w_o_a, w_o_b, w_out, t5_table)
    n = 8
    seqs = [[x_prompt[c], x_sample[c]] for c in range(n)]
    ys = run_layer(seqs, shared, n)
    Lp = x_prompt.shape[1]
    y_prompt = np.stack([ys[c][:Lp] for c in range(n)], axis=0)
    y_sample = np.stack([ys[c][Lp:] for c in range(n)], axis=0)
    return (y_prompt.astype(np.float32), y_sample.astype(np.float32))
```
